# Optimizing a Trainium2 kernel written in Bass

```python
import math
import jax, jax.numpy as jnp
from jax import lax
import numpy as np

D_MODEL = 1024
BATCH = 2
SEQ = 16384
DEPTH = 4

N_MIXERS = 3
EPS = 1e-6
N_A = (DEPTH + 2) // 3
N_B = (DEPTH + 1) // 3
N_C = DEPTH // 3

HG_HEADS = 8
HG_DK = D_MODEL // HG_HEADS
HG_DV = D_MODEL // HG_HEADS
HG_CHUNK = 64
HG_IN = 4 * D_MODEL

SW_HEADS = 16
SW_KV_HEADS = 4
SW_GROUP = SW_HEADS // SW_KV_HEADS
SW_DH = 64
SW_WINDOW = 128
SW_BLOCK = 128
SW_QW = SW_HEADS * SW_DH
SW_KVW = SW_KV_HEADS * SW_DH
SW_IN = 2 * SW_QW + 2 * SW_KVW
ROPE_THETA = 10000.0
POS_OFFSET_MAX = 4096

GD_QK_HEADS = 8
GD_V_HEADS = 16
GD_DK = 128
GD_DV = 128
GD_CONV = 4
GD_CHUNK = 64
GD_QKW = GD_QK_HEADS * GD_DK
GD_VW = GD_V_HEADS * GD_DV
GD_QKV = 2 * GD_QKW + GD_VW
GD_IN = GD_QKV + GD_VW + 2 * GD_V_HEADS

kernel_name = 'hybrid_hgrn2_swa_sink_gdn_interleaved'


def rmsnorm(x, g):
    xf = x.astype(jnp.float32)
    y = xf * lax.rsqrt(jnp.mean(xf * xf, axis=-1, keepdims=True) + EPS)
    return (y * g.astype(jnp.float32)).astype(x.dtype)


def l2norm(x):
    xf = x.astype(jnp.float32)
    return xf * lax.rsqrt(jnp.sum(xf * xf, axis=-1, keepdims=True) + EPS)


def rope(x, ang):
    ang = ang.reshape(ang.shape[:2] + (1,) * (x.ndim - 3) + ang.shape[-1:])
    cos, sin = jnp.cos(ang), jnp.sin(ang)
    x1, x2 = jnp.split(x.astype(jnp.float32), 2, axis=-1)
    return jnp.concatenate([x1 * cos - x2 * sin, x2 * cos + x1 * sin], axis=-1).astype(x.dtype)


def to_chunks(a, chunk):
    B, T = a.shape[:2]
    a = a.reshape((B, T // chunk, chunk) + a.shape[2:])
    return jnp.moveaxis(jnp.moveaxis(a, 1, 0), 3, 2)


def from_chunks(a):
    a = jnp.moveaxis(jnp.moveaxis(a, 2, 3), 0, 1)
    return a.reshape((a.shape[0], a.shape[1] * a.shape[2]) + a.shape[3:])


def causal_conv(x, w):
    K, C = w.shape
    return lax.conv_general_dilated(x, w[:, None, :].astype(x.dtype), window_strides=(1,),
                                    padding=[(K - 1, 0)], dimension_numbers=('NWC', 'WIO', 'NWC'),
                                    feature_group_count=C)


def hgrn2_scan(q, k, v, log_f):
    B, T, H, dk = q.shape
    dv = v.shape[-1]
    causal = jnp.tril(jnp.ones((HG_CHUNK, HG_CHUNK), bool))

    def step(S, inp):
        q_, k_, v_, lf = inp
        b = jnp.cumsum(lf, axis=-2)
        diff = b[..., :, None, :] - b[..., None, :, :]
        dec = jnp.exp(jnp.where(causal[:, :, None], diff, -jnp.inf))
        A = jnp.einsum('bhtd,bhsd,bhtsd->bhts', q_, k_, dec)
        o = jnp.einsum('bhts,bhse->bhte', A, v_) + jnp.einsum('bhtd,bhde->bhte', q_ * jnp.exp(b), S)
        b_last = b[..., -1:, :]
        S = jnp.exp(b_last)[..., 0, :, None] * S + jnp.einsum('bhsd,bhse->bhde', k_ * jnp.exp(b_last - b), v_)
        return S, o

    xs = tuple(to_chunks(a.astype(jnp.float32), HG_CHUNK) for a in (q, k, v, log_f))
    S0 = jnp.zeros((B, H, dk, dv), jnp.float32)
    _, o = lax.scan(step, S0, xs)
    return from_chunks(o)


def hgrn2_mixer(h, w_in, w_out, onorm_g, lb):
    B, T, _ = h.shape
    q, f_pre, i_, z = jnp.split(h @ w_in, [D_MODEL, 2 * D_MODEL, 3 * D_MODEL], axis=-1)
    f_pre = f_pre.astype(jnp.float32)
    lb = lb.astype(jnp.float32)
    log_f = jnp.log(lb + (1.0 - lb) * jax.nn.sigmoid(f_pre))
    k = (1.0 - lb) * jax.nn.sigmoid(-f_pre)
    q = jax.nn.silu(q)
    shp = (B, T, HG_HEADS, HG_DK)
    o = hgrn2_scan(q.reshape(shp), k.reshape(shp), i_.reshape(B, T, HG_HEADS, HG_DV), log_f.reshape(shp))
    o = rmsnorm(o, onorm_g).reshape(B, T, HG_HEADS * HG_DV).astype(h.dtype)
    return (o * jax.nn.silu(z)) @ w_out


def swa_mixer(h, w_in, w_out, qn_g, kn_g, sinks, ang):
    B, T, _ = h.shape
    nb = T // SW_BLOCK
    q, k, v, z = jnp.split(h @ w_in, [SW_QW, SW_QW + SW_KVW, SW_QW + 2 * SW_KVW], axis=-1)
    q = rope(rmsnorm(q.reshape(B, T, SW_KV_HEADS, SW_GROUP, SW_DH), qn_g), ang)
    k = rope(rmsnorm(k.reshape(B, T, SW_KV_HEADS, SW_DH), kn_g), ang)
    v = v.reshape(B, T, SW_KV_HEADS, SW_DH)

    def band(a):
        cur = a.reshape(B, nb, SW_BLOCK, SW_KV_HEADS, SW_DH)
        prev = jnp.concatenate([jnp.zeros_like(cur[:, :1]), cur[:, :-1]], axis=1)
        return jnp.concatenate([prev, cur], axis=2)

    qb = q.reshape(B, nb, SW_BLOCK, SW_KV_HEADS, SW_GROUP, SW_DH)
    kb, vb = band(k), band(v)
    s = jnp.einsum('bnqhgd,bnkhd->bnhgqk', qb, kb).astype(jnp.float32) * (SW_DH ** -0.5)
    qi = jnp.arange(SW_BLOCK)[:, None]
    kj = jnp.arange(2 * SW_BLOCK)[None, :]
    rel = qi + SW_BLOCK - kj
    in_band = (rel >= 0) & (rel < SW_WINDOW)
    key_pos = jnp.arange(nb)[:, None] * SW_BLOCK + jnp.arange(2 * SW_BLOCK)[None, :] - SW_BLOCK
    mask = in_band[None] & (key_pos >= 0)[:, None, :]
    s = jnp.where(mask[None, :, None, None], s, -jnp.inf)
    sink = sinks.astype(jnp.float32).reshape(SW_KV_HEADS, SW_GROUP)[None, None, :, :, None, None]
    m = jnp.maximum(jnp.max(s, axis=-1, keepdims=True), sink)
    p = jnp.exp(s - m)
    p = p / (jnp.sum(p, axis=-1, keepdims=True) + jnp.exp(sink - m))
    o = jnp.einsum('bnhgqk,bnkhd->bnqhgd', p.astype(h.dtype), vb).reshape(B, T, SW_QW)
    return (o * jax.nn.silu(z)) @ w_out


def gated_delta_scan(q, k, v, beta, g):
    B, T, H, dk = q.shape
    dv = v.shape[-1]
    incl = jnp.tril(jnp.ones((GD_CHUNK, GD_CHUNK), bool))
    strict = jnp.tril(jnp.ones((GD_CHUNK, GD_CHUNK), jnp.float32), -1)
    eye = jnp.eye(GD_CHUNK, dtype=jnp.float32)

    def step(S, inp):
        q_, k_, v_, beta_, g_ = inp
        d = jnp.cumsum(g_, axis=-1)
        dec = jnp.exp(jnp.where(incl, d[..., :, None] - d[..., None, :], -jnp.inf))
        kb = k_ * beta_[..., None]
        A = jnp.einsum('bhid,bhjd->bhij', kb, k_) * dec * strict
        rhs = jnp.concatenate([v_ * beta_[..., None], kb * jnp.exp(d)[..., None]], axis=-1)
        X = lax.linalg.triangular_solve(A + eye, rhs, left_side=True, lower=True, unit_diagonal=True)
        u, w = X[..., :dv], X[..., dv:]
        v_new = u - jnp.einsum('bhid,bhde->bhie', w, S)
        qk = jnp.einsum('bhid,bhjd->bhij', q_, k_) * dec
        o = jnp.einsum('bhid,bhde->bhie', q_ * jnp.exp(d)[..., None], S) + jnp.einsum('bhij,bhje->bhie', qk, v_new)
        d_last = d[..., -1:]
        S = S * jnp.exp(d_last)[..., None] + jnp.einsum('bhid,bhie->bhde', k_ * jnp.exp(d_last - d)[..., None], v_new)
        return S, o

    xs = tuple(to_chunks(a.astype(jnp.float32), GD_CHUNK) for a in (q, k, v, beta, g))
    S0 = jnp.zeros((B, H, dk, dv), jnp.float32)
    _, o = lax.scan(step, S0, xs)
    return from_chunks(o)


def gdn_mixer(h, w_in, w_out, conv_w, a_log, dt_bias, onorm_g):
    B, T, _ = h.shape
    qkv, z, a, b = jnp.split(h @ w_in, [GD_QKV, GD_QKV + GD_VW, GD_QKV + GD_VW + GD_V_HEADS], axis=-1)
    qkv = jax.nn.silu(causal_conv(qkv, conv_w))
    q, k, v = jnp.split(qkv, [GD_QKW, 2 * GD_QKW], axis=-1)
    rep = GD_V_HEADS // GD_QK_HEADS
    q = jnp.repeat(l2norm(q.reshape(B, T, GD_QK_HEADS, GD_DK)) * (GD_DK ** -0.5), rep, axis=2)
    k = jnp.repeat(l2norm(k.reshape(B, T, GD_QK_HEADS, GD_DK)), rep, axis=2)
    v = v.reshape(B, T, GD_V_HEADS, GD_DV)
    beta = jax.nn.sigmoid(b.astype(jnp.float32))
    g = -jnp.exp(a_log.astype(jnp.float32)) * jax.nn.softplus(a.astype(jnp.float32) + dt_bias.astype(jnp.float32))
    o = gated_delta_scan(q, k, v, beta, g)
    o = rmsnorm(o, onorm_g).reshape(B, T, GD_VW).astype(h.dtype)
    return (o * jax.nn.silu(z)) @ w_out


def setup_inputs(seed: int = 0) -> dict:
    key = jax.random.key(seed)
    ks = jax.random.split(key, 24)
    nrm = lambda k, shape, s: jax.random.normal(k, shape, jnp.float32) * s
    x = nrm(ks[0], (BATCH, SEQ, D_MODEL), 1.0)
    c = nrm(ks[1], (BATCH, D_MODEL), 1.0)
    positions = (jnp.arange(SEQ, dtype=jnp.int32)[None, :]
                 + jax.random.randint(ks[2], (BATCH, 1), 0, POS_OFFSET_MAX, dtype=jnp.int32))
    hgrn_lb = nrm(ks[3], (DEPTH, D_MODEL), 0.1)
    ada_w = nrm(ks[4], (DEPTH, D_MODEL, 3 * D_MODEL), 0.5 * D_MODEL ** -0.5)
    ada_b = nrm(ks[5], (DEPTH, 3 * D_MODEL), 0.01)
    norm_g = 1.0 + nrm(ks[6], (DEPTH, D_MODEL), 0.02)
    hg_in_w = nrm(ks[7], (N_A, D_MODEL, HG_IN), D_MODEL ** -0.5)
    hg_out_w = nrm(ks[8], (N_A, HG_HEADS * HG_DV, D_MODEL), (HG_HEADS * HG_DV) ** -0.5)
    hg_onorm = 1.0 + nrm(ks[9], (N_A, HG_DV), 0.02)
    sw_in_w = nrm(ks[10], (N_B, D_MODEL, SW_IN), D_MODEL ** -0.5)
    sw_out_w = nrm(ks[11], (N_B, SW_QW, D_MODEL), SW_QW ** -0.5)
    sw_qnorm = 1.0 + nrm(ks[12], (N_B, SW_DH), 0.02)
    sw_knorm = 1.0 + nrm(ks[13], (N_B, SW_DH), 0.02)
    sw_sinks = nrm(ks[14], (N_B, SW_HEADS), 0.5)
    gd_in_w = nrm(ks[15], (N_C, D_MODEL, GD_IN), D_MODEL ** -0.5)
    gd_out_w = nrm(ks[16], (N_C, GD_VW, D_MODEL), GD_VW ** -0.5)
    gd_conv_w = nrm(ks[17], (N_C, GD_CONV, GD_QKV), GD_CONV ** -0.5)
    gd_a_log = jnp.log(jax.random.uniform(ks[18], (N_C, GD_V_HEADS), jnp.float32, 1.0, 16.0))
    dt = jnp.exp(jax.random.uniform(ks[19], (N_C, GD_V_HEADS), jnp.float32, math.log(1e-3), math.log(1e-1)))
    gd_dt_bias = dt + jnp.log(-jnp.expm1(-dt))
    gd_onorm = 1.0 + nrm(ks[20], (N_C, GD_DV), 0.02)
    return {'x': x, 'c': c, 'positions': positions, 'hgrn_lb': hgrn_lb,
            'ada_w': ada_w, 'ada_b': ada_b, 'norm_g': norm_g,
            'hg_in_w': hg_in_w, 'hg_out_w': hg_out_w, 'hg_onorm': hg_onorm,
            'sw_in_w': sw_in_w, 'sw_out_w': sw_out_w, 'sw_qnorm': sw_qnorm, 'sw_knorm': sw_knorm,
            'sw_sinks': sw_sinks,
            'gd_in_w': gd_in_w, 'gd_out_w': gd_out_w, 'gd_conv_w': gd_conv_w,
            'gd_a_log': gd_a_log, 'gd_dt_bias': gd_dt_bias, 'gd_onorm': gd_onorm}


def reference(x, c, positions, hgrn_lb, ada_w, ada_b, norm_g,
              hg_in_w, hg_out_w, hg_onorm,
              sw_in_w, sw_out_w, sw_qnorm, sw_knorm, sw_sinks,
              gd_in_w, gd_out_w, gd_conv_w, gd_a_log, gd_dt_bias, gd_onorm):
    lb_all = jnp.cumsum(jax.nn.softmax(hgrn_lb.astype(jnp.float32), axis=0), axis=0)
    lb_all = lb_all - lb_all[0:1]
    inv_freq = ROPE_THETA ** (-jnp.arange(0, SW_DH, 2, dtype=jnp.float32) / SW_DH)
    ang = positions.astype(jnp.float32)[..., None] * inv_freq
    for i in range(DEPTH):
        j = i // N_MIXERS
        mod = (c @ ada_w[i] + ada_b[i])[:, None, :]
        shift, scale, gate = jnp.split(mod, 3, axis=-1)
        h = rmsnorm(x, norm_g[i]) * (1.0 + scale) + shift
        kind = i % N_MIXERS
        if kind == 0:
            y = hgrn2_mixer(h, hg_in_w[j], hg_out_w[j], hg_onorm[j], lb_all[i])
        elif kind == 1:
            y = swa_mixer(h, sw_in_w[j], sw_out_w[j], sw_qnorm[j], sw_knorm[j], sw_sinks[j], ang)
        else:
            y = gdn_mixer(h, gd_in_w[j], gd_out_w[j], gd_conv_w[j], gd_a_log[j], gd_dt_bias[j], gd_onorm[j])
        x = x + gate * y
    return x
```

```python
import numpy as np
import ml_dtypes
from contextlib import ExitStack

import concourse.bass as bass
import concourse.mybir as mybir
from concourse.bass_utils import run_bass_kernel_spmd

F32 = mybir.dt.float32
BF16 = mybir.dt.bfloat16
I32 = mybir.dt.int32
U32 = mybir.dt.uint32
AF = mybir.ActivationFunctionType
ALU = mybir.AluOpType
AX = mybir.AxisListType

D = 1024
KT = 8
EPS = 1e-6
MT = 512
NDMASEM = 40
SW = 1024


class Sub:
    def __init__(self, ap, tag):
        self.ap = ap
        self.tag = tag


class Prog:
    ENGS = ("pe", "act", "dve", "pool", "sp")
    WRITE_KW = ("out", "accum_out", "ap")

    def __init__(self, nc):
        self.nc = nc
        self.ins = []
        self.writers = {}
        self.readers = {}
        self.dma_sem_of = {}
        self.dma_tot = []
        self.same_engine_sync = True
        self._bar_pending = set()
        self._bar_ids = []
        self.marks = {}
        self.psum_excl = True

    def _key(self, v):
        if isinstance(v, Sub):
            return (v.ap.tensor.name, v.tag), v.ap
        return (v.tensor.name, None), v

    def op(self, eng, meth, **kw):
        reads, writes, real = [], [], {}
        for k, v in kw.items():
            if isinstance(v, (Sub, bass.AP)):
                key, ap = self._key(v)
                real[k] = ap
                (writes if k in self.WRITE_KW else reads).append(key)
            else:
                real[k] = v
        i = len(self.ins)
        deps = set()
        if eng in self._bar_pending:
            deps.update(self._bar_ids)
            self._bar_pending.discard(eng)
        for key in reads:
            deps.update(self.writers.get(key, ()))
            if self.psum_excl and key[0].startswith("ps"):
                deps.update(self.readers.get(key, ()))
        for key in writes:
            deps.update(self.writers.get(key, ()))
            deps.update(self.readers.get(key, ()))
        for key in reads:
            self.readers.setdefault(key, []).append(i)
        for key in writes:
            self.writers[key] = [i]
            self.readers[key] = []
        dma = meth == "dma_start"
        rec = dict(eng=eng, meth=meth, kw=real, deps=deps, dma=dma, inc=False, cnt=0, sem=None, tgt=0)
        if dma:
            sbside = None
            for k in ("out", "in_"):
                ap = real[k]
                if "SB" in type(ap.tensor).__name__:
                    sbside = ap.tensor.name
            assert sbside is not None, "dma needs an SBUF side"
            s = self.dma_sem_of.setdefault(sbside, len(self.dma_sem_of))
            if s >= len(self.dma_tot):
                self.dma_tot.append(0)
            self.dma_tot[s] += 16
            rec["sem"] = s
            rec["tgt"] = self.dma_tot[s]
        self.ins.append(rec)
        return i

    def mark(self, name):
        self.marks[name] = len(self.ins)

    def barrier(self):
        last = {}
        lastdma = {}
        for j, r in enumerate(self.ins):
            if r["dma"]:
                lastdma[r["sem"]] = j
            else:
                last[r["eng"]] = j
        self._bar_ids = list(last.values()) + list(lastdma.values())
        self._bar_pending = set(self.ENGS)
        self.dma_sem_of = {}

    def setup(self, stack):
        nc = self.nc
        self.sems = {e: stack.enter_context(nc.semaphore("s_" + e)) for e in self.ENGS}
        self.dsem = [stack.enter_context(nc.semaphore("d_%d" % k)) for k in range(NDMASEM)]
        self.waited = {e: {p: -1 for p in self.ENGS} for e in self.ENGS}
        self.dma_waited = {e: set() for e in self.ENGS}
        self.cnt = {e: 0 for e in self.ENGS}
        self.done = 0

    def emit(self, final=False):
        nc = self.nc
        import os as _os
        ins = self.ins
        lo, hi = self.done, len(ins)
        stop = _os.environ.get("KSTOP")
        if stop:
            n = int(stop) if stop.isdigit() else self.marks.get(stop, hi)
            hi = max(lo, min(hi, n))
        assert len(self.dma_tot) <= NDMASEM, len(self.dma_tot)
        waited, dma_waited = self.waited, self.dma_waited
        for i in range(lo, hi):
            r = ins[i]
            e = r["eng"]
            need = {}
            dm = []
            for j in r["deps"]:
                p = ins[j]
                if p["dma"]:
                    if j not in dma_waited[e]:
                        dm.append(j)
                else:
                    pe_ = p["eng"]
                    if pe_ == e and not r["dma"]:
                        if not (self.same_engine_sync and e != "pe"):
                            continue
                    if j > waited[e][pe_]:
                        need[pe_] = max(need.get(pe_, -1), j)
            for pe_, j in list(need.items()):
                if j < lo and not ins[j]["inc"]:
                    jj = j
                    while jj < lo and not (ins[jj]["eng"] == pe_ and ins[jj]["inc"] and not ins[jj]["dma"]):
                        jj += 1
                    assert jj < lo, "no increment available for cross-block dependency"
                    need[pe_] = jj
            r["w_cmp"] = need
            r["w_dma"] = dm
            for pe_, j in need.items():
                waited[e][pe_] = max(waited[e][pe_], j)
                ins[j]["inc"] = True
            for j in dm:
                dma_waited[e].add(j)
        lastc = {}
        for i in range(lo, hi):
            if not ins[i]["dma"]:
                lastc[ins[i]["eng"]] = i
        for i in lastc.values():
            ins[i]["inc"] = True
        for i in range(lo, hi):
            r = ins[i]
            if r["inc"] and not r["dma"]:
                self.cnt[r["eng"]] += 1
                r["cnt"] = self.cnt[r["eng"]]
        per = {e: [] for e in self.ENGS}
        for i in range(lo, hi):
            per[ins[i]["eng"]].append(ins[i])
        sems, dsem = self.sems, self.dsem
        dma_tot = [0] * len(self.dma_tot)
        for r in ins[:hi]:
            if r["dma"]:
                dma_tot[r["sem"]] = max(dma_tot[r["sem"]], r["tgt"])
        self.done = len(ins)

        def run(eng_obj, lst, fin=False):
            for r in lst:
                for pe_, j in r["w_cmp"].items():
                    eng_obj.wait_ge(sems[pe_], ins[j]["cnt"])
                for j in r["w_dma"]:
                    eng_obj.wait_ge(dsem[ins[j]["sem"]], ins[j]["tgt"])
                inst = getattr(eng_obj, r["meth"])(**r["kw"])
                if r["dma"]:
                    inst.then_inc(dsem[r["sem"]], 16)
                elif r["inc"]:
                    inst.then_inc(sems[r["eng"]], 1)
            if fin:
                for k in range(len(dma_tot)):
                    if dma_tot[k]:
                        eng_obj.wait_ge(dsem[k], dma_tot[k])

        with nc.Block() as block:
            @block.tensor
            def _(eng):
                run(eng, per["pe"])

            @block.scalar
            def _(eng):
                run(eng, per["act"])

            @block.vector
            def _(eng):
                run(eng, per["dve"])

            @block.gpsimd
            def _(eng):
                run(eng, per["pool"])

            @block.sync
            def _(eng):
                run(eng, per["sp"], fin=final)


def make_consts():
    c = {}
    c["ident_f"] = np.eye(128, dtype=np.float32)
    c["ident_b"] = np.eye(128, dtype=np.float32).astype(ml_dtypes.bfloat16)
    c["ones_f"] = np.ones((128, 128), np.float32)
    for ch in (32, 64):
        s = np.arange(ch)[:, None]
        t = np.arange(MT)[None, :] % ch
        c["hg_maskP%d" % ch] = (s <= t).astype(np.uint32)
        rm = np.ones((128, MT), np.float32)
        rm[:, ::ch] = 0.0
        c["resetmask%d" % ch] = rm
    k = np.arange(128)[:, None]
    q = np.arange(MT)[None, :] % 128
    c["sw_mask_cur"] = (k <= q).astype(np.float32).astype(ml_dtypes.bfloat16)
    c["sw_mask_prev"] = (k > q).astype(np.float32).astype(ml_dtypes.bfloat16)
    rot = np.zeros((128, 128), np.float32)
    for blk in (0, 64):
        for m in range(32):
            rot[blk + m + 32, blk + m] = -1.0
            rot[blk + m, blk + m + 32] = 1.0
    c["sw_rotT"] = rot
    ob = np.zeros((128, 128), np.float32)
    ob[0:64, 0:64] = 1.0 / 64
    ob[64:128, 64:128] = 1.0 / 64
    c["sw_ones_blk"] = ob
    inv = (10000.0 ** (-np.arange(0, 64, 2, dtype=np.float32) / 64)).astype(np.float32)
    c["sw_invfreq"] = np.tile(inv, 4).reshape(128, 1).astype(np.float32)
    jj = np.arange(64)[:, None]
    ii = np.arange(256)[None, :] % 64
    c["gd_maskS"] = (jj < ii).astype(np.float32)
    c["gd_maskI"] = (jj <= ii).astype(np.float32)
    rm = np.ones((16, 256), np.float32)
    rm[:, ::64] = 0.0
    c["gd_rm"] = rm
    sel = np.zeros((16, 16, 128), np.float32)
    for h in range(16):
        sel[h, h, :] = 1.0
    c["gd_sel"] = sel.reshape(16, 16 * 128)
    return c


CONST_SPECS = {
    "gd_maskS": ([64, 256], F32),
    "gd_maskI": ([64, 256], F32),
    "gd_rm": ([16, 256], F32),
    "gd_sel": ([16, 2048], F32),
    "sw_mask_cur": ([128, MT], BF16),
    "sw_mask_prev": ([128, MT], BF16),
    "sw_rotT": ([128, 128], F32),
    "sw_ones_blk": ([128, 128], F32),
    "sw_invfreq": ([128, 1], F32),
    "ident_f": ([128, 128], F32),
    "ident_b": ([128, 128], BF16),
    "ones_f": ([128, 128], F32),
    "hg_maskP32": ([32, MT], U32),
    "hg_maskP64": ([64, MT], U32),
    "resetmask32": ([128, MT], F32),
    "resetmask64": ([128, MT], F32),
}


COMMON_CONSTS = ("ident_f", "ident_b", "ones_f")
HG_CONSTS = ("hg_maskP32", "hg_maskP64", "resetmask32", "resetmask64")
GD_CONSTS = ("gd_maskS", "gd_maskI", "gd_sel", "gd_rm")
SW_CONSTS = ("sw_mask_cur", "sw_mask_prev", "sw_rotT", "sw_ones_blk", "sw_invfreq")


class Ctx:
    pass


def load_consts(g, names, pfx=""):
    for k in names:
        shape, dt = CONST_SPECS[k]
        g.c[k] = g.sb(pfx + "c_" + k, shape, dt)
        g.P.op("sp", "dma_start", out=g.c[k][:], in_=g.dr[k][:, :])


def build(T, layers, n_hg=2):
    assert T % MT == 0
    nc = bass.Bass("TRN2", target_bir_lowering=False)
    P = Prog(nc)
    g = Ctx()
    g.nc, g.P, g.T = nc, P, T
    dr = {}

    def din(name, shape, dt=F32):
        dr[name] = nc.dram_tensor(name, shape, dt, kind="ExternalInput").ap()
        return dr[name]

    din("x", [T, D])
    din("c8", [8, 128])
    din("positions", [1, T], I32)
    din("hgrn_lb", [32, 128])
    din("ada_w", [4, D, 3 * D])
    din("ada_b", [4, 24, 128])
    din("ada_b_row", [4, 1, 3 * D])
    din("norm_g", [4, 8, 128])
    din("hg_in_w", [n_hg, D, 4 * D])
    din("hg_out_w", [n_hg, D, D])
    din("hg_onorm", [n_hg, 1, 128])
    din("sw_in_w", [1, D, 2560])
    din("sw_out_w", [1, D, D])
    din("sw_qnorm", [1, 64])
    din("sw_knorm", [1, 64])
    din("sw_sinks", [1, 16])
    din("gd_in_w", [1, D, 6176])
    din("gd_out_w", [1, 2 * D, D])
    din("gd_conv_w", [4, 4096])
    din("gd_a_log", [16, 1])
    din("gd_dt_bias", [16, 1])
    din("gd_onorm", [1, 128])
    for k, (shape, dt) in CONST_SPECS.items():
        din(k, shape, dt)
    out = nc.dram_tensor("out", [T, D], F32, kind="ExternalOutput").ap()
    scr = [nc.dram_tensor("xs%d" % i, [T, D], F32, kind="Internal").ap() for i in range(2)]
    g.gd_wscr = nc.dram_tensor("gd_wscr", [32, 128, 1024], BF16, kind="Internal").ap()
    g.dr = dr

    with ExitStack() as st:
        def sbg(name, shape, dt=F32):
            return st.enter_context(nc.sbuf_tensor(name, shape, dt))

        P.setup(st)
        g.sb = sbg
        g.psum = [st.enter_context(nc.psum_tensor("ps%d" % i, [128, 512], F32)) for i in range(8)]
        g.c = {}
        load_consts(g, COMMON_CONSTS)
        g.stage = [sbg("stage%d" % i, [128, SW], F32) for i in range(2)]
        P.mark("prep_common")
        prep_common(g)
        P.mark("after_prep_common")
        src = dr["x"]
        srctag = lambda ap, tag: ap
        for li, layer in enumerate(layers):
            dst = out if li == len(layers) - 1 else scr[li % 2]
            kind = layer % 3
            with ExitStack() as lst:
                g.sb = lambda name, shape, dt=F32: lst.enter_context(nc.sbuf_tensor(name, shape, dt))
                if kind == 0:
                    load_consts(g, HG_CONSTS, "L%d" % layer)
                    hgrn2_layer(g, layer, src, srctag, dst)
                elif kind == 1:
                    load_consts(g, SW_CONSTS, "L%d" % layer)
                    swa_layer(g, layer, src, srctag, dst)
                else:
                    load_consts(g, GD_CONSTS, "L%d" % layer)
                    gdn_layer(g, layer, src, srctag, dst)
                P.barrier()
                P.emit(final=(li == len(layers) - 1))
            src = dst
            srctag = Sub
    return nc


def transpose_small(g, dst, src_rows, nrows, ps):
    P = g.P
    P.op("pe", "transpose", out=ps[:, 0:nrows], in_=src_rows, identity=g.c["ident_f"][0:nrows, 0:nrows])
    P.op("dve", "tensor_copy", out=dst, in_=ps[:, 0:nrows])


def W(ap, k):
    return Sub(ap, ("w", k))


def prep_common(g):
    P, sb, dr = g.P, g.sb, g.dr
    ps = g.psum[0]
    g.rows_a = sb("rows_a", [32, 128], F32)
    g.rows_g = sb("rows_g", [8, 128], F32)
    rows_c = sb("rows_c", [8, 128], F32)
    g.c_col = sb("c_col", [128, 8], F32)
    P.op("sp", "dma_start", out=rows_c[:], in_=dr["c8"][:, :])
    transpose_small(g, g.c_col[:], rows_c[:], 8, ps)
    P.op("sp", "dma_start", out=g.rows_a[:], in_=dr["hgrn_lb"][:, :])
    lbT = sb("lbT", [128, 32], F32)
    transpose_small(g, lbT[:], g.rows_a[:], 32, g.psum[1])
    ex = sb("lb_exp", [128, 32], F32)
    P.op("act", "activation", out=ex[:], in_=lbT[:], func=AF.Exp)
    den = sb("lb_den", [128, 8], F32)
    P.op("dve", "tensor_tensor", out=den[:], in0=ex[:, 0:8], in1=ex[:, 8:16], op=ALU.add)
    P.op("dve", "tensor_tensor", out=den[:], in0=den[:], in1=ex[:, 16:24], op=ALU.add)
    P.op("dve", "tensor_tensor", out=den[:], in0=den[:], in1=ex[:, 24:32], op=ALU.add)
    rden = sb("lb_rden", [128, 8], F32)
    P.op("dve", "reciprocal", out=rden[:], in_=den[:])
    g.lb = {}
    lb0 = sb("lb_l0", [128, 8], F32)
    P.op("dve", "memset", ap=lb0[:], constant=0.0)
    g.lb[0] = lb0
    num = sb("lb_num", [128, 8], F32)
    P.op("dve", "tensor_tensor", out=num[:], in0=ex[:, 8:16], in1=ex[:, 16:24], op=ALU.add)
    P.op("dve", "tensor_tensor", out=num[:], in0=num[:], in1=ex[:, 24:32], op=ALU.add)
    lb3 = sb("lb_l3", [128, 8], F32)
    P.op("dve", "tensor_tensor", out=lb3[:], in0=num[:], in1=rden[:], op=ALU.mult)
    g.lb[3] = lb3
    g.modT = sb("modT", [128, 16], F32)
    g.gsT = sb("gsT", [128, 8], F32)
    g.gate_row = sb("gate_row", [1, D], F32)
    g.gate_b = sb("gate_b", [128, D], F32)
    g.adab_T = sb("adab_T", [128, 24], F32)
    g.ng_T = sb("ng_T", [128, 8], F32)


def prep_layer(g, layer):
    P, dr = g.P, g.dr
    ps = g.psum[0]
    P.op("sp", "dma_start", out=g.rows_a[0:24, :], in_=dr["ada_b"][layer])
    transpose_small(g, g.adab_T[:], g.rows_a[0:24, :], 24, g.psum[1])
    P.op("sp", "dma_start", out=g.rows_g[:], in_=dr["norm_g"][layer])
    transpose_small(g, g.ng_T[:], g.rows_g[:], 8, g.psum[2])
    aw = dr["ada_w"][layer].rearrange("(k p) n -> p k n", p=128)
    for j in range(16):
        t = g.stage[j % 2][:, 0:1024].rearrange("p (k n) -> p k n", k=KT)
        P.op("sp", "dma_start", out=t, in_=aw[:, :, j * 128:(j + 1) * 128])
        for k in range(KT):
            P.op("pe", "matmul", out=ps[:, j:j + 1], lhsT=t[:, k, :], rhs=g.c_col[:, k:k + 1],
                 start=(k == 0), stop=(k == KT - 1))
    P.op("dve", "tensor_tensor", out=g.modT[:], in0=ps[:, 0:16], in1=g.adab_T[:, 0:16], op=ALU.add)
    P.op("dve", "scalar_tensor_tensor", out=g.gsT[:], in0=g.modT[:, 8:16], scalar=1.0, in1=g.ng_T[:],
         op0=ALU.add, op1=ALU.mult)
    P.op("sp", "dma_start", out=g.gate_row[:], in_=dr["ada_b_row"][layer][:, 2 * D:3 * D])
    for cg in range(8):
        psg = g.psum[3 + cg % 2]
        t = g.stage[cg % 2][:, 0:1024].rearrange("p (k n) -> p k n", k=KT)
        P.op("sp", "dma_start", out=t, in_=aw[:, :, 2 * D + cg * 128:2 * D + (cg + 1) * 128])
        for k in range(KT):
            P.op("pe", "matmul", out=psg[0:1, 0:128], lhsT=g.c_col[:, k:k + 1], rhs=t[:, k, :],
                 start=(k == 0), stop=(k == KT - 1))
        P.op("dve", "tensor_tensor", out=g.gate_row[0:1, cg * 128:(cg + 1) * 128], in0=psg[0:1, 0:128],
             in1=g.gate_row[0:1, cg * 128:(cg + 1) * 128], op=ALU.add)
    for cg in range(2):
        psg = g.psum[5 + cg]
        P.op("pe", "matmul", out=psg[:], lhsT=g.c["ones_f"][0:1, :], rhs=g.gate_row[0:1, cg * 512:(cg + 1) * 512],
             start=True, stop=True)
        P.op("dve", "tensor_copy", out=g.gate_b[:, cg * 512:(cg + 1) * 512], in_=psg[:])


def load_weight_bf16(g, dst3, wsrc, ncols, colscale=None, kbase=0):
    P = g.P
    nk = wsrc.shape[0] // 128
    i = 0
    for k in range(nk):
        for c0 in range(0, ncols, SW):
            w = min(SW, ncols - c0)
            stg = g.stage[i % 2]
            P.op("sp", "dma_start", out=stg[:, 0:w], in_=wsrc[k * 128:(k + 1) * 128, c0:c0 + w])
            eng = ("dve", "pool")[i % 2]
            dstv = W(dst3[:, k, c0:c0 + w], kbase + k)
            if colscale is None:
                P.op(eng, "tensor_copy", out=dstv, in_=stg[:, 0:w])
            else:
                P.op(eng, "tensor_tensor", out=dstv, in0=stg[:, 0:w], in1=colscale[:, c0:c0 + w], op=ALU.mult)
            i += 1


def prologue(g, mt, src, srctag, hT, xb, xnb, ss, rstd, mtl=MT):
    P = g.P
    P.op("pool", "memset", ap=ss[:], constant=0.0)
    for blk in range(mtl // 128):
        t0 = mt * mtl + blk * 128
        x = xb[blk % len(xb)]
        xn = xnb[blk % len(xnb)]
        P.op("sp", "dma_start", out=x[:], in_=srctag(src[t0:t0 + 128, :], t0 // 128))
        P.op("act", "activation", out=xn[:], in_=x[:], func=AF.Square, accum_out=ss[:, blk:blk + 1])
        P.op("dve", "tensor_scalar", out=rstd[:, blk:blk + 1], in0=ss[:, blk:blk + 1], scalar1=1.0 / D, scalar2=EPS,
             op0=ALU.mult, op1=ALU.add)
        P.op("act", "activation", out=rstd[:, blk:blk + 1], in_=rstd[:, blk:blk + 1], func=AF.Sqrt)
        P.op("dve", "reciprocal", out=rstd[:, blk:blk + 1], in_=rstd[:, blk:blk + 1])
        P.op("dve", "tensor_scalar", out=xn[:], in0=x[:], scalar1=rstd[:, blk:blk + 1], scalar2=None, op0=ALU.mult)
        for kg in range(2):
            ps = g.psum[kg]
            psb = ps[:].bitcast(BF16)
            for kk in range(4):
                k = kg * 4 + kk
                P.op("pe", "transpose", out=psb[:, kk * 128:(kk + 1) * 128], in_=xn[:, k * 128:(k + 1) * 128],
                     identity=g.c["ident_b"][:])
            for kk in range(4):
                k = kg * 4 + kk
                dstv = Sub(hT[:, k, blk * 128:(blk + 1) * 128], k)
                srcv = psb[:, kk * 128:(kk + 1) * 128]
                if k % 2 == 0:
                    P.op("dve", "tensor_scalar", out=dstv, in0=srcv, scalar1=g.gsT[:, k:k + 1],
                         scalar2=g.modT[:, k:k + 1], op0=ALU.mult, op1=ALU.add)
                else:
                    P.op("act", "activation", out=dstv, in_=srcv, func=AF.Identity, scale=g.gsT[:, k:k + 1],
                         bias=g.modT[:, k:k + 1])


def epilogue(g, mt, src, srctag, dst, og, nk, w_out, xb, psa, psb_, ogtag=True, mtl=MT):
    P = g.P
    for blk in range(mtl // 128):
        t0 = mt * mtl + blk * 128
        x = xb[blk % len(xb)]
        P.op("sp", "dma_start", out=x[:], in_=srctag(src[t0:t0 + 128, :], t0 // 128))
        for cg in range(2):
            pst = (psa, psb_)[cg]
            for k in range(nk):
                ogv = og[:, k, blk * 128:(blk + 1) * 128]
                P.op("pe", "matmul", out=pst[:], lhsT=Sub(ogv, k) if ogtag else ogv,
                     rhs=W(w_out[:, k, cg * 512:(cg + 1) * 512], 100 + k), start=(k == 0), stop=(k == nk - 1))
            P.op("dve", "tensor_tensor", out=x[:, cg * 512:(cg + 1) * 512], in0=pst[:], in1=x[:, cg * 512:(cg + 1) * 512],
                 op=ALU.add)
        P.op("sp", "dma_start", out=Sub(dst[t0:t0 + 128, :], t0 // 128), in_=x[:])


def hgrn2_layer(g, layer, src, srctag, dst):
    P, sb, dr, nc = g.P, g.sb, g.dr, g.nc
    j = layer // 3
    T = g.T
    CH = 32 if layer == 0 else 64
    NCH = MT // CH
    P.mark("prep_layer")
    prep_layer(g, layer)
    P.mark("after_prep_layer")
    L = "L%d_" % layer
    g.warena = sb(L + "warena", [128, KT * 5120], BF16)
    w_in = g.warena[:, 0:KT * 4096].rearrange("p (k n) -> p k n", k=KT)
    w_out = g.warena[:, KT * 4096:KT * 4096 + KT * 1024].rearrange("p (k n) -> p k n", k=KT)
    load_weight_bf16(g, w_in, dr["hg_in_w"][j], 4096)
    load_weight_bf16(g, w_out, dr["hg_out_w"][j], 1024, colscale=g.gate_b, kbase=100)
    lb = g.lb[layer]
    oml = sb(L + "oml", [128, 8])
    noml = sb(L + "noml", [128, 8])
    P.op("dve", "tensor_scalar", out=oml[:], in0=lb[:], scalar1=-1.0, scalar2=1.0, op0=ALU.mult, op1=ALU.add)
    P.op("dve", "tensor_scalar", out=noml[:], in0=oml[:], scalar1=-1.0, scalar2=None, op0=ALU.mult)
    onr = sb(L + "onr", [1, 128])
    ong = sb(L + "ong", [128, 1])
    P.op("sp", "dma_start", out=onr[:], in_=dr["hg_onorm"][j])
    transpose_small(g, ong[:], onr[0:1, :], 1, g.psum[7])
    ones_m = sb(L + "ones_m", [128, 128])
    epsc = sb(L + "epsc", [128, 1])
    P.op("dve", "memset", ap=epsc[:], constant=EPS)
    P.op("dve", "tensor_scalar", out=ones_m[:], in0=g.c["ones_f"][:], scalar1=1.0 / 128, scalar2=None, op0=ALU.mult)

    xb = [sb(L + "xb%d" % i, [128, D]) for i in range(2)]
    xnb = [sb(L + "xn%d" % i, [128, D], BF16) for i in range(2)]
    ss = sb(L + "ss", [128, 4])
    rstd = sb(L + "rstd", [128, 4])
    hT = sb(L + "hT", [128, KT, MT], BF16)
    t_q = sb(L + "t_q", [128, MT])
    t_f = sb(L + "t_f", [128, MT])
    t_k = sb(L + "t_k", [128, MT])
    t_l = sb(L + "t_l", [128, MT])
    t_b = sb(L + "t_b", [128, MT])
    t_bm = sb(L + "t_bm", [128, MT])
    t_e1 = sb(L + "t_e1", [128, MT])
    t_e2 = sb(L + "t_e2", [128, MT])
    qT = [sb(L + "qT%d" % i, [128, MT], BF16) for i in range(2)]
    kT = [sb(L + "kT%d" % i, [128, MT], BF16) for i in range(2)]
    ktok = [sb(L + "ktok%d" % i, [CH, NCH, 128], BF16) for i in range(2)]
    vtok = [sb(L + "vtok%d" % i, [CH, NCH, 128], BF16) for i in range(2)]
    vT = sb(L + "vT", [128, MT], BF16)
    zs = [sb(L + "zs%d" % i, [128, MT], BF16) for i in range(2)]
    er = sb(L + "er", [128, NCH])
    ebl = sb(L + "ebl", [128, NCH])
    eblr = sb(L + "eblr", [128, NCH])
    atm = [sb(L + "atm%d" % i, [CH, MT], BF16) for i in range(2)]
    for i in range(2):
        P.op("pool", "memset", ap=atm[i][:], constant=0.0)
    S = [sb(L + "S%d" % h, [128, 128]) for h in range(8)]
    srb = [sb(L + "srb%d" % i, [128, 128], BF16) for i in range(2)]
    stmp = [sb(L + "stmp%d" % i, [128, 128]) for i in range(2)]
    osq = sb(L + "osq", [128, MT])
    orstd = sb(L + "orstd", [128, MT])
    otmp = sb(L + "otmp", [128, MT])
    og = sb(L + "og", [128, 8, MT], BF16)
    for h in range(8):
        P.op("pool", "memset", ap=S[h][:], constant=0.0)

    ps = g.psum
    PS_Q, PS_F, PS_Z, PS_V, PS_A, PS_O, PS_S, PS_X = range(8)
    P.mark("after_wload")
    for mt in range(T // MT):
        prologue(g, mt, src, srctag, hT, xb, xnb, ss, rstd)
        P.mark("after_prologue%d" % mt)
        for h in range(8):
            pp = h % 2
            P.mark("head%d_%d" % (mt, h))
            for (pst, c0) in ((PS_Q, h * 128), (PS_F, 1024 + h * 128), (PS_V, 2048 + h * 128), (PS_Z, 3072 + h * 128)):
                for k in range(KT):
                    P.op("pe", "matmul", out=ps[pst][:], lhsT=W(w_in[:, k, c0:c0 + 128], k), rhs=Sub(hT[:, k, :], k),
                         start=(k == 0), stop=(k == KT - 1))
            P.mark("gates%d_%d" % (mt, h))
            P.op("act", "activation", out=t_q[:], in_=ps[PS_Q][:], func=AF.Silu)
            P.op("act", "activation", out=t_f[:], in_=ps[PS_F][:], func=AF.Sigmoid)
            P.op("act", "activation", out=zs[pp][:], in_=ps[PS_Z][:], func=AF.Silu)
            P.op("act", "activation", out=vT[:], in_=ps[PS_V][:], func=AF.Copy)
            P.op("dve", "tensor_scalar", out=t_k[:], in0=t_f[:], scalar1=noml[:, h:h + 1], scalar2=oml[:, h:h + 1],
                 op0=ALU.mult, op1=ALU.add)
            P.op("act", "activation", out=t_l[:], in_=t_f[:], func=AF.Ln, scale=oml[:, h:h + 1], bias=lb[:, h:h + 1])
            P.op("dve", "tensor_tensor_scan", out=t_b[:], data0=g.c["resetmask%d" % CH][:], data1=t_l[:], initial=0.0,
                 op0=ALU.mult, op1=ALU.add)
            b3 = t_b[:].rearrange("p (c j) -> p c j", j=CH)
            bm3 = t_bm[:].rearrange("p (c j) -> p c j", j=CH)
            RI = CH // 2 - 1
            P.op("dve", "tensor_tensor", out=bm3, in0=b3, in1=b3[:, :, RI:RI + 1].to_broadcast([128, NCH, CH]),
                 op=ALU.subtract)
            P.op("act", "activation", out=t_e1[:], in_=t_bm[:], func=AF.Exp)
            P.op("act", "activation", out=t_e2[:], in_=t_bm[:], func=AF.Exp, scale=-1.0)
            P.op("act", "activation", out=er[:], in_=b3[:, :, RI], func=AF.Exp)
            P.op("act", "activation", out=ebl[:], in_=b3[:, :, CH - 1], func=AF.Exp)
            P.op("act", "activation", out=eblr[:], in_=bm3[:, :, CH - 1], func=AF.Exp)
            P.op("pool", "tensor_tensor", out=qT[pp][:], in0=t_q[:], in1=t_e1[:], op=ALU.mult)
            P.op("dve", "tensor_tensor", out=kT[pp][:], in0=t_k[:], in1=t_e2[:], op=ALU.mult)
            P.mark("tok%d_%d" % (mt, h))
            psb = ps[PS_X][:].bitcast(BF16)
            for (srcT, dstk) in ((kT[pp], ktok[pp]), (vT, vtok[pp])):
                for c0 in range(0, NCH, 8):
                    for c in range(c0, c0 + 8):
                        P.op("pe", "transpose", out=psb[0:CH, (c - c0) * 128:(c - c0 + 1) * 128],
                             in_=srcT[:, c * CH:(c + 1) * CH], identity=g.c["ident_b"][:])
                    P.op("act", "activation", out=dstk[:, c0:c0 + 8, :].rearrange("p a b -> p (a b)"), in_=psb[0:CH, :],
                         func=AF.Copy)
            a = atm[pp]
            for c in range(NCH):
                tcs = slice(c * CH, (c + 1) * CH)
                P.op("pe", "matmul", out=ps[PS_A][0:CH, tcs], lhsT=kT[pp][:, tcs], rhs=qT[pp][:, tcs], start=True, stop=True)
            P.op("dve", "copy_predicated", out=a[:], mask=g.c["hg_maskP%d" % CH][:], data=ps[PS_A][0:CH, :])
            P.mark("scan%d_%d" % (mt, h))
            for c in range(NCH):
                tcs = slice(c * CH, (c + 1) * CH)
                sr = srb[c % 2]
                tmp = stmp[c % 2]
                P.op("pool", "tensor_scalar", out=sr[:], in0=S[h][:], scalar1=er[:, c:c + 1], scalar2=None, op0=ALU.mult)
                P.op("pe", "matmul", out=ps[PS_O][:, tcs], lhsT=sr[:], rhs=qT[pp][:, tcs], start=True, stop=False)
                P.op("pe", "matmul", out=ps[PS_O][:, tcs], lhsT=vtok[pp][:, c, :], rhs=a[:, tcs], start=False, stop=True)
                P.op("pe", "matmul", out=ps[PS_S][:, 0:128], lhsT=ktok[pp][:, c, :], rhs=vtok[pp][:, c, :],
                     start=True, stop=True)
                P.op("pool", "tensor_scalar", out=tmp[:], in0=S[h][:], scalar1=ebl[:, c:c + 1], scalar2=None, op0=ALU.mult)
                P.op("dve", "scalar_tensor_tensor", out=S[h][:], in0=ps[PS_S][:, 0:128], scalar=eblr[:, c:c + 1],
                     in1=tmp[:], op0=ALU.mult, op1=ALU.add)
            P.mark("onorm%d_%d" % (mt, h))
            P.op("act", "activation", out=osq[:], in_=ps[PS_O][:], func=AF.Square)
            P.op("pe", "matmul", out=ps[PS_A][:], lhsT=ones_m[:], rhs=osq[:], start=True, stop=True)
            P.op("act", "activation", out=orstd[:], in_=ps[PS_A][:], func=AF.Sqrt, bias=epsc[:, 0:1])
            P.op("dve", "reciprocal", out=orstd[:], in_=orstd[:])
            P.op("dve", "scalar_tensor_tensor", out=otmp[:], in0=ps[PS_O][:], scalar=ong[:, 0:1], in1=orstd[:],
                 op0=ALU.mult, op1=ALU.mult)
            P.op("pool", "tensor_tensor", out=Sub(og[:, h, :], h), in0=otmp[:], in1=zs[pp][:], op=ALU.mult)
        P.mark("epilogue%d" % mt)
        epilogue(g, mt, src, srctag, dst, og, KT, w_out, xb, ps[PS_Q], ps[PS_F])


def swa_layer(g, layer, src, srctag, dst):
    import math
    P, sb, dr, nc = g.P, g.sb, g.dr, g.nc
    T = g.T
    P.mark("prep_layer")
    prep_layer(g, layer)
    L = "L%d_" % layer
    NB = MT // 128
    g.warena = sb(L + "warena", [128, KT * 4096], BF16)
    w_in = g.warena[:, 0:KT * 3072].rearrange("p (k n) -> p k n", k=KT)
    w_out = g.warena[:, KT * 3072:KT * 3072 + KT * 1024].rearrange("p (k n) -> p k n", k=KT)
    WQ, WK, WV, WZ = 0, 1024, 1536, 2048
    win = dr["sw_in_w"][0]
    i = 0
    for k in range(KT):
        for (c0, w, dup, d0) in ((0, 1024, False, WQ), (1024, 256, True, WK), (1280, 256, True, WV), (1536, 1024, False, WZ)):
            stg = g.stage[i % 2]
            eng = ("dve", "pool")[i % 2]
            i += 1
            P.op("sp", "dma_start", out=stg[:, 0:w], in_=win[k * 128:(k + 1) * 128, c0:c0 + w])
            if not dup:
                P.op(eng, "tensor_copy", out=W(w_in[:, k, d0:d0 + w], k), in_=stg[:, 0:w])
            else:
                dv = w_in[:, k, d0:d0 + 512].rearrange("p (h r d) -> p h r d", h=4, r=2)
                sv = stg[:, 0:256].rearrange("p (h d) -> p h d", h=4)
                for r in range(2):
                    P.op(eng, "tensor_copy", out=W(dv[:, :, r, :], k), in_=sv)
    load_weight_bf16(g, w_out, dr["sw_out_w"][0], 1024, colscale=g.gate_b, kbase=100)

    gq = sb(L + "gq", [128, 1])
    gk = sb(L + "gk", [128, 1])
    for (dst_, nm) in ((gq, "sw_qnorm"), (gk, "sw_knorm")):
        r1 = sb(L + nm + "_r", [1, 128])
        P.op("sp", "dma_start", out=r1[0:1, 0:64], in_=dr[nm][0:1, :])
        P.op("sp", "dma_start", out=r1[0:1, 64:128], in_=dr[nm][0:1, :])
        transpose_small(g, dst_[:], r1[0:1, :], 1, g.psum[7])
    epsc = sb(L + "epsc", [128, 1])
    P.op("dve", "memset", ap=epsc[:], constant=EPS)
    negpi = sb(L + "negpi", [128, 1])
    P.op("dve", "memset", ap=negpi[:], constant=-math.pi)
    cpi = sb(L + "cpi", [128, 3])
    P.op("dve", "memset", ap=cpi[:, 0:1], constant=math.pi)
    P.op("dve", "memset", ap=cpi[:, 1:2], constant=1.5 * math.pi)
    P.op("dve", "memset", ap=cpi[:, 2:3], constant=2 * math.pi)
    sk_r = sb(L + "sk_r", [1, 16])
    P.op("sp", "dma_start", out=sk_r[:], in_=dr["sw_sinks"][0:1, :])
    P.op("act", "activation", out=sk_r[:], in_=sk_r[:], func=AF.Exp)
    P.op("pe", "matmul", out=g.psum[6][:, 0:16], lhsT=g.c["ones_f"][0:1, :], rhs=sk_r[0:1, :], start=True, stop=True)
    esb = sb(L + "esb", [128, 16])
    P.op("dve", "tensor_copy", out=esb[:], in_=g.psum[6][:, 0:16])
    esb4 = esb[:].rearrange("p (h j s) -> p h j s", h=4, j=2)
    esb2 = sb(L + "esb2", [128, 4, 2])
    P.op("dve", "tensor_copy", out=esb2[0:64, :, :], in_=esb4[0:64, :, :, 0])
    P.op("dve", "tensor_copy", out=esb2[64:128, :, :], in_=esb4[64:128, :, :, 1])
    esf = sb(L + "esf", [128, 4, MT])
    for h in range(4):
        for sc in range(2):
            P.op("dve", "tensor_copy", out=esf[:, h, sc * 256:(sc + 1) * 256].rearrange("p (j q) -> p j q", j=2),
                 in_=esb2[:, h, :].unsqueeze(2).to_broadcast([128, 2, 128]))
    ones_b = sb(L + "ones_b", [128, 128], BF16)
    P.op("dve", "tensor_copy", out=ones_b[:], in_=g.c["ones_f"][:])

    xb = [sb(L + "xb%d" % i, [128, D]) for i in range(2)]
    xnb = [sb(L + "xn%d" % i, [128, D], BF16) for i in range(2)]
    ss = sb(L + "ss", [128, 4])
    rstd = sb(L + "rstd", [128, 4])
    hT = sb(L + "hT", [128, KT, MT], BF16)
    cos2 = sb(L + "cos2", [128, MT])
    sin2 = sb(L + "sin2", [128, MT])
    t_sq = sb(L + "t_sq", [128, MT])
    t_rs = sb(L + "t_rs", [128, MT])
    t_qn = sb(L + "t_qn", [128, MT])
    t_a = sb(L + "t_a", [128, MT])
    t_b = sb(L + "t_b", [128, MT])
    posi, ang, ua = t_a[:].bitcast(I32), t_b, t_sq
    qg = sb(L + "qg", [128, 8, MT], BF16)
    kg = sb(L + "kg", [128, 4, (NB + 1) * 128], BF16)
    vd = sb(L + "vd", [128, NB + 1, 512], BF16)
    zs = sb(L + "zs", [128, 8, MT], BF16)
    pT = [sb(L + "pT%d" % i, [128, MT], BF16) for i in range(2)]
    dtmp, otmp = t_rs, t_qn
    og = sb(L + "og", [128, 8, MT], BF16)
    ps = g.psum
    PS_A, PS_B, PS_N, PS_R, PS_S0, PS_S1, PS_O, PS_D = range(8)

    def normrope(pst, gain, out_ap):
        P.op("act", "activation", out=t_sq[:], in_=pst[:], func=AF.Square)
        P.op("pe", "matmul", out=ps[PS_N][:], lhsT=g.c["sw_ones_blk"][:], rhs=t_sq[:], start=True, stop=True)
        P.op("act", "activation", out=t_rs[:], in_=ps[PS_N][:], func=AF.Sqrt, bias=epsc[:, 0:1])
        P.op("dve", "reciprocal", out=t_rs[:], in_=t_rs[:])
        P.op("dve", "scalar_tensor_tensor", out=t_qn[:], in0=pst[:], scalar=gain[:, 0:1], in1=t_rs[:],
             op0=ALU.mult, op1=ALU.mult)
        P.op("pe", "matmul", out=ps[PS_R][:], lhsT=g.c["sw_rotT"][:], rhs=t_qn[:], start=True, stop=True)
        P.op("pool", "tensor_tensor", out=t_a[:], in0=t_qn[:], in1=cos2[:], op=ALU.mult)
        P.op("dve", "tensor_tensor", out=t_b[:], in0=ps[PS_R][:], in1=sin2[:], op=ALU.mult)
        P.op("pool", "tensor_tensor", out=out_ap, in0=t_a[:], in1=t_b[:], op=ALU.add)

    for mt in range(T // MT):
        prologue(g, mt, src, srctag, hT, xb, xnb, ss, rstd)
        P.mark("sw_rope%d" % mt)
        P.op("sp", "dma_start", out=posi, in_=dr["positions"][0:1, mt * MT:(mt + 1) * MT].partition_broadcast(128))
        P.op("dve", "tensor_copy", out=ang[:], in_=posi)
        P.op("dve", "tensor_scalar", out=ang[:], in0=ang[:], scalar1=g.c["sw_invfreq"][:, 0:1], scalar2=None, op0=ALU.mult)
        C1 = 6.28125
        C2 = 2 * math.pi - C1
        P.op("dve", "tensor_scalar", out=ua[:], in0=ang[:], scalar1=1.0 / (2 * math.pi), scalar2=None, op0=ALU.mult)
        P.op("dve", "tensor_copy", out=posi, in_=ua[:])
        P.op("dve", "tensor_copy", out=ua[:], in_=posi)
        P.op("dve", "scalar_tensor_tensor", out=ang[:], in0=ua[:], scalar=-C1, in1=ang[:], op0=ALU.mult, op1=ALU.add)
        P.op("dve", "scalar_tensor_tensor", out=ang[:], in0=ua[:], scalar=-C2, in1=ang[:], op0=ALU.mult, op1=ALU.add)
        P.op("dve", "tensor_scalar", out=ua[:], in0=ang[:], scalar1=math.pi, scalar2=None, op0=ALU.is_gt)
        P.op("dve", "scalar_tensor_tensor", out=ang[:], in0=ua[:], scalar=-2 * math.pi, in1=ang[:], op0=ALU.mult, op1=ALU.add)
        P.op("dve", "tensor_scalar", out=ang[:], in0=ang[:], scalar1=math.pi, scalar2=-math.pi, op0=ALU.min, op1=ALU.max)
        P.op("act", "activation", out=sin2[:], in_=ang[:], func=AF.Sin)
        P.op("act", "activation", out=ua[:], in_=ang[:], func=AF.Sin, scale=0.5)
        P.op("dve", "tensor_tensor", out=ua[:], in0=ua[:], in1=ua[:], op=ALU.mult)
        P.op("dve", "tensor_scalar", out=cos2[:], in0=ua[:], scalar1=-2.0, scalar2=1.0, op0=ALU.mult, op1=ALU.add)
        P.mark("sw_proj%d" % mt)
        for p in range(8):
            pst = ps[PS_A + p % 2]
            for k in range(KT):
                P.op("pe", "matmul", out=pst[:], lhsT=W(w_in[:, k, WQ + p * 128:WQ + (p + 1) * 128], k), rhs=Sub(hT[:, k, :], k),
                     start=(k == 0), stop=(k == KT - 1))
            normrope(pst, gq, Sub(qg[:, p, :], p))
        for h in range(4):
            pst = ps[PS_A + h % 2]
            for k in range(KT):
                P.op("pe", "matmul", out=pst[:], lhsT=W(w_in[:, k, WK + h * 128:WK + (h + 1) * 128], k), rhs=Sub(hT[:, k, :], k),
                     start=(k == 0), stop=(k == KT - 1))
            normrope(pst, gk, Sub(kg[:, h, 128:128 + MT], ("k", h)))
        for blk in range(NB):
            pst = ps[PS_A + blk % 2]
            for k in range(KT):
                P.op("pe", "matmul", out=pst[:], lhsT=Sub(hT[:, k, blk * 128:(blk + 1) * 128], k),
                     rhs=W(w_in[:, k, WV:WV + 512], k), start=(k == 0), stop=(k == KT - 1))
            P.op("act", "activation", out=Sub(vd[:, 1 + blk, :], 1 + blk), in_=pst[:], func=AF.Copy)
        for p in range(8):
            pst = ps[PS_A + p % 2]
            for k in range(KT):
                P.op("pe", "matmul", out=pst[:], lhsT=W(w_in[:, k, WZ + p * 128:WZ + (p + 1) * 128], k), rhs=Sub(hT[:, k, :], k),
                     start=(k == 0), stop=(k == KT - 1))
            P.op("act", "activation", out=Sub(zs[:, p, :], p), in_=pst[:], func=AF.Silu)
        P.mark("sw_attn%d" % mt)
        for qb in range(NB):
            qcs = slice(qb * 128, (qb + 1) * 128)
            first = (mt == 0 and qb == 0)
            kbs = ([] if first else [0]) + [1]
            for h in range(4):
                for kb in kbs:
                    kblk = qb + kb
                    kcs = slice(kblk * 128, (kblk + 1) * 128)
                    pt = pT[kb]
                    for sl in range(2):
                        pss = ps[(PS_S0, PS_N)[sl] + kb]
                        rs = slice(sl * 64, (sl + 1) * 64)
                        P.op("pe", "matmul", out=pss[:, 0:256],
                             lhsT=Sub(kg[rs, h, kcs], ("k", h)) if kblk > 0 else Sub(kg[rs, h, kcs], ("kh", h)),
                             rhs=qg[rs, 2 * h:2 * h + 2, qcs], start=True, stop=True)
                        P.op("act", "activation", out=pt[:, sl * 256:(sl + 1) * 256], in_=pss[:, 0:256], func=AF.Exp, scale=0.125)
                    P.op("pool", "tensor_tensor", out=pt[:], in0=pt[:], in1=g.c["sw_mask_prev" if kb == 0 else "sw_mask_cur"][:],
                         op=ALU.mult)
                for ii, kb in enumerate(kbs):
                    kblk = qb + kb
                    vsub = Sub(vd[:, kblk, h * 128:(h + 1) * 128], kblk)
                    P.op("pe", "matmul", out=ps[PS_O][:], lhsT=vsub, rhs=pT[kb][:], start=(ii == 0), stop=(ii == len(kbs) - 1))
                for ii, kb in enumerate(kbs):
                    P.op("pe", "matmul", out=ps[PS_D][:], lhsT=ones_b[:], rhs=pT[kb][:], start=(ii == 0), stop=(ii == len(kbs) - 1))
                P.op("dve", "tensor_tensor", out=dtmp[:], in0=ps[PS_D][:], in1=esf[:, h, :], op=ALU.add)
                P.op("dve", "reciprocal", out=dtmp[:], in_=dtmp[:])
                P.op("dve", "tensor_tensor", out=otmp[:], in0=ps[PS_O][:], in1=dtmp[:], op=ALU.mult)
                for sl in range(2):
                    rs = slice(sl * 64, (sl + 1) * 64)
                    P.op("pool", "tensor_tensor", out=og[rs, 2 * h:2 * h + 2, qcs],
                         in0=otmp[rs, sl * 256:(sl + 1) * 256].rearrange("p (j q) -> p j q", j=2),
                         in1=zs[rs, 2 * h:2 * h + 2, qcs], op=ALU.mult)
        for h in range(4):
            P.op("pool", "tensor_copy", out=Sub(kg[:, h, 0:128], ("kh", h)), in_=Sub(kg[:, h, NB * 128:(NB + 1) * 128], ("k", h)))
        P.op("pool", "tensor_copy", out=Sub(vd[:, 0, :], 0), in_=Sub(vd[:, NB, :], NB))
        P.mark("epilogue%d" % mt)
        epilogue(g, mt, src, srctag, dst, og, KT, w_out, xb, ps[PS_A], ps[PS_B], ogtag=False)


def gdn_layer(g, layer, src, srctag, dst):
    P, sb, dr, nc = g.P, g.sb, g.dr, g.nc
    T = g.T
    GT = 256
    NC = GT // 64
    P.mark("prep_layer")
    prep_layer(g, layer)
    L = "L%d_" % layer
    NRES = 2080
    g.warena = sb(L + "warena", [128, KT * NRES + 16 * 1024], BF16)
    w_in = g.warena[:, 0:KT * NRES].rearrange("p (k n) -> p k n", k=KT)
    w_out = g.warena[:, KT * NRES:KT * NRES + 16 * 1024].rearrange("p (k n) -> p k n", k=16)
    gin = dr["gd_in_w"][0]
    load_weight_bf16(g, w_in[:, :, 0:2048], gin[:, 0:2048], 2048)
    load_weight_bf16(g, w_in[:, :, 2048:2080], gin[:, 6144:6176], 32)
    load_weight_bf16(g, w_out, dr["gd_out_w"][0], 1024, colscale=g.gate_b, kbase=100)
    wscr = g.gd_wscr
    cbf = [sb(L + "cbf%d" % i, [128, SW], BF16) for i in range(2)]
    i = 0
    for k in range(KT):
        for ch in range(4):
            stg = g.stage[i % 2]
            cb = cbf[i % 2]
            P.op("sp", "dma_start", out=stg[:, 0:1024], in_=gin[k * 128:(k + 1) * 128, 2048 + ch * 1024:2048 + (ch + 1) * 1024])
            P.op(("dve", "pool")[i % 2], "tensor_copy", out=cb[:], in_=stg[:, 0:1024])
            P.op("sp", "dma_start", out=wscr[ch * 8:(ch + 1) * 8, :, k * 128:(k + 1) * 128].rearrange("h p n -> p h n"),
                 in_=cb[:].rearrange("p (h n) -> p h n", h=8))
            i += 1
    P.barrier()
    wbuf = [sb(L + "wbuf%d" % i, [128, 1024], BF16) for i in range(2)]
    g.wcnt = 0

    def stream_w(hd):
        wb = wbuf[g.wcnt % 2]
        g.wcnt += 1
        P.op("sp", "dma_start", out=wb[:], in_=wscr[hd])
        return wb
    ps = g.psum
    PA, PB, PX, PQ, PD, PM, PN, PT = range(8)
    PO = PD
    cwr = sb(L + "cwr", [4, 4096])
    P.op("sp", "dma_start", out=cwr[:], in_=dr["gd_conv_w"][:, :])
    for ct in range(32):
        P.op("pe", "transpose", out=ps[PA][:, ct * 4:(ct + 1) * 4], in_=cwr[0:4, ct * 128:(ct + 1) * 128],
             identity=g.c["ident_f"][0:4, 0:4])
    cw = sb(L + "cw", [128, 128])
    P.op("dve", "tensor_copy", out=cw[:], in_=ps[PA][:, 0:128])
    alog = sb(L + "alog", [16, 1])
    dtb = sb(L + "dtb", [16, 1])
    P.op("sp", "dma_start", out=alog[:], in_=dr["gd_a_log"][:, :])
    P.op("sp", "dma_start", out=dtb[:], in_=dr["gd_dt_bias"][:, :])
    nA = sb(L + "nA", [16, 1])
    P.op("act", "activation", out=nA[:], in_=alog[:], func=AF.Exp)
    P.op("dve", "tensor_scalar", out=nA[:], in0=nA[:], scalar1=-1.0, scalar2=None, op0=ALU.mult)
    one16 = sb(L + "one16", [16, 1])
    P.op("dve", "memset", ap=one16[:], constant=1.0)
    onr = sb(L + "onr", [1, 128])
    ong = sb(L + "ong", [128, 1])
    P.op("sp", "dma_start", out=onr[:], in_=dr["gd_onorm"][:, :])
    transpose_small(g, ong[:], onr[0:1, :], 1, ps[PB])
    ones_m = sb(L + "ones_m", [128, 128])
    P.op("dve", "tensor_scalar", out=ones_m[:], in0=g.c["ones_f"][:], scalar1=1.0 / 128, scalar2=None, op0=ALU.mult)
    epsc = sb(L + "epsc", [128, 1])
    P.op("dve", "memset", ap=epsc[:], constant=EPS)
    identb64 = g.c["ident_b"]

    xb = [sb(L + "xb0", [128, D])]
    xnb = [sb(L + "xn0", [128, D], BF16)]
    ss = sb(L + "ss", [128, 4])
    rstd = sb(L + "rstd", [128, 4])
    hT = sb(L + "hT", [128, KT, GT], BF16)
    halo = sb(L + "halo", [128, 32, 3])
    P.op("pool", "memset", ap=halo[:], constant=0.0)
    xc = sb(L + "xc", [128, 3 + GT])
    acc = sb(L + "acc", [128, GT])
    f_a = sb(L + "f_a", [128, GT])
    f_b = sb(L + "f_b", [128, GT])
    f_c = sb(L + "f_c", [128, GT])
    qT = sb(L + "qT", [128, GT], BF16)
    kT = sb(L + "kT", [128, GT], BF16)
    kbT = sb(L + "kbT", [128, GT], BF16)
    qd = [sb(L + "qd%d" % i, [128, GT], BF16) for i in range(2)]
    vT = sb(L + "vT", [128, GT], BF16)
    zs = [sb(L + "zs%d" % i, [128, GT], BF16) for i in range(2)]
    edl4 = [sb(L + "edl4_%d" % i, [128, NC]) for i in range(2)]
    r_beta = sb(L + "r_beta", [16, GT])
    r_g = sb(L + "r_g", [16, GT])
    r_d = sb(L + "r_d", [16, GT])
    r_be = sb(L + "r_be", [16, GT])
    r_edl = sb(L + "r_edl", [16, GT])
    tk_b = sb(L + "tk_b", [64, 16, NC])
    tk_be = sb(L + "tk_be", [64, 16, NC])
    tk_edl = sb(L + "tk_edl", [64, 16, NC])
    tk_d = sb(L + "tk_d", [64, 16, NC])
    Et = sb(L + "Et", [64, GT])
    Em = sb(L + "Em", [64, GT])
    Mk = [sb(L + "Mk%d" % i, [64, 512]) for i in range(2)]
    Nk = [sb(L + "Nk%d" % i, [64, 512]) for i in range(2)]
    Tk = [sb(L + "Tk%d" % i, [64, 512]) for i in range(2)]
    Tbf = sb(L + "Tbf", [64, 512], BF16)
    qkM = sb(L + "qkM", [64, 512], BF16)
    vb = sb(L + "vb", [64, 8, 128], BF16)
    kbd = sb(L + "kbd", [64, 8, 128], BF16)
    khat = sb(L + "khat", [64, 8, 128], BF16)
    wTn = sb(L + "wTn", [128, 512], BF16)
    vn = [sb(L + "vn%d" % i, [64, 128], BF16) for i in range(2)]
    S = [sb(L + "S%d" % h, [128, 128]) for h in range(16)]
    Sb = [sb(L + "Sb%d" % h, [128, 128], BF16) for h in range(16)]
    for h in range(16):
        P.op("pool", "memset", ap=S[h][:], constant=0.0)
        P.op("pool", "memset", ap=Sb[h][:], constant=0.0)
    og = sb(L + "og", [128, 16, GT], BF16)
    identI = sb(L + "identI", [64, 512])
    for b in range(8):
        P.op("dve", "tensor_copy", out=identI[:, b * 64:(b + 1) * 64], in_=g.c["ident_f"][0:64, 0:64])
    rm16 = g.c["gd_rm"][:, :]
    sel = g.c["gd_sel"]

    def conv_silu(pst, ct, out_ap, func=AF.Silu):
        P.op("pool", "tensor_copy", out=xc[:, 0:3], in_=halo[:, ct, :])
        P.op("act", "activation", out=xc[:, 3:3 + GT], in_=pst[:, 0:GT], func=AF.Copy)
        P.op("pool", "tensor_copy", out=halo[:, ct, :], in_=xc[:, GT:GT + 3])
        P.op("dve", "tensor_scalar", out=acc[:], in0=xc[:, 3:3 + GT], scalar1=cw[:, ct * 4 + 3:ct * 4 + 4], scalar2=None,
             op0=ALU.mult)
        for jt in (2, 1, 0):
            P.op("dve", "scalar_tensor_tensor", out=acc[:], in0=xc[:, jt:jt + GT], scalar=cw[:, ct * 4 + jt:ct * 4 + jt + 1],
                 in1=acc[:], op0=ALU.mult, op1=ALU.add)
        P.op("act", "activation", out=out_ap, in_=acc[:], func=func)

    def inproj(pst, c0, m=128):
        for k in range(KT):
            P.op("pe", "matmul", out=pst, lhsT=W(w_in[:, k, c0:c0 + m], k), rhs=Sub(hT[:, k, :], k),
                 start=(k == 0), stop=(k == KT - 1))

    def inproj_s(pst, hd):
        wb = stream_w(hd)
        for k in range(KT):
            P.op("pe", "matmul", out=pst, lhsT=wb[:, k * 128:(k + 1) * 128], rhs=Sub(hT[:, k, :], k),
                 start=(k == 0), stop=(k == KT - 1))

    def l2norm(src_f, out_bf, scale):
        P.op("pool", "tensor_tensor", out=f_b[:], in0=src_f[:], in1=src_f[:], op=ALU.mult)
        P.op("pe", "matmul", out=ps[PB][:, 0:GT], lhsT=g.c["ones_f"][:], rhs=f_b[:], start=True, stop=True)
        P.op("act", "activation", out=f_c[:], in_=ps[PB][:, 0:GT], func=AF.Sqrt, bias=epsc[:, 0:1])
        P.op("dve", "reciprocal", out=f_c[:], in_=f_c[:])
        P.op("dve", "scalar_tensor_tensor", out=out_bf, in0=src_f[:], scalar=scale, in1=f_c[:], op0=ALU.mult, op1=ALU.mult)

    for mt in range(T // GT):
        prologue(g, mt, src, srctag, hT, xb, xnb, ss, rstd, mtl=GT)
        P.mark("gd_rows%d" % mt)
        inproj(ps[PA][0:16, 0:GT], 2048, 16)
        inproj(ps[PA][0:16, GT:2 * GT], 2064, 16)
        P.op("act", "activation", out=r_beta[:], in_=ps[PA][0:16, GT:2 * GT], func=AF.Sigmoid)
        P.op("act", "activation", out=r_g[:], in_=ps[PA][0:16, 0:GT], func=AF.Exp, bias=dtb[:, 0:1])
        P.op("act", "activation", out=r_g[:], in_=r_g[:], func=AF.Ln, bias=one16[:, 0:1])
        P.op("dve", "tensor_scalar", out=r_g[:], in0=r_g[:], scalar1=nA[:, 0:1], scalar2=None, op0=ALU.mult)
        P.op("dve", "tensor_tensor_scan", out=r_d[:], data0=rm16, data1=r_g[:], initial=0.0, op0=ALU.mult, op1=ALU.add)
        P.op("act", "activation", out=r_be[:], in_=r_d[:], func=AF.Exp)
        P.op("dve", "tensor_tensor", out=r_be[:], in0=r_be[:], in1=r_beta[:], op=ALU.mult)
        d3 = r_d[:].rearrange("p (c j) -> p c j", j=64)
        P.op("dve", "tensor_tensor", out=r_edl[:].rearrange("p (c j) -> p c j", j=64), in0=d3,
             in1=d3[:, :, 63:64].to_broadcast([16, NC, 64]), op=ALU.subtract)
        P.op("act", "activation", out=r_edl[:], in_=r_edl[:], func=AF.Exp, scale=-1.0)
        for (row, tok) in ((r_beta, tk_b), (r_be, tk_be), (r_edl, tk_edl), (r_d, tk_d)):
            for c in range(NC):
                P.op("pe", "transpose", out=ps[PB][0:64, c * 16:(c + 1) * 16], in_=row[0:16, c * 64:(c + 1) * 64],
                     identity=g.c["ident_f"][0:16, 0:16])
            P.op("dve", "tensor_copy", out=tok[:].rearrange("p h c -> p c h"),
                 in_=ps[PB][0:64, 0:NC * 16].rearrange("p (c h) -> p c h", h=16))
        for gq in range(8):
            P.mark("gd_g%d_%d" % (mt, gq))
            inproj(ps[PA][:, 0:GT], gq * 128)
            conv_silu(ps[PA], gq, f_a[:])
            l2norm(f_a, qT[:], 128 ** -0.5)
            inproj(ps[PA][:, 0:GT], 1024 + gq * 128)
            conv_silu(ps[PA], 8 + gq, f_a[:])
            l2norm(f_a, kT[:], 1.0)
            pxb = ps[PX][:].bitcast(BF16)
            for c in range(NC):
                P.op("pe", "transpose", out=pxb[0:64, c * 128:(c + 1) * 128], in_=kT[:, c * 64:(c + 1) * 64],
                     identity=g.c["ident_b"][:])
            for hh in range(2):
                h = 2 * gq + hh
                bs = slice(hh * NC, (hh + 1) * NC)
                for (tok, dstk) in ((tk_be, kbd), (tk_edl, khat)):
                    P.op("dve", "tensor_tensor", out=dstk[:, bs, :],
                         in0=pxb[0:64, 0:NC * 128].rearrange("p (c d) -> p c d", d=128),
                         in1=tok[:, h, :].unsqueeze(2).to_broadcast([64, NC, 128]), op=ALU.mult)
            for c in range(NC):
                P.op("pe", "matmul", out=ps[PQ][0:64, c * 64:(c + 1) * 64], lhsT=kT[:, c * 64:(c + 1) * 64],
                     rhs=qT[:, c * 64:(c + 1) * 64], start=True, stop=True)
            for hh in range(2):
                h = 2 * gq + hh
                bs = slice(hh * NC, (hh + 1) * NC)
                cs = slice(hh * GT, (hh + 1) * GT)
                inproj_s(ps[PA][:, 0:GT], h)
                conv_silu(ps[PA], 16 + h, vT[:])
                inproj_s(ps[PB][:, 0:GT], 16 + h)
                P.op("act", "activation", out=zs[hh][:], in_=ps[PB][:, 0:GT], func=AF.Silu)
                for c in range(NC):
                    P.op("pe", "transpose", out=pxb[0:64, c * 128:(c + 1) * 128], in_=vT[:, c * 64:(c + 1) * 64],
                         identity=g.c["ident_b"][:])
                P.op("dve", "tensor_tensor", out=vb[:, bs, :], in0=pxb[0:64, 0:NC * 128].rearrange("p (c d) -> p c d", d=128),
                     in1=tk_b[:, h, :].unsqueeze(2).to_broadcast([64, NC, 128]), op=ALU.mult)
                P.op("pe", "matmul", out=ps[PD][:, 0:GT], lhsT=sel[0:16, h * 128:(h + 1) * 128], rhs=r_beta[:], start=True, stop=True)
                P.op("pe", "matmul", out=ps[PD][:, GT:2 * GT], lhsT=sel[0:16, h * 128:(h + 1) * 128], rhs=r_d[:], start=True, stop=True)
                P.op("dve", "tensor_tensor", out=kbT[:], in0=kT[:], in1=ps[PD][:, 0:GT], op=ALU.mult)
                P.op("act", "activation", out=f_b[:], in_=ps[PD][:, GT:2 * GT], func=AF.Exp)
                P.op("pool", "tensor_tensor", out=qd[hh][:], in0=qT[:], in1=f_b[:], op=ALU.mult)
                P.op("act", "activation", out=edl4[hh][:], in_=ps[PD][:, GT:2 * GT].rearrange("p (c j) -> p c j", j=64)[:, :, 63],
                     func=AF.Exp)
                P.op("dve", "tensor_tensor", out=Et[:].rearrange("p (c j) -> p c j", j=64),
                     in0=ps[PD][0:64, GT:2 * GT].rearrange("p (c j) -> p c j", j=64),
                     in1=tk_d[:, h, :].unsqueeze(2).to_broadcast([64, NC, 64]), op=ALU.subtract)
                P.op("dve", "tensor_scalar", out=Et[:], in0=Et[:], scalar1=0.0, scalar2=None, op0=ALU.min)
                P.op("act", "activation", out=Et[:], in_=Et[:], func=AF.Exp)
                for c in range(NC):
                    P.op("pe", "matmul", out=ps[PM][0:64, (hh * NC + c) * 64:(hh * NC + c + 1) * 64],
                         lhsT=kT[:, c * 64:(c + 1) * 64], rhs=kbT[:, c * 64:(c + 1) * 64], start=True, stop=True)
                P.op("pool", "tensor_tensor", out=Em[:], in0=Et[:], in1=g.c["gd_maskS"][:, :], op=ALU.mult)
                P.op("dve", "scalar_tensor_tensor", out=Mk[0][:, cs], in0=ps[PM][0:64, cs], scalar=-1.0, in1=Em[:],
                     op0=ALU.mult, op1=ALU.mult)
                P.op("pool", "tensor_tensor", out=Em[:], in0=Et[:], in1=g.c["gd_maskI"][:, :], op=ALU.mult)
                P.op("dve", "tensor_tensor", out=qkM[:, cs], in0=ps[PQ][0:64, 0:GT], in1=Em[:], op=ALU.mult)
            for b in range(8):
                P.op("pe", "transpose", out=ps[PX][0:64, b * 64:(b + 1) * 64], in_=Mk[0][:, b * 64:(b + 1) * 64],
                     identity=g.c["ident_f"][0:64, 0:64])
            P.op("act", "activation", out=Nk[0][:], in_=ps[PX][0:64, 0:512], func=AF.Copy)
            P.op("pool", "tensor_tensor", out=Tk[0][:], in0=Mk[0][:], in1=identI[:], op=ALU.add)
            for st_ in range(5):
                cur, nxt = st_ % 2, (st_ + 1) % 2
                if st_ < 4:
                    for b in range(8):
                        bc = slice(b * 64, (b + 1) * 64)
                        P.op("pe", "matmul", out=ps[PM][0:64, bc], lhsT=Nk[cur][:, bc], rhs=Mk[cur][:, bc], start=True, stop=True)
                    P.op("dve", "tensor_copy", out=Mk[nxt][:], in_=ps[PM][0:64, :])
                for b in range(8):
                    bc = slice(b * 64, (b + 1) * 64)
                    P.op("pe", "matmul", out=ps[PN][0:64, bc], lhsT=Mk[cur][:, bc], rhs=Nk[cur][:, bc], start=True, stop=True)
                P.op("act", "activation", out=Nk[nxt][:], in_=ps[PN][0:64, :], func=AF.Copy)
                for b in range(8):
                    bc = slice(b * 64, (b + 1) * 64)
                    P.op("pe", "matmul", out=ps[PT][0:64, bc], lhsT=Nk[nxt][:, bc], rhs=Tk[cur][:, bc], start=True, stop=True)
                P.op("dve", "tensor_tensor", out=Tk[nxt][:], in0=ps[PT][0:64, :], in1=Tk[cur][:], op=ALU.add)
            P.op("pool", "tensor_copy", out=Tbf[:], in_=Tk[1][:])
            TT = Tbf
            for b in range(8):
                bc = slice(b * 64, (b + 1) * 64)
                P.op("pe", "matmul", out=ps[PN][:, bc], lhsT=kbd[:, b, :], rhs=TT[:, bc], start=True, stop=True)
            P.op("act", "activation", out=wTn[:], in_=ps[PN][:], func=AF.Copy, scale=-1.0)
            for hh in range(2):
                h = 2 * gq + hh
                for c in range(NC):
                    b = hh * NC + c
                    bc = slice(b * 64, (b + 1) * 64)
                    v_ = vn[c % 2]
                    P.op("pe", "matmul", out=ps[PT][0:64, 0:128], lhsT=TT[:, bc], rhs=vb[:, b, :], start=True, stop=False)
                    P.op("pe", "matmul", out=ps[PT][0:64, 0:128], lhsT=wTn[:, bc], rhs=Sb[h][:], start=False, stop=True)
                    P.op("act", "activation", out=v_[:], in_=ps[PT][0:64, 0:128], func=AF.Copy)
                    P.op("pe", "matmul", out=ps[PO][:, bc], lhsT=Sb[h][:], rhs=qd[hh][:, c * 64:(c + 1) * 64], start=True, stop=False)
                    P.op("pe", "matmul", out=ps[PO][:, bc], lhsT=v_[:], rhs=qkM[:, bc], start=False, stop=True)
                    P.op("pe", "matmul", out=ps[PM][:, 0:128], lhsT=khat[:, b, :], rhs=v_[:], start=True, stop=True)
                    P.op("dve", "scalar_tensor_tensor", out=S[h][:], in0=S[h][:], scalar=edl4[hh][:, c:c + 1], in1=ps[PM][:, 0:128],
                         op0=ALU.mult, op1=ALU.add)
                    P.op("pool", "tensor_copy", out=Sb[h][:], in_=S[h][:])
            for hh in range(2):
                h = 2 * gq + hh
                cs = slice(hh * GT, (hh + 1) * GT)
                P.op("act", "activation", out=f_a[:], in_=ps[PO][:, cs], func=AF.Square)
                P.op("pe", "matmul", out=ps[PB][:, 0:GT], lhsT=ones_m[:], rhs=f_a[:], start=True, stop=True)
                P.op("act", "activation", out=f_c[:], in_=ps[PB][:, 0:GT], func=AF.Sqrt, bias=epsc[:, 0:1])
                P.op("dve", "reciprocal", out=f_c[:], in_=f_c[:])
                P.op("dve", "scalar_tensor_tensor", out=f_b[:], in0=ps[PO][:, cs], scalar=ong[:, 0:1], in1=f_c[:],
                     op0=ALU.mult, op1=ALU.mult)
                P.op("pool", "tensor_tensor", out=Sub(og[:, h, :], h), in0=f_b[:], in1=zs[hh][:], op=ALU.mult)
        P.mark("epilogue%d" % mt)
        epilogue(g, mt, src, srctag, dst, og, 16, w_out, xb, ps[PA], ps[PB], mtl=GT)


def core_inputs(b, T, inputs, consts):
    f = np.float32
    m = {
        "x": np.ascontiguousarray(inputs["x"][b, :T]).astype(f, copy=False),
        "c8": np.ascontiguousarray(inputs["c"][b].reshape(8, 128)),
        "positions": np.ascontiguousarray(inputs["positions"][b, :T].reshape(1, T)).astype(np.int32, copy=False),
        "hgrn_lb": np.ascontiguousarray(inputs["hgrn_lb"].reshape(32, 128)),
        "ada_w": inputs["ada_w"],
        "ada_b": np.ascontiguousarray(inputs["ada_b"].reshape(4, 24, 128)),
        "ada_b_row": np.ascontiguousarray(inputs["ada_b"].reshape(4, 1, 3 * D)),
        "norm_g": np.ascontiguousarray(inputs["norm_g"].reshape(4, 8, 128)),
        "hg_in_w": inputs["hg_in_w"],
        "hg_out_w": inputs["hg_out_w"],
        "hg_onorm": np.ascontiguousarray(inputs["hg_onorm"].reshape(-1, 1, 128)),
        "sw_in_w": inputs["sw_in_w"],
        "sw_out_w": inputs["sw_out_w"],
        "sw_qnorm": inputs["sw_qnorm"],
        "sw_knorm": inputs["sw_knorm"],
        "sw_sinks": inputs["sw_sinks"],
        "gd_in_w": inputs["gd_in_w"],
        "gd_out_w": inputs["gd_out_w"],
        "gd_conv_w": np.ascontiguousarray(inputs["gd_conv_w"][0]),
        "gd_a_log": np.ascontiguousarray(inputs["gd_a_log"].reshape(16, 1)),
        "gd_dt_bias": np.ascontiguousarray(inputs["gd_dt_bias"].reshape(16, 1)),
        "gd_onorm": np.ascontiguousarray(inputs["gd_onorm"].reshape(1, 128)),
    }
    m.update(consts)
    return m


def run_layers(inputs, layers, T, ncores=2, trace=False):
    inputs = {k: np.asarray(v) for k, v in inputs.items()}
    consts = make_consts()
    nc = build(T, layers)
    in_maps = [core_inputs(b, T, inputs, consts) for b in range(ncores)]
    res = run_bass_kernel_spmd(nc, in_maps, core_ids=list(range(ncores)), trace=trace)
    out = np.stack([np.asarray(r["out"]) for r in res.results], axis=0)
    return out, res


def kernel(**inputs):
    T = inputs["x"].shape[1]
    out, _ = run_layers(inputs, [0, 1, 2, 3], T)
    return out.astype(np.float32, copy=False)
```

```python
import numpy as np
import ml_dtypes
from contextlib import ExitStack

import concourse.bass as bass
import concourse.mybir as mybir
from concourse.bass_utils import run_bass_kernel_spmd

F32 = mybir.dt.float32
BF16 = mybir.dt.bfloat16
I32 = mybir.dt.int32
U32 = mybir.dt.uint32
AF = mybir.ActivationFunctionType
ALU = mybir.AluOpType
AX = mybir.AxisListType

D = 1024
KT = 8
EPS = 1e-6
MT = 512
NDMASEM = 40
SW = 1024


class Sub:
    def __init__(self, ap, tag):
        self.ap = ap
        self.tag = tag


class Prog:
    ENGS = ("pe", "act", "dve", "pool", "sp")
    WRITE_KW = ("out", "accum_out", "ap")

    def __init__(self, nc):
        self.nc = nc
        self.ins = []
        self.writers = {}
        self.readers = {}
        self.dma_sem_of = {}
        self.dma_tot = []
        self.same_engine_sync = True
        self._bar_pending = set()
        self._bar_ids = []
        self.marks = {}
        self.psum_excl = True

    def _key(self, v):
        if isinstance(v, Sub):
            return (v.ap.tensor.name, v.tag), v.ap
        return (v.tensor.name, None), v

    def op(self, eng, meth, **kw):
        reads, writes, real = [], [], {}
        for k, v in kw.items():
            if isinstance(v, (Sub, bass.AP)):
                key, ap = self._key(v)
                real[k] = ap
                (writes if k in self.WRITE_KW else reads).append(key)
            else:
                real[k] = v
        i = len(self.ins)
        deps = set()
        if eng in self._bar_pending:
            deps.update(self._bar_ids)
            self._bar_pending.discard(eng)
        for key in reads:
            deps.update(self.writers.get(key, ()))
            if self.psum_excl and key[0].startswith("ps"):
                deps.update(self.readers.get(key, ()))
        for key in writes:
            deps.update(self.writers.get(key, ()))
            deps.update(self.readers.get(key, ()))
        for key in reads:
            self.readers.setdefault(key, []).append(i)
        for key in writes:
            self.writers[key] = [i]
            self.readers[key] = []
        dma = meth == "dma_start"
        rec = dict(eng=eng, meth=meth, kw=real, deps=deps, dma=dma, inc=False, cnt=0, sem=None, tgt=0)
        if dma:
            sbside = None
            for k in ("out", "in_"):
                ap = real[k]
                if "SB" in type(ap.tensor).__name__:
                    sbside = ap.tensor.name
            assert sbside is not None, "dma needs an SBUF side"
            s = self.dma_sem_of.setdefault(sbside, len(self.dma_sem_of))
            if s >= len(self.dma_tot):
                self.dma_tot.append(0)
            self.dma_tot[s] += 16
            rec["sem"] = s
            rec["tgt"] = self.dma_tot[s]
        self.ins.append(rec)
        return i

    def mark(self, name):
        self.marks[name] = len(self.ins)

    def barrier(self):
        last = {}
        lastdma = {}
        for j, r in enumerate(self.ins):
            if r["dma"]:
                lastdma[r["sem"]] = j
            else:
                last[r["eng"]] = j
        self._bar_ids = list(last.values()) + list(lastdma.values())
        self._bar_pending = set(self.ENGS)
        self.dma_sem_of = {}

    def setup(self, stack):
        nc = self.nc
        self.sems = {e: stack.enter_context(nc.semaphore("s_" + e)) for e in self.ENGS}
        self.dsem = [stack.enter_context(nc.semaphore("d_%d" % k)) for k in range(NDMASEM)]
        self.waited = {e: {p: -1 for p in self.ENGS} for e in self.ENGS}
        self.dma_waited = {e: set() for e in self.ENGS}
        self.cnt = {e: 0 for e in self.ENGS}
        self.done = 0

    def emit(self, final=False):
        nc = self.nc
        import os as _os
        ins = self.ins
        lo, hi = self.done, len(ins)
        stop = _os.environ.get("KSTOP")
        if stop:
            n = int(stop) if stop.isdigit() else self.marks.get(stop, hi)
            hi = max(lo, min(hi, n))
        assert len(self.dma_tot) <= NDMASEM, len(self.dma_tot)
        waited, dma_waited = self.waited, self.dma_waited
        for i in range(lo, hi):
            r = ins[i]
            e = r["eng"]
            need = {}
            dm = []
            for j in r["deps"]:
                p = ins[j]
                if p["dma"]:
                    if j not in dma_waited[e]:
                        dm.append(j)
                else:
                    pe_ = p["eng"]
                    if pe_ == e and not r["dma"]:
                        if not (self.same_engine_sync and e != "pe"):
                            continue
                    if j > waited[e][pe_]:
                        need[pe_] = max(need.get(pe_, -1), j)
            for pe_, j in list(need.items()):
                if j < lo and not ins[j]["inc"]:
                    jj = j
                    while jj < lo and not (ins[jj]["eng"] == pe_ and ins[jj]["inc"] and not ins[jj]["dma"]):
                        jj += 1
                    assert jj < lo, "no increment available for cross-block dependency"
                    need[pe_] = jj
            r["w_cmp"] = need
            r["w_dma"] = dm
            for pe_, j in need.items():
                waited[e][pe_] = max(waited[e][pe_], j)
                ins[j]["inc"] = True
            for j in dm:
                dma_waited[e].add(j)
        lastc = {}
        for i in range(lo, hi):
            if not ins[i]["dma"]:
                lastc[ins[i]["eng"]] = i
        for i in lastc.values():
            ins[i]["inc"] = True
        for i in range(lo, hi):
            r = ins[i]
            if r["inc"] and not r["dma"]:
                self.cnt[r["eng"]] += 1
                r["cnt"] = self.cnt[r["eng"]]
        per = {e: [] for e in self.ENGS}
        for i in range(lo, hi):
            per[ins[i]["eng"]].append(ins[i])
        sems, dsem = self.sems, self.dsem
        dma_tot = [0] * len(self.dma_tot)
        for r in ins[:hi]:
            if r["dma"]:
                dma_tot[r["sem"]] = max(dma_tot[r["sem"]], r["tgt"])
        self.done = len(ins)

        def run(eng_obj, lst, fin=False):
            for r in lst:
                for pe_, j in r["w_cmp"].items():
                    eng_obj.wait_ge(sems[pe_], ins[j]["cnt"])
                for j in r["w_dma"]:
                    eng_obj.wait_ge(dsem[ins[j]["sem"]], ins[j]["tgt"])
                inst = getattr(eng_obj, r["meth"])(**r["kw"])
                if r["dma"]:
                    inst.then_inc(dsem[r["sem"]], 16)
                elif r["inc"]:
                    inst.then_inc(sems[r["eng"]], 1)
            if fin:
                for k in range(len(dma_tot)):
                    if dma_tot[k]:
                        eng_obj.wait_ge(dsem[k], dma_tot[k])

        with nc.Block() as block:
            @block.tensor
            def _(eng):
                run(eng, per["pe"])

            @block.scalar
            def _(eng):
                run(eng, per["act"])

            @block.vector
            def _(eng):
                run(eng, per["dve"])

            @block.gpsimd
            def _(eng):
                run(eng, per["pool"])

            @block.sync
            def _(eng):
                run(eng, per["sp"], fin=final)


def make_consts():
    c = {}
    c["ident_f"] = np.eye(128, dtype=np.float32)
    c["ident_b"] = np.eye(128, dtype=np.float32).astype(ml_dtypes.bfloat16)
    c["ones_f"] = np.ones((128, 128), np.float32)
    for ch in (32, 64):
        s = np.arange(ch)[:, None]
        t = np.arange(MT)[None, :] % ch
        c["hg_maskP%d" % ch] = (s <= t).astype(np.uint32)
        rm = np.ones((128, MT), np.float32)
        rm[:, ::ch] = 0.0
        c["resetmask%d" % ch] = rm
    k = np.arange(128)[:, None]
    q = np.arange(MT)[None, :] % 128
    c["sw_mask_cur"] = (k <= q).astype(np.float32).astype(ml_dtypes.bfloat16)
    c["sw_mask_prev"] = (k > q).astype(np.float32).astype(ml_dtypes.bfloat16)
    rot = np.zeros((128, 128), np.float32)
    for blk in (0, 64):
        for m in range(32):
            rot[blk + m + 32, blk + m] = -1.0
            rot[blk + m, blk + m + 32] = 1.0
    c["sw_rotT"] = rot
    ob = np.zeros((128, 128), np.float32)
    ob[0:64, 0:64] = 1.0 / 64
    ob[64:128, 64:128] = 1.0 / 64
    c["sw_ones_blk"] = ob
    inv = (10000.0 ** (-np.arange(0, 64, 2, dtype=np.float32) / 64)).astype(np.float32)
    c["sw_invfreq"] = np.tile(inv, 4).reshape(128, 1).astype(np.float32)
    jj = np.arange(64)[:, None]
    ii = np.arange(256)[None, :] % 64
    c["gd_maskS"] = (jj < ii).astype(np.float32)
    c["gd_maskI"] = (jj <= ii).astype(np.float32)
    rm = np.ones((16, 256), np.float32)
    rm[:, ::64] = 0.0
    c["gd_rm"] = rm
    sel = np.zeros((16, 16, 128), np.float32)
    for h in range(16):
        sel[h, h, :] = 1.0
    c["gd_sel"] = sel.reshape(16, 16 * 128)
    return c


CONST_SPECS = {
    "gd_maskS": ([64, 256], F32),
    "gd_maskI": ([64, 256], F32),
    "gd_rm": ([16, 256], F32),
    "gd_sel": ([16, 2048], F32),
    "sw_mask_cur": ([128, MT], BF16),
    "sw_mask_prev": ([128, MT], BF16),
    "sw_rotT": ([128, 128], F32),
    "sw_ones_blk": ([128, 128], F32),
    "sw_invfreq": ([128, 1], F32),
    "ident_f": ([128, 128], F32),
    "ident_b": ([128, 128], BF16),
    "ones_f": ([128, 128], F32),
    "hg_maskP32": ([32, MT], U32),
    "hg_maskP64": ([64, MT], U32),
    "resetmask32": ([128, MT], F32),
    "resetmask64": ([128, MT], F32),
}


COMMON_CONSTS = ("ident_f", "ident_b", "ones_f")
HG_CONSTS = ("hg_maskP32", "hg_maskP64", "resetmask32", "resetmask64")
GD_CONSTS = ("gd_maskS", "gd_maskI", "gd_sel", "gd_rm")
SW_CONSTS = ("sw_mask_cur", "sw_mask_prev", "sw_rotT", "sw_ones_blk", "sw_invfreq")


class Ctx:
    pass


def load_consts(g, names, pfx=""):
    for k in names:
        shape, dt = CONST_SPECS[k]
        g.c[k] = g.sb(pfx + "c_" + k, shape, dt)
        g.P.op("sp", "dma_start", out=g.c[k][:], in_=g.dr[k][:, :])


def build(T, layers, n_hg=2):
    assert T % MT == 0
    nc = bass.Bass("TRN2", target_bir_lowering=False)
    P = Prog(nc)
    g = Ctx()
    g.nc, g.P, g.T = nc, P, T
    dr = {}

    def din(name, shape, dt=F32):
        dr[name] = nc.dram_tensor(name, shape, dt, kind="ExternalInput").ap()
        return dr[name]

    din("x", [T, D])
    din("c8", [8, 128])
    din("positions", [1, T], I32)
    din("hgrn_lb", [32, 128])
    din("ada_w", [4, D, 3 * D])
    din("ada_b", [4, 24, 128])
    din("ada_b_row", [4, 1, 3 * D])
    din("norm_g", [4, 8, 128])
    din("hg_in_w", [n_hg, D, 4 * D])
    din("hg_out_w", [n_hg, D, D])
    din("hg_onorm", [n_hg, 1, 128])
    din("sw_in_w", [1, D, 2560])
    din("sw_out_w", [1, D, D])
    din("sw_qnorm", [1, 64])
    din("sw_knorm", [1, 64])
    din("sw_sinks", [1, 16])
    din("gd_in_w", [1, D, 6176])
    din("gd_out_w", [1, 2 * D, D])
    din("gd_conv_w", [4, 4096])
    din("gd_a_log", [16, 1])
    din("gd_dt_bias", [16, 1])
    din("gd_onorm", [1, 128])
    for k, (shape, dt) in CONST_SPECS.items():
        din(k, shape, dt)
    out = nc.dram_tensor("out", [T, D], F32, kind="ExternalOutput").ap()
    scr = [nc.dram_tensor("xs%d" % i, [T, D], F32, kind="Internal").ap() for i in range(2)]
    g.gd_wscr = nc.dram_tensor("gd_wscr", [32, 128, 1024], BF16, kind="Internal").ap()
    g.dr = dr

    with ExitStack() as st:
        def sbg(name, shape, dt=F32):
            return st.enter_context(nc.sbuf_tensor(name, shape, dt))

        P.setup(st)
        g.sb = sbg
        g.psum = [st.enter_context(nc.psum_tensor("ps%d" % i, [128, 512], F32)) for i in range(8)]
        g.c = {}
        load_consts(g, COMMON_CONSTS)
        g.stage = [sbg("stage%d" % i, [128, SW], F32) for i in range(2)]
        P.mark("prep_common")
        prep_common(g)
        P.mark("after_prep_common")
        src = dr["x"]
        srctag = lambda ap, tag: ap
        for li, layer in enumerate(layers):
            dst = out if li == len(layers) - 1 else scr[li % 2]
            kind = layer % 3
            with ExitStack() as lst:
                g.sb = lambda name, shape, dt=F32: lst.enter_context(nc.sbuf_tensor(name, shape, dt))
                if kind == 0:
                    load_consts(g, HG_CONSTS, "L%d" % layer)
                    hgrn2_layer(g, layer, src, srctag, dst)
                elif kind == 1:
                    load_consts(g, SW_CONSTS, "L%d" % layer)
                    swa_layer(g, layer, src, srctag, dst)
                else:
                    load_consts(g, GD_CONSTS, "L%d" % layer)
                    gdn_layer(g, layer, src, srctag, dst)
                P.barrier()
                P.emit(final=(li == len(layers) - 1))
            src = dst
            srctag = Sub
    return nc


def transpose_small(g, dst, src_rows, nrows, ps):
    P = g.P
    P.op("pe", "transpose", out=ps[:, 0:nrows], in_=src_rows, identity=g.c["ident_f"][0:nrows, 0:nrows])
    P.op("dve", "tensor_copy", out=dst, in_=ps[:, 0:nrows])


def W(ap, k):
    return Sub(ap, ("w", k))


def prep_common(g):
    P, sb, dr = g.P, g.sb, g.dr
    ps = g.psum[0]
    g.rows_a = sb("rows_a", [32, 128], F32)
    g.rows_g = sb("rows_g", [8, 128], F32)
    rows_c = sb("rows_c", [8, 128], F32)
    g.c_col = sb("c_col", [128, 8], F32)
    P.op("sp", "dma_start", out=rows_c[:], in_=dr["c8"][:, :])
    transpose_small(g, g.c_col[:], rows_c[:], 8, ps)
    P.op("sp", "dma_start", out=g.rows_a[:], in_=dr["hgrn_lb"][:, :])
    lbT = sb("lbT", [128, 32], F32)
    transpose_small(g, lbT[:], g.rows_a[:], 32, g.psum[1])
    ex = sb("lb_exp", [128, 32], F32)
    P.op("act", "activation", out=ex[:], in_=lbT[:], func=AF.Exp)
    den = sb("lb_den", [128, 8], F32)
    P.op("dve", "tensor_tensor", out=den[:], in0=ex[:, 0:8], in1=ex[:, 8:16], op=ALU.add)
    P.op("dve", "tensor_tensor", out=den[:], in0=den[:], in1=ex[:, 16:24], op=ALU.add)
    P.op("dve", "tensor_tensor", out=den[:], in0=den[:], in1=ex[:, 24:32], op=ALU.add)
    rden = sb("lb_rden", [128, 8], F32)
    P.op("dve", "reciprocal", out=rden[:], in_=den[:])
    g.lb = {}
    lb0 = sb("lb_l0", [128, 8], F32)
    P.op("dve", "memset", ap=lb0[:], constant=0.0)
    g.lb[0] = lb0
    num = sb("lb_num", [128, 8], F32)
    P.op("dve", "tensor_tensor", out=num[:], in0=ex[:, 8:16], in1=ex[:, 16:24], op=ALU.add)
    P.op("dve", "tensor_tensor", out=num[:], in0=num[:], in1=ex[:, 24:32], op=ALU.add)
    lb3 = sb("lb_l3", [128, 8], F32)
    P.op("dve", "tensor_tensor", out=lb3[:], in0=num[:], in1=rden[:], op=ALU.mult)
    g.lb[3] = lb3
    g.modT = sb("modT", [128, 16], F32)
    g.gsT = sb("gsT", [128, 8], F32)
    g.gate_row = sb("gate_row", [1, D], F32)
    g.gate_b = sb("gate_b", [128, D], F32)
    g.adab_T = sb("adab_T", [128, 24], F32)
    g.ng_T = sb("ng_T", [128, 8], F32)


def prep_layer(g, layer):
    P, dr = g.P, g.dr
    ps = g.psum[0]
    P.op("sp", "dma_start", out=g.rows_a[0:24, :], in_=dr["ada_b"][layer])
    transpose_small(g, g.adab_T[:], g.rows_a[0:24, :], 24, g.psum[1])
    P.op("sp", "dma_start", out=g.rows_g[:], in_=dr["norm_g"][layer])
    transpose_small(g, g.ng_T[:], g.rows_g[:], 8, g.psum[2])
    aw = dr["ada_w"][layer].rearrange("(k p) n -> p k n", p=128)
    for j in range(16):
        t = g.stage[j % 2][:, 0:1024].rearrange("p (k n) -> p k n", k=KT)
        P.op("sp", "dma_start", out=t, in_=aw[:, :, j * 128:(j + 1) * 128])
        for k in range(KT):
            P.op("pe", "matmul", out=ps[:, j:j + 1], lhsT=t[:, k, :], rhs=g.c_col[:, k:k + 1],
                 start=(k == 0), stop=(k == KT - 1))
    P.op("dve", "tensor_tensor", out=g.modT[:], in0=ps[:, 0:16], in1=g.adab_T[:, 0:16], op=ALU.add)
    P.op("dve", "scalar_tensor_tensor", out=g.gsT[:], in0=g.modT[:, 8:16], scalar=1.0, in1=g.ng_T[:],
         op0=ALU.add, op1=ALU.mult)
    P.op("sp", "dma_start", out=g.gate_row[:], in_=dr["ada_b_row"][layer][:, 2 * D:3 * D])
    for cg in range(8):
        psg = g.psum[3 + cg % 2]
        t = g.stage[cg % 2][:, 0:1024].rearrange("p (k n) -> p k n", k=KT)
        P.op("sp", "dma_start", out=t, in_=aw[:, :, 2 * D + cg * 128:2 * D + (cg + 1) * 128])
        for k in range(KT):
            P.op("pe", "matmul", out=psg[0:1, 0:128], lhsT=g.c_col[:, k:k + 1], rhs=t[:, k, :],
                 start=(k == 0), stop=(k == KT - 1))
        P.op("dve", "tensor_tensor", out=g.gate_row[0:1, cg * 128:(cg + 1) * 128], in0=psg[0:1, 0:128],
             in1=g.gate_row[0:1, cg * 128:(cg + 1) * 128], op=ALU.add)
    for cg in range(2):
        psg = g.psum[5 + cg]
        P.op("pe", "matmul", out=psg[:], lhsT=g.c["ones_f"][0:1, :], rhs=g.gate_row[0:1, cg * 512:(cg + 1) * 512],
             start=True, stop=True)
        P.op("dve", "tensor_copy", out=g.gate_b[:, cg * 512:(cg + 1) * 512], in_=psg[:])


def load_weight_bf16(g, dst3, wsrc, ncols, colscale=None, kbase=0):
    P = g.P
    nk = wsrc.shape[0] // 128
    i = 0
    for k in range(nk):
        for c0 in range(0, ncols, SW):
            w = min(SW, ncols - c0)
            stg = g.stage[i % 2]
            P.op("sp", "dma_start", out=stg[:, 0:w], in_=wsrc[k * 128:(k + 1) * 128, c0:c0 + w])
            eng = ("dve", "pool")[i % 2]
            dstv = W(dst3[:, k, c0:c0 + w], kbase + k)
            if colscale is None:
                P.op(eng, "tensor_copy", out=dstv, in_=stg[:, 0:w])
            else:
                P.op(eng, "tensor_tensor", out=dstv, in0=stg[:, 0:w], in1=colscale[:, c0:c0 + w], op=ALU.mult)
            i += 1


def prologue(g, mt, src, srctag, hT, xb, xnb, ss, rstd, mtl=MT):
    P = g.P
    P.op("pool", "memset", ap=ss[:], constant=0.0)
    for blk in range(mtl // 128):
        t0 = mt * mtl + blk * 128
        x = xb[blk % len(xb)]
        xn = xnb[blk % len(xnb)]
        P.op("sp", "dma_start", out=x[:], in_=srctag(src[t0:t0 + 128, :], t0 // 128))
        P.op("act", "activation", out=xn[:], in_=x[:], func=AF.Square, accum_out=ss[:, blk:blk + 1])
        P.op("dve", "tensor_scalar", out=rstd[:, blk:blk + 1], in0=ss[:, blk:blk + 1], scalar1=1.0 / D, scalar2=EPS,
             op0=ALU.mult, op1=ALU.add)
        P.op("act", "activation", out=rstd[:, blk:blk + 1], in_=rstd[:, blk:blk + 1], func=AF.Sqrt)
        P.op("dve", "reciprocal", out=rstd[:, blk:blk + 1], in_=rstd[:, blk:blk + 1])
        P.op("dve", "tensor_scalar", out=xn[:], in0=x[:], scalar1=rstd[:, blk:blk + 1], scalar2=None, op0=ALU.mult)
        for kg in range(2):
            ps = g.psum[kg]
            psb = ps[:].bitcast(BF16)
            for kk in range(4):
                k = kg * 4 + kk
                P.op("pe", "transpose", out=psb[:, kk * 128:(kk + 1) * 128], in_=xn[:, k * 128:(k + 1) * 128],
                     identity=g.c["ident_b"][:])
            for kk in range(4):
                k = kg * 4 + kk
                dstv = Sub(hT[:, k, blk * 128:(blk + 1) * 128], k)
                srcv = psb[:, kk * 128:(kk + 1) * 128]
                if k % 2 == 0:
                    P.op("dve", "tensor_scalar", out=dstv, in0=srcv, scalar1=g.gsT[:, k:k + 1],
                         scalar2=g.modT[:, k:k + 1], op0=ALU.mult, op1=ALU.add)
                else:
                    P.op("act", "activation", out=dstv, in_=srcv, func=AF.Identity, scale=g.gsT[:, k:k + 1],
                         bias=g.modT[:, k:k + 1])


def epilogue(g, mt, src, srctag, dst, og, nk, w_out, xb, psa, psb_, ogtag=True, mtl=MT):
    P = g.P
    for blk in range(mtl // 128):
        t0 = mt * mtl + blk * 128
        x = xb[blk % len(xb)]
        P.op("sp", "dma_start", out=x[:], in_=srctag(src[t0:t0 + 128, :], t0 // 128))
        for cg in range(2):
            pst = (psa, psb_)[cg]
            for k in range(nk):
                ogv = og[:, k, blk * 128:(blk + 1) * 128]
                P.op("pe", "matmul", out=pst[:], lhsT=Sub(ogv, k) if ogtag else ogv,
                     rhs=W(w_out[:, k, cg * 512:(cg + 1) * 512], 100 + k), start=(k == 0), stop=(k == nk - 1))
            P.op("dve", "tensor_tensor", out=x[:, cg * 512:(cg + 1) * 512], in0=pst[:], in1=x[:, cg * 512:(cg + 1) * 512],
                 op=ALU.add)
        P.op("sp", "dma_start", out=Sub(dst[t0:t0 + 128, :], t0 // 128), in_=x[:])


def hgrn2_layer(g, layer, src, srctag, dst):
    P, sb, dr, nc = g.P, g.sb, g.dr, g.nc
    j = layer // 3
    T = g.T
    CH = 32 if layer == 0 else 64
    NCH = MT // CH
    P.mark("prep_layer")
    prep_layer(g, layer)
    P.mark("after_prep_layer")
    L = "L%d_" % layer
    g.warena = sb(L + "warena", [128, KT * 5120], BF16)
    w_in = g.warena[:, 0:KT * 4096].rearrange("p (k n) -> p k n", k=KT)
    w_out = g.warena[:, KT * 4096:KT * 4096 + KT * 1024].rearrange("p (k n) -> p k n", k=KT)
    load_weight_bf16(g, w_in, dr["hg_in_w"][j], 4096)
    load_weight_bf16(g, w_out, dr["hg_out_w"][j], 1024, colscale=g.gate_b, kbase=100)
    lb = g.lb[layer]
    oml = sb(L + "oml", [128, 8])
    noml = sb(L + "noml", [128, 8])
    P.op("dve", "tensor_scalar", out=oml[:], in0=lb[:], scalar1=-1.0, scalar2=1.0, op0=ALU.mult, op1=ALU.add)
    P.op("dve", "tensor_scalar", out=noml[:], in0=oml[:], scalar1=-1.0, scalar2=None, op0=ALU.mult)
    onr = sb(L + "onr", [1, 128])
    ong = sb(L + "ong", [128, 1])
    P.op("sp", "dma_start", out=onr[:], in_=dr["hg_onorm"][j])
    transpose_small(g, ong[:], onr[0:1, :], 1, g.psum[7])
    ones_m = sb(L + "ones_m", [128, 128])
    epsc = sb(L + "epsc", [128, 1])
    P.op("dve", "memset", ap=epsc[:], constant=EPS)
    P.op("dve", "tensor_scalar", out=ones_m[:], in0=g.c["ones_f"][:], scalar1=1.0 / 128, scalar2=None, op0=ALU.mult)

    xb = [sb(L + "xb%d" % i, [128, D]) for i in range(2)]
    xnb = [sb(L + "xn%d" % i, [128, D], BF16) for i in range(2)]
    ss = sb(L + "ss", [128, 4])
    rstd = sb(L + "rstd", [128, 4])
    hT = sb(L + "hT", [128, KT, MT], BF16)
    t_q = sb(L + "t_q", [128, MT])
    t_f = sb(L + "t_f", [128, MT])
    t_k = sb(L + "t_k", [128, MT])
    t_l = sb(L + "t_l", [128, MT])
    t_b = sb(L + "t_b", [128, MT])
    t_bm = sb(L + "t_bm", [128, MT])
    t_e1 = sb(L + "t_e1", [128, MT])
    t_e2 = sb(L + "t_e2", [128, MT])
    qT = [sb(L + "qT%d" % i, [128, MT], BF16) for i in range(2)]
    kT = [sb(L + "kT%d" % i, [128, MT], BF16) for i in range(2)]
    ktok = [sb(L + "ktok%d" % i, [CH, NCH, 128], BF16) for i in range(2)]
    vtok = [sb(L + "vtok%d" % i, [CH, NCH, 128], BF16) for i in range(2)]
    vT = sb(L + "vT", [128, MT], BF16)
    zs = [sb(L + "zs%d" % i, [128, MT], BF16) for i in range(2)]
    er = sb(L + "er", [128, NCH])
    ebl = sb(L + "ebl", [128, NCH])
    eblr = sb(L + "eblr", [128, NCH])
    atm = [sb(L + "atm%d" % i, [CH, MT], BF16) for i in range(2)]
    for i in range(2):
        P.op("pool", "memset", ap=atm[i][:], constant=0.0)
    S = [sb(L + "S%d" % h, [128, 128]) for h in range(8)]
    srb = [sb(L + "srb%d" % i, [128, 128], BF16) for i in range(2)]
    stmp = [sb(L + "stmp%d" % i, [128, 128]) for i in range(2)]
    osq = sb(L + "osq", [128, MT])
    orstd = sb(L + "orstd", [128, MT])
    otmp = sb(L + "otmp", [128, MT])
    og = sb(L + "og", [128, 8, MT], BF16)
    for h in range(8):
        P.op("pool", "memset", ap=S[h][:], constant=0.0)

    ps = g.psum
    PS_Q, PS_F, PS_Z, PS_V, PS_A, PS_O, PS_S, PS_X = range(8)
    PS_O2, PS_S2 = PS_Q, PS_F
    khT = [sb(L + "khT%d" % i, [128, MT], BF16) for i in range(2)]
    er2 = [er, sb(L + "er_b", [128, NCH])]
    ebl2 = [ebl, sb(L + "ebl_b", [128, NCH])]
    srb2 = [srb, [sb(L + "srbB%d" % i, [128, 128], BF16) for i in range(2)]]
    P.mark("after_wload")
    RI = CH // 2 - 1
    for mt in range(T // MT):
        prologue(g, mt, src, srctag, hT, xb, xnb, ss, rstd)
        P.mark("after_prologue%d" % mt)
        for hp in range(4):
            for pp in range(2):
                h = 2 * hp + pp
                P.mark("head%d_%d" % (mt, h))
                for (pst, c0) in ((PS_Q, h * 128), (PS_F, 1024 + h * 128), (PS_V, 2048 + h * 128), (PS_Z, 3072 + h * 128)):
                    for k in range(KT):
                        P.op("pe", "matmul", out=ps[pst][:], lhsT=W(w_in[:, k, c0:c0 + 128], k), rhs=Sub(hT[:, k, :], k),
                             start=(k == 0), stop=(k == KT - 1))
                P.op("act", "activation", out=t_q[:], in_=ps[PS_Q][:], func=AF.Silu)
                P.op("act", "activation", out=t_f[:], in_=ps[PS_F][:], func=AF.Sigmoid)
                P.op("act", "activation", out=zs[pp][:], in_=ps[PS_Z][:], func=AF.Silu)
                P.op("act", "activation", out=vT[:], in_=ps[PS_V][:], func=AF.Copy)
                P.op("dve", "tensor_scalar", out=t_k[:], in0=t_f[:], scalar1=noml[:, h:h + 1], scalar2=oml[:, h:h + 1],
                     op0=ALU.mult, op1=ALU.add)
                P.op("act", "activation", out=t_l[:], in_=t_f[:], func=AF.Ln, scale=oml[:, h:h + 1], bias=lb[:, h:h + 1])
                P.op("dve", "tensor_tensor_scan", out=t_b[:], data0=g.c["resetmask%d" % CH][:], data1=t_l[:], initial=0.0,
                     op0=ALU.mult, op1=ALU.add)
                b3 = t_b[:].rearrange("p (c j) -> p c j", j=CH)
                bm3 = t_bm[:].rearrange("p (c j) -> p c j", j=CH)
                P.op("dve", "tensor_tensor", out=bm3, in0=b3, in1=b3[:, :, RI:RI + 1].to_broadcast([128, NCH, CH]),
                     op=ALU.subtract)
                P.op("act", "activation", out=t_e1[:], in_=t_bm[:], func=AF.Exp)
                P.op("act", "activation", out=t_e2[:], in_=t_bm[:], func=AF.Exp, scale=-1.0)
                P.op("act", "activation", out=er2[pp][:], in_=b3[:, :, RI], func=AF.Exp)
                P.op("act", "activation", out=ebl2[pp][:], in_=b3[:, :, CH - 1], func=AF.Exp)
                P.op("pool", "tensor_tensor", out=qT[pp][:], in0=t_q[:], in1=t_e1[:], op=ALU.mult)
                P.op("dve", "tensor_tensor", out=kT[pp][:], in0=t_k[:], in1=t_e2[:], op=ALU.mult)
                P.op("dve", "tensor_tensor", out=bm3, in0=b3[:, :, CH - 1:CH].to_broadcast([128, NCH, CH]), in1=b3,
                     op=ALU.subtract)
                P.op("act", "activation", out=t_e1[:], in_=t_bm[:], func=AF.Exp)
                P.op("dve", "tensor_tensor", out=khT[pp][:], in0=t_k[:], in1=t_e1[:], op=ALU.mult)
                psb = ps[PS_X][:].bitcast(BF16)
                for (srcT, dstk) in ((khT[pp], ktok[pp]), (vT, vtok[pp])):
                    for c0 in range(0, NCH, 8):
                        for c in range(c0, c0 + 8):
                            P.op("pe", "transpose", out=psb[0:CH, (c - c0) * 128:(c - c0 + 1) * 128],
                                 in_=srcT[:, c * CH:(c + 1) * CH], identity=g.c["ident_b"][:])
                        P.op("act", "activation", out=dstk[:, c0:c0 + 8, :].rearrange("p a b -> p (a b)"), in_=psb[0:CH, :],
                             func=AF.Copy)
                a = atm[pp]
                for c in range(NCH):
                    tcs = slice(c * CH, (c + 1) * CH)
                    P.op("pe", "matmul", out=ps[PS_A][0:CH, tcs], lhsT=kT[pp][:, tcs], rhs=qT[pp][:, tcs], start=True, stop=True)
                P.op("dve", "copy_predicated", out=a[:], mask=g.c["hg_maskP%d" % CH][:], data=ps[PS_A][0:CH, :])
            P.mark("scan%d_%d" % (mt, hp))
            for c in range(NCH):
                tcs = slice(c * CH, (c + 1) * CH)
                for pp in range(2):
                    h = 2 * hp + pp
                    pso = ps[(PS_O, PS_O2)[pp]]
                    pss = ps[(PS_S, PS_S2)[pp]]
                    sr = srb2[pp][c % 2]
                    P.op("act", "activation", out=sr[:], in_=S[h][:], func=AF.Copy, scale=er2[pp][:, c:c + 1])
                    P.op("pe", "matmul", out=pso[:, tcs], lhsT=sr[:], rhs=qT[pp][:, tcs], start=True, stop=False)
                    P.op("pe", "matmul", out=pso[:, tcs], lhsT=vtok[pp][:, c, :], rhs=atm[pp][:, tcs], start=False, stop=True)
                    P.op("pe", "matmul", out=pss[:, 0:128], lhsT=ktok[pp][:, c, :], rhs=vtok[pp][:, c, :],
                         start=True, stop=True)
                    P.op("dve", "scalar_tensor_tensor", out=S[h][:], in0=S[h][:], scalar=ebl2[pp][:, c:c + 1],
                         in1=pss[:, 0:128], op0=ALU.mult, op1=ALU.add)
            for pp in range(2):
                h = 2 * hp + pp
                pso = ps[(PS_O, PS_O2)[pp]]
                P.op("act", "activation", out=osq[:], in_=pso[:], func=AF.Square)
                P.op("pe", "matmul", out=ps[PS_A][:], lhsT=ones_m[:], rhs=osq[:], start=True, stop=True)
                P.op("act", "activation", out=orstd[:], in_=ps[PS_A][:], func=AF.Sqrt, bias=epsc[:, 0:1])
                P.op("dve", "reciprocal", out=orstd[:], in_=orstd[:])
                P.op("dve", "scalar_tensor_tensor", out=otmp[:], in0=pso[:], scalar=ong[:, 0:1], in1=orstd[:],
                     op0=ALU.mult, op1=ALU.mult)
                P.op("pool", "tensor_tensor", out=Sub(og[:, h, :], h), in0=otmp[:], in1=zs[pp][:], op=ALU.mult)
        P.mark("epilogue%d" % mt)
        epilogue(g, mt, src, srctag, dst, og, KT, w_out, xb, ps[PS_V], ps[PS_Z])


def swa_layer(g, layer, src, srctag, dst):
    import math
    P, sb, dr, nc = g.P, g.sb, g.dr, g.nc
    T = g.T
    P.mark("prep_layer")
    prep_layer(g, layer)
    L = "L%d_" % layer
    NB = MT // 128
    g.warena = sb(L + "warena", [128, KT * 4096], BF16)
    w_in = g.warena[:, 0:KT * 3072].rearrange("p (k n) -> p k n", k=KT)
    w_out = g.warena[:, KT * 3072:KT * 3072 + KT * 1024].rearrange("p (k n) -> p k n", k=KT)
    WQ, WK, WV, WZ = 0, 1024, 1536, 2048
    win = dr["sw_in_w"][0]
    i = 0
    for k in range(KT):
        for (c0, w, dup, d0) in ((0, 1024, False, WQ), (1024, 256, True, WK), (1280, 256, True, WV), (1536, 1024, False, WZ)):
            stg = g.stage[i % 2]
            eng = ("dve", "pool")[i % 2]
            i += 1
            P.op("sp", "dma_start", out=stg[:, 0:w], in_=win[k * 128:(k + 1) * 128, c0:c0 + w])
            if not dup:
                P.op(eng, "tensor_copy", out=W(w_in[:, k, d0:d0 + w], k), in_=stg[:, 0:w])
            else:
                dv = w_in[:, k, d0:d0 + 512].rearrange("p (h r d) -> p h r d", h=4, r=2)
                sv = stg[:, 0:256].rearrange("p (h d) -> p h d", h=4)
                for r in range(2):
                    P.op(eng, "tensor_copy", out=W(dv[:, :, r, :], k), in_=sv)
    load_weight_bf16(g, w_out, dr["sw_out_w"][0], 1024, colscale=g.gate_b, kbase=100)

    gq = sb(L + "gq", [128, 1])
    gk = sb(L + "gk", [128, 1])
    for (dst_, nm) in ((gq, "sw_qnorm"), (gk, "sw_knorm")):
        r1 = sb(L + nm + "_r", [1, 128])
        P.op("sp", "dma_start", out=r1[0:1, 0:64], in_=dr[nm][0:1, :])
        P.op("sp", "dma_start", out=r1[0:1, 64:128], in_=dr[nm][0:1, :])
        transpose_small(g, dst_[:], r1[0:1, :], 1, g.psum[7])
    epsc = sb(L + "epsc", [128, 1])
    P.op("dve", "memset", ap=epsc[:], constant=EPS)
    negpi = sb(L + "negpi", [128, 1])
    P.op("dve", "memset", ap=negpi[:], constant=-math.pi)
    cpi = sb(L + "cpi", [128, 3])
    P.op("dve", "memset", ap=cpi[:, 0:1], constant=math.pi)
    P.op("dve", "memset", ap=cpi[:, 1:2], constant=1.5 * math.pi)
    P.op("dve", "memset", ap=cpi[:, 2:3], constant=2 * math.pi)
    sk_r = sb(L + "sk_r", [1, 16])
    P.op("sp", "dma_start", out=sk_r[:], in_=dr["sw_sinks"][0:1, :])
    P.op("act", "activation", out=sk_r[:], in_=sk_r[:], func=AF.Exp)
    P.op("pe", "matmul", out=g.psum[6][:, 0:16], lhsT=g.c["ones_f"][0:1, :], rhs=sk_r[0:1, :], start=True, stop=True)
    esb = sb(L + "esb", [128, 16])
    P.op("dve", "tensor_copy", out=esb[:], in_=g.psum[6][:, 0:16])
    esb4 = esb[:].rearrange("p (h j s) -> p h j s", h=4, j=2)
    esb2 = sb(L + "esb2", [128, 4, 2])
    P.op("dve", "tensor_copy", out=esb2[0:64, :, :], in_=esb4[0:64, :, :, 0])
    P.op("dve", "tensor_copy", out=esb2[64:128, :, :], in_=esb4[64:128, :, :, 1])
    esf = sb(L + "esf", [128, 4, MT])
    for h in range(4):
        for sc in range(2):
            P.op("dve", "tensor_copy", out=esf[:, h, sc * 256:(sc + 1) * 256].rearrange("p (j q) -> p j q", j=2),
                 in_=esb2[:, h, :].unsqueeze(2).to_broadcast([128, 2, 128]))
    ones_b = sb(L + "ones_b", [128, 128], BF16)
    P.op("dve", "tensor_copy", out=ones_b[:], in_=g.c["ones_f"][:])

    xb = [sb(L + "xb%d" % i, [128, D]) for i in range(2)]
    xnb = [sb(L + "xn%d" % i, [128, D], BF16) for i in range(2)]
    ss = sb(L + "ss", [128, 4])
    rstd = sb(L + "rstd", [128, 4])
    hT = sb(L + "hT", [128, KT, MT], BF16)
    cos2 = sb(L + "cos2", [128, MT])
    sin2 = sb(L + "sin2", [128, MT])
    t_sq = sb(L + "t_sq", [128, MT])
    t_rs = sb(L + "t_rs", [128, MT])
    t_qn = sb(L + "t_qn", [128, MT])
    t_a = sb(L + "t_a", [128, MT])
    t_b = sb(L + "t_b", [128, MT])
    posi, ang, ua = t_a[:].bitcast(I32), t_b, t_sq
    qg = sb(L + "qg", [128, 8, MT], BF16)
    kg = sb(L + "kg", [128, 4, (NB + 1) * 128], BF16)
    vd = sb(L + "vd", [128, NB + 1, 512], BF16)
    zs = sb(L + "zs", [128, 8, MT], BF16)
    pT = [sb(L + "pT%d" % i, [128, MT], BF16) for i in range(2)]
    dtmp, otmp = t_rs, t_qn
    og = sb(L + "og", [128, 8, MT], BF16)
    ps = g.psum
    PS_A, PS_B, PS_N, PS_R, PS_S0, PS_S1, PS_O, PS_D = range(8)

    def normrope(pst, gain, out_ap):
        P.op("act", "activation", out=t_sq[:], in_=pst[:], func=AF.Square)
        P.op("pe", "matmul", out=ps[PS_N][:], lhsT=g.c["sw_ones_blk"][:], rhs=t_sq[:], start=True, stop=True)
        P.op("act", "activation", out=t_rs[:], in_=ps[PS_N][:], func=AF.Sqrt, bias=epsc[:, 0:1])
        P.op("dve", "reciprocal", out=t_rs[:], in_=t_rs[:])
        P.op("dve", "scalar_tensor_tensor", out=t_qn[:], in0=pst[:], scalar=gain[:, 0:1], in1=t_rs[:],
             op0=ALU.mult, op1=ALU.mult)
        P.op("pe", "matmul", out=ps[PS_R][:], lhsT=g.c["sw_rotT"][:], rhs=t_qn[:], start=True, stop=True)
        P.op("pool", "tensor_tensor", out=t_a[:], in0=t_qn[:], in1=cos2[:], op=ALU.mult)
        P.op("dve", "tensor_tensor", out=t_b[:], in0=ps[PS_R][:], in1=sin2[:], op=ALU.mult)
        P.op("pool", "tensor_tensor", out=out_ap, in0=t_a[:], in1=t_b[:], op=ALU.add)

    for mt in range(T // MT):
        prologue(g, mt, src, srctag, hT, xb, xnb, ss, rstd)
        P.mark("sw_rope%d" % mt)
        P.op("sp", "dma_start", out=posi, in_=dr["positions"][0:1, mt * MT:(mt + 1) * MT].partition_broadcast(128))
        P.op("dve", "tensor_copy", out=ang[:], in_=posi)
        P.op("dve", "tensor_scalar", out=ang[:], in0=ang[:], scalar1=g.c["sw_invfreq"][:, 0:1], scalar2=None, op0=ALU.mult)
        C1 = 6.28125
        C2 = 2 * math.pi - C1
        P.op("dve", "tensor_scalar", out=ua[:], in0=ang[:], scalar1=1.0 / (2 * math.pi), scalar2=None, op0=ALU.mult)
        P.op("dve", "tensor_copy", out=posi, in_=ua[:])
        P.op("dve", "tensor_copy", out=ua[:], in_=posi)
        P.op("dve", "scalar_tensor_tensor", out=ang[:], in0=ua[:], scalar=-C1, in1=ang[:], op0=ALU.mult, op1=ALU.add)
        P.op("dve", "scalar_tensor_tensor", out=ang[:], in0=ua[:], scalar=-C2, in1=ang[:], op0=ALU.mult, op1=ALU.add)
        P.op("dve", "tensor_scalar", out=ua[:], in0=ang[:], scalar1=math.pi, scalar2=None, op0=ALU.is_gt)
        P.op("dve", "scalar_tensor_tensor", out=ang[:], in0=ua[:], scalar=-2 * math.pi, in1=ang[:], op0=ALU.mult, op1=ALU.add)
        P.op("dve", "tensor_scalar", out=ang[:], in0=ang[:], scalar1=math.pi, scalar2=-math.pi, op0=ALU.min, op1=ALU.max)
        P.op("act", "activation", out=sin2[:], in_=ang[:], func=AF.Sin)
        P.op("act", "activation", out=ua[:], in_=ang[:], func=AF.Sin, scale=0.5)
        P.op("dve", "tensor_tensor", out=ua[:], in0=ua[:], in1=ua[:], op=ALU.mult)
        P.op("dve", "tensor_scalar", out=cos2[:], in0=ua[:], scalar1=-2.0, scalar2=1.0, op0=ALU.mult, op1=ALU.add)
        P.mark("sw_proj%d" % mt)
        for p in range(8):
            pst = ps[PS_A + p % 2]
            for k in range(KT):
                P.op("pe", "matmul", out=pst[:], lhsT=W(w_in[:, k, WQ + p * 128:WQ + (p + 1) * 128], k), rhs=Sub(hT[:, k, :], k),
                     start=(k == 0), stop=(k == KT - 1))
            normrope(pst, gq, Sub(qg[:, p, :], p))
        for h in range(4):
            pst = ps[PS_A + h % 2]
            for k in range(KT):
                P.op("pe", "matmul", out=pst[:], lhsT=W(w_in[:, k, WK + h * 128:WK + (h + 1) * 128], k), rhs=Sub(hT[:, k, :], k),
                     start=(k == 0), stop=(k == KT - 1))
            normrope(pst, gk, Sub(kg[:, h, 128:128 + MT], ("k", h)))
        for blk in range(NB):
            pst = ps[PS_A + blk % 2]
            for k in range(KT):
                P.op("pe", "matmul", out=pst[:], lhsT=Sub(hT[:, k, blk * 128:(blk + 1) * 128], k),
                     rhs=W(w_in[:, k, WV:WV + 512], k), start=(k == 0), stop=(k == KT - 1))
            P.op("act", "activation", out=Sub(vd[:, 1 + blk, :], 1 + blk), in_=pst[:], func=AF.Copy)
        for p in range(8):
            pst = ps[PS_A + p % 2]
            for k in range(KT):
                P.op("pe", "matmul", out=pst[:], lhsT=W(w_in[:, k, WZ + p * 128:WZ + (p + 1) * 128], k), rhs=Sub(hT[:, k, :], k),
                     start=(k == 0), stop=(k == KT - 1))
            P.op("act", "activation", out=Sub(zs[:, p, :], p), in_=pst[:], func=AF.Silu)
        P.mark("sw_attn%d" % mt)
        for qb in range(NB):
            qcs = slice(qb * 128, (qb + 1) * 128)
            first = (mt == 0 and qb == 0)
            kbs = ([] if first else [0]) + [1]
            for h in range(4):
                for kb in kbs:
                    kblk = qb + kb
                    kcs = slice(kblk * 128, (kblk + 1) * 128)
                    pt = pT[kb]
                    for sl in range(2):
                        pss = ps[(PS_S0, PS_N)[sl] + kb]
                        rs = slice(sl * 64, (sl + 1) * 64)
                        P.op("pe", "matmul", out=pss[:, 0:256],
                             lhsT=Sub(kg[rs, h, kcs], ("k", h)) if kblk > 0 else Sub(kg[rs, h, kcs], ("kh", h)),
                             rhs=qg[rs, 2 * h:2 * h + 2, qcs], start=True, stop=True)
                        P.op("act", "activation", out=pt[:, sl * 256:(sl + 1) * 256], in_=pss[:, 0:256], func=AF.Exp, scale=0.125)
                    P.op("pool", "tensor_tensor", out=pt[:], in0=pt[:], in1=g.c["sw_mask_prev" if kb == 0 else "sw_mask_cur"][:],
                         op=ALU.mult)
                for ii, kb in enumerate(kbs):
                    kblk = qb + kb
                    vsub = Sub(vd[:, kblk, h * 128:(h + 1) * 128], kblk)
                    P.op("pe", "matmul", out=ps[PS_O][:], lhsT=vsub, rhs=pT[kb][:], start=(ii == 0), stop=(ii == len(kbs) - 1))
                for ii, kb in enumerate(kbs):
                    P.op("pe", "matmul", out=ps[PS_D][:], lhsT=ones_b[:], rhs=pT[kb][:], start=(ii == 0), stop=(ii == len(kbs) - 1))
                P.op("dve", "tensor_tensor", out=dtmp[:], in0=ps[PS_D][:], in1=esf[:, h, :], op=ALU.add)
                P.op("dve", "reciprocal", out=dtmp[:], in_=dtmp[:])
                P.op("dve", "tensor_tensor", out=otmp[:], in0=ps[PS_O][:], in1=dtmp[:], op=ALU.mult)
                for sl in range(2):
                    rs = slice(sl * 64, (sl + 1) * 64)
                    P.op("pool", "tensor_tensor", out=og[rs, 2 * h:2 * h + 2, qcs],
                         in0=otmp[rs, sl * 256:(sl + 1) * 256].rearrange("p (j q) -> p j q", j=2),
                         in1=zs[rs, 2 * h:2 * h + 2, qcs], op=ALU.mult)
        for h in range(4):
            P.op("pool", "tensor_copy", out=Sub(kg[:, h, 0:128], ("kh", h)), in_=Sub(kg[:, h, NB * 128:(NB + 1) * 128], ("k", h)))
        P.op("pool", "tensor_copy", out=Sub(vd[:, 0, :], 0), in_=Sub(vd[:, NB, :], NB))
        P.mark("epilogue%d" % mt)
        epilogue(g, mt, src, srctag, dst, og, KT, w_out, xb, ps[PS_A], ps[PS_B], ogtag=False)


def gdn_layer(g, layer, src, srctag, dst):
    P, sb, dr, nc = g.P, g.sb, g.dr, g.nc
    T = g.T
    GT = 256
    NC = GT // 64
    P.mark("prep_layer")
    prep_layer(g, layer)
    L = "L%d_" % layer
    NRES = 2080
    g.warena = sb(L + "warena", [128, KT * NRES + 16 * 1024], BF16)
    w_in = g.warena[:, 0:KT * NRES].rearrange("p (k n) -> p k n", k=KT)
    w_out = g.warena[:, KT * NRES:KT * NRES + 16 * 1024].rearrange("p (k n) -> p k n", k=16)
    gin = dr["gd_in_w"][0]
    load_weight_bf16(g, w_in[:, :, 0:2048], gin[:, 0:2048], 2048)
    load_weight_bf16(g, w_in[:, :, 2048:2080], gin[:, 6144:6176], 32)
    load_weight_bf16(g, w_out, dr["gd_out_w"][0], 1024, colscale=g.gate_b, kbase=100)
    wscr = g.gd_wscr
    cbf = [sb(L + "cbf%d" % i, [128, SW], BF16) for i in range(2)]
    i = 0
    for k in range(KT):
        for ch in range(4):
            stg = g.stage[i % 2]
            cb = cbf[i % 2]
            P.op("sp", "dma_start", out=stg[:, 0:1024], in_=gin[k * 128:(k + 1) * 128, 2048 + ch * 1024:2048 + (ch + 1) * 1024])
            P.op(("dve", "pool")[i % 2], "tensor_copy", out=cb[:], in_=stg[:, 0:1024])
            P.op("sp", "dma_start", out=wscr[ch * 8:(ch + 1) * 8, :, k * 128:(k + 1) * 128].rearrange("h p n -> p h n"),
                 in_=cb[:].rearrange("p (h n) -> p h n", h=8))
            i += 1
    P.barrier()
    wbuf = [sb(L + "wbuf%d" % i, [128, 1024], BF16) for i in range(2)]
    g.wcnt = 0

    def stream_w(hd):
        wb = wbuf[g.wcnt % 2]
        g.wcnt += 1
        P.op("sp", "dma_start", out=wb[:], in_=wscr[hd])
        return wb
    ps = g.psum
    PA, PB, PX, PQ, PD, PM, PN, PT = range(8)
    PO = PD
    cwr = sb(L + "cwr", [4, 4096])
    P.op("sp", "dma_start", out=cwr[:], in_=dr["gd_conv_w"][:, :])
    for ct in range(32):
        P.op("pe", "transpose", out=ps[PA][:, ct * 4:(ct + 1) * 4], in_=cwr[0:4, ct * 128:(ct + 1) * 128],
             identity=g.c["ident_f"][0:4, 0:4])
    cw = sb(L + "cw", [128, 128])
    P.op("dve", "tensor_copy", out=cw[:], in_=ps[PA][:, 0:128])
    alog = sb(L + "alog", [16, 1])
    dtb = sb(L + "dtb", [16, 1])
    P.op("sp", "dma_start", out=alog[:], in_=dr["gd_a_log"][:, :])
    P.op("sp", "dma_start", out=dtb[:], in_=dr["gd_dt_bias"][:, :])
    nA = sb(L + "nA", [16, 1])
    P.op("act", "activation", out=nA[:], in_=alog[:], func=AF.Exp)
    P.op("dve", "tensor_scalar", out=nA[:], in0=nA[:], scalar1=-1.0, scalar2=None, op0=ALU.mult)
    one16 = sb(L + "one16", [16, 1])
    P.op("dve", "memset", ap=one16[:], constant=1.0)
    onr = sb(L + "onr", [1, 128])
    ong = sb(L + "ong", [128, 1])
    P.op("sp", "dma_start", out=onr[:], in_=dr["gd_onorm"][:, :])
    transpose_small(g, ong[:], onr[0:1, :], 1, ps[PB])
    ones_m = sb(L + "ones_m", [128, 128])
    P.op("dve", "tensor_scalar", out=ones_m[:], in0=g.c["ones_f"][:], scalar1=1.0 / 128, scalar2=None, op0=ALU.mult)
    epsc = sb(L + "epsc", [128, 1])
    P.op("dve", "memset", ap=epsc[:], constant=EPS)
    identb64 = g.c["ident_b"]

    xb = [sb(L + "xb0", [128, D])]
    xnb = [sb(L + "xn0", [128, D], BF16)]
    ss = sb(L + "ss", [128, 4])
    rstd = sb(L + "rstd", [128, 4])
    hT = sb(L + "hT", [128, KT, GT], BF16)
    halo = sb(L + "halo", [128, 32, 3])
    P.op("pool", "memset", ap=halo[:], constant=0.0)
    xc = sb(L + "xc", [128, 3 + GT])
    acc = sb(L + "acc", [128, GT])
    f_a = sb(L + "f_a", [128, GT])
    f_b = sb(L + "f_b", [128, GT])
    f_c = sb(L + "f_c", [128, GT])
    qT = sb(L + "qT", [128, GT], BF16)
    kT = sb(L + "kT", [128, GT], BF16)
    kbT = sb(L + "kbT", [128, GT], BF16)
    qd = [sb(L + "qd%d" % i, [128, GT], BF16) for i in range(2)]
    vT = sb(L + "vT", [128, GT], BF16)
    zs = [sb(L + "zs%d" % i, [128, GT], BF16) for i in range(2)]
    edl4 = [sb(L + "edl4_%d" % i, [128, NC]) for i in range(2)]
    r_beta = sb(L + "r_beta", [16, GT])
    r_g = sb(L + "r_g", [16, GT])
    r_d = sb(L + "r_d", [16, GT])
    r_be = sb(L + "r_be", [16, GT])
    r_edl = sb(L + "r_edl", [16, GT])
    tk_b = sb(L + "tk_b", [64, 16, NC])
    tk_be = sb(L + "tk_be", [64, 16, NC])
    tk_edl = sb(L + "tk_edl", [64, 16, NC])
    tk_d = sb(L + "tk_d", [64, 16, NC])
    Et = sb(L + "Et", [64, GT])
    Em = sb(L + "Em", [64, GT])
    Mk = [sb(L + "Mk%d" % i, [64, 512]) for i in range(2)]
    Nk = [sb(L + "Nk%d" % i, [64, 512]) for i in range(2)]
    Tk = [sb(L + "Tk%d" % i, [64, 512]) for i in range(2)]
    Tbf = sb(L + "Tbf", [64, 512], BF16)
    qkM = sb(L + "qkM", [64, 512], BF16)
    vb = sb(L + "vb", [64, 8, 128], BF16)
    kbd = sb(L + "kbd", [64, 8, 128], BF16)
    khat = sb(L + "khat", [64, 8, 128], BF16)
    wTn = sb(L + "wTn", [128, 512], BF16)
    vn = [sb(L + "vn%d" % i, [64, 128], BF16) for i in range(2)]
    S = [sb(L + "S%d" % h, [128, 128]) for h in range(16)]
    Sb = [sb(L + "Sb%d" % h, [128, 128], BF16) for h in range(16)]
    for h in range(16):
        P.op("pool", "memset", ap=S[h][:], constant=0.0)
        P.op("pool", "memset", ap=Sb[h][:], constant=0.0)
    og = sb(L + "og", [128, 16, GT], BF16)
    identI = sb(L + "identI", [64, 512])
    for b in range(8):
        P.op("dve", "tensor_copy", out=identI[:, b * 64:(b + 1) * 64], in_=g.c["ident_f"][0:64, 0:64])
    rm16 = g.c["gd_rm"][:, :]
    sel = g.c["gd_sel"]

    def conv_silu(pst, ct, out_ap, func=AF.Silu):
        P.op("pool", "tensor_copy", out=xc[:, 0:3], in_=halo[:, ct, :])
        P.op("act", "activation", out=xc[:, 3:3 + GT], in_=pst[:, 0:GT], func=AF.Copy)
        P.op("pool", "tensor_copy", out=halo[:, ct, :], in_=xc[:, GT:GT + 3])
        P.op("dve", "tensor_scalar", out=acc[:], in0=xc[:, 3:3 + GT], scalar1=cw[:, ct * 4 + 3:ct * 4 + 4], scalar2=None,
             op0=ALU.mult)
        for jt in (2, 1, 0):
            P.op("dve", "scalar_tensor_tensor", out=acc[:], in0=xc[:, jt:jt + GT], scalar=cw[:, ct * 4 + jt:ct * 4 + jt + 1],
                 in1=acc[:], op0=ALU.mult, op1=ALU.add)
        P.op("act", "activation", out=out_ap, in_=acc[:], func=func)

    def inproj(pst, c0, m=128):
        for k in range(KT):
            P.op("pe", "matmul", out=pst, lhsT=W(w_in[:, k, c0:c0 + m], k), rhs=Sub(hT[:, k, :], k),
                 start=(k == 0), stop=(k == KT - 1))

    def inproj_s(pst, hd):
        wb = stream_w(hd)
        for k in range(KT):
            P.op("pe", "matmul", out=pst, lhsT=wb[:, k * 128:(k + 1) * 128], rhs=Sub(hT[:, k, :], k),
                 start=(k == 0), stop=(k == KT - 1))

    def l2norm(src_f, out_bf, scale):
        P.op("pool", "tensor_tensor", out=f_b[:], in0=src_f[:], in1=src_f[:], op=ALU.mult)
        P.op("pe", "matmul", out=ps[PB][:, 0:GT], lhsT=g.c["ones_f"][:], rhs=f_b[:], start=True, stop=True)
        P.op("act", "activation", out=f_c[:], in_=ps[PB][:, 0:GT], func=AF.Sqrt, bias=epsc[:, 0:1])
        P.op("dve", "reciprocal", out=f_c[:], in_=f_c[:])
        P.op("dve", "scalar_tensor_tensor", out=out_bf, in0=src_f[:], scalar=scale, in1=f_c[:], op0=ALU.mult, op1=ALU.mult)

    for mt in range(T // GT):
        prologue(g, mt, src, srctag, hT, xb, xnb, ss, rstd, mtl=GT)
        P.mark("gd_rows%d" % mt)
        inproj(ps[PA][0:16, 0:GT], 2048, 16)
        inproj(ps[PA][0:16, GT:2 * GT], 2064, 16)
        P.op("act", "activation", out=r_beta[:], in_=ps[PA][0:16, GT:2 * GT], func=AF.Sigmoid)
        P.op("act", "activation", out=r_g[:], in_=ps[PA][0:16, 0:GT], func=AF.Exp, bias=dtb[:, 0:1])
        P.op("act", "activation", out=r_g[:], in_=r_g[:], func=AF.Ln, bias=one16[:, 0:1])
        P.op("dve", "tensor_scalar", out=r_g[:], in0=r_g[:], scalar1=nA[:, 0:1], scalar2=None, op0=ALU.mult)
        P.op("dve", "tensor_tensor_scan", out=r_d[:], data0=rm16, data1=r_g[:], initial=0.0, op0=ALU.mult, op1=ALU.add)
        P.op("act", "activation", out=r_be[:], in_=r_d[:], func=AF.Exp)
        P.op("dve", "tensor_tensor", out=r_be[:], in0=r_be[:], in1=r_beta[:], op=ALU.mult)
        d3 = r_d[:].rearrange("p (c j) -> p c j", j=64)
        P.op("dve", "tensor_tensor", out=r_edl[:].rearrange("p (c j) -> p c j", j=64), in0=d3,
             in1=d3[:, :, 63:64].to_broadcast([16, NC, 64]), op=ALU.subtract)
        P.op("act", "activation", out=r_edl[:], in_=r_edl[:], func=AF.Exp, scale=-1.0)
        for (row, tok) in ((r_beta, tk_b), (r_be, tk_be), (r_edl, tk_edl), (r_d, tk_d)):
            for c in range(NC):
                P.op("pe", "transpose", out=ps[PB][0:64, c * 16:(c + 1) * 16], in_=row[0:16, c * 64:(c + 1) * 64],
                     identity=g.c["ident_f"][0:16, 0:16])
            P.op("dve", "tensor_copy", out=tok[:].rearrange("p h c -> p c h"),
                 in_=ps[PB][0:64, 0:NC * 16].rearrange("p (c h) -> p c h", h=16))
        for gq in range(8):
            P.mark("gd_g%d_%d" % (mt, gq))
            inproj(ps[PA][:, 0:GT], gq * 128)
            conv_silu(ps[PA], gq, f_a[:])
            l2norm(f_a, qT[:], 128 ** -0.5)
            inproj(ps[PA][:, 0:GT], 1024 + gq * 128)
            conv_silu(ps[PA], 8 + gq, f_a[:])
            l2norm(f_a, kT[:], 1.0)
            pxb = ps[PX][:].bitcast(BF16)
            for c in range(NC):
                P.op("pe", "transpose", out=pxb[0:64, c * 128:(c + 1) * 128], in_=kT[:, c * 64:(c + 1) * 64],
                     identity=g.c["ident_b"][:])
            for hh in range(2):
                h = 2 * gq + hh
                bs = slice(hh * NC, (hh + 1) * NC)
                for (tok, dstk) in ((tk_be, kbd), (tk_edl, khat)):
                    P.op("dve", "tensor_tensor", out=dstk[:, bs, :],
                         in0=pxb[0:64, 0:NC * 128].rearrange("p (c d) -> p c d", d=128),
                         in1=tok[:, h, :].unsqueeze(2).to_broadcast([64, NC, 128]), op=ALU.mult)
            for c in range(NC):
                P.op("pe", "matmul", out=ps[PQ][0:64, c * 64:(c + 1) * 64], lhsT=kT[:, c * 64:(c + 1) * 64],
                     rhs=qT[:, c * 64:(c + 1) * 64], start=True, stop=True)
            for hh in range(2):
                h = 2 * gq + hh
                bs = slice(hh * NC, (hh + 1) * NC)
                cs = slice(hh * GT, (hh + 1) * GT)
                inproj_s(ps[PA][:, 0:GT], h)
                conv_silu(ps[PA], 16 + h, vT[:])
                inproj_s(ps[PB][:, 0:GT], 16 + h)
                P.op("act", "activation", out=zs[hh][:], in_=ps[PB][:, 0:GT], func=AF.Silu)
                for c in range(NC):
                    P.op("pe", "transpose", out=pxb[0:64, c * 128:(c + 1) * 128], in_=vT[:, c * 64:(c + 1) * 64],
                         identity=g.c["ident_b"][:])
                P.op("dve", "tensor_tensor", out=vb[:, bs, :], in0=pxb[0:64, 0:NC * 128].rearrange("p (c d) -> p c d", d=128),
                     in1=tk_b[:, h, :].unsqueeze(2).to_broadcast([64, NC, 128]), op=ALU.mult)
                P.op("pe", "matmul", out=ps[PD][:, 0:GT], lhsT=sel[0:16, h * 128:(h + 1) * 128], rhs=r_beta[:], start=True, stop=True)
                P.op("pe", "matmul", out=ps[PD][:, GT:2 * GT], lhsT=sel[0:16, h * 128:(h + 1) * 128], rhs=r_d[:], start=True, stop=True)
                P.op("dve", "tensor_tensor", out=kbT[:], in0=kT[:], in1=ps[PD][:, 0:GT], op=ALU.mult)
                P.op("act", "activation", out=f_b[:], in_=ps[PD][:, GT:2 * GT], func=AF.Exp)
                P.op("pool", "tensor_tensor", out=qd[hh][:], in0=qT[:], in1=f_b[:], op=ALU.mult)
                P.op("act", "activation", out=edl4[hh][:], in_=ps[PD][:, GT:2 * GT].rearrange("p (c j) -> p c j", j=64)[:, :, 63],
                     func=AF.Exp)
                P.op("dve", "tensor_tensor", out=Et[:].rearrange("p (c j) -> p c j", j=64),
                     in0=ps[PD][0:64, GT:2 * GT].rearrange("p (c j) -> p c j", j=64),
                     in1=tk_d[:, h, :].unsqueeze(2).to_broadcast([64, NC, 64]), op=ALU.subtract)
                P.op("dve", "tensor_scalar", out=Et[:], in0=Et[:], scalar1=0.0, scalar2=None, op0=ALU.min)
                P.op("act", "activation", out=Et[:], in_=Et[:], func=AF.Exp)
                for c in range(NC):
                    P.op("pe", "matmul", out=ps[PM][0:64, (hh * NC + c) * 64:(hh * NC + c + 1) * 64],
                         lhsT=kT[:, c * 64:(c + 1) * 64], rhs=kbT[:, c * 64:(c + 1) * 64], start=True, stop=True)
                P.op("pool", "tensor_tensor", out=Em[:], in0=Et[:], in1=g.c["gd_maskS"][:, :], op=ALU.mult)
                P.op("dve", "scalar_tensor_tensor", out=Mk[0][:, cs], in0=ps[PM][0:64, cs], scalar=-1.0, in1=Em[:],
                     op0=ALU.mult, op1=ALU.mult)
                P.op("pool", "tensor_tensor", out=Em[:], in0=Et[:], in1=g.c["gd_maskI"][:, :], op=ALU.mult)
                P.op("dve", "tensor_tensor", out=qkM[:, cs], in0=ps[PQ][0:64, 0:GT], in1=Em[:], op=ALU.mult)
            for b in range(8):
                P.op("pe", "transpose", out=ps[PX][0:64, b * 64:(b + 1) * 64], in_=Mk[0][:, b * 64:(b + 1) * 64],
                     identity=g.c["ident_f"][0:64, 0:64])
            P.op("act", "activation", out=Nk[0][:], in_=ps[PX][0:64, 0:512], func=AF.Copy)
            P.op("pool", "tensor_tensor", out=Tk[0][:], in0=Mk[0][:], in1=identI[:], op=ALU.add)
            for st_ in range(5):
                cur, nxt = st_ % 2, (st_ + 1) % 2
                if st_ < 4:
                    for b in range(8):
                        bc = slice(b * 64, (b + 1) * 64)
                        P.op("pe", "matmul", out=ps[PM][0:64, bc], lhsT=Nk[cur][:, bc], rhs=Mk[cur][:, bc], start=True, stop=True)
                    P.op("dve", "tensor_copy", out=Mk[nxt][:], in_=ps[PM][0:64, :])
                for b in range(8):
                    bc = slice(b * 64, (b + 1) * 64)
                    P.op("pe", "matmul", out=ps[PN][0:64, bc], lhsT=Mk[cur][:, bc], rhs=Nk[cur][:, bc], start=True, stop=True)
                P.op("act", "activation", out=Nk[nxt][:], in_=ps[PN][0:64, :], func=AF.Copy)
                for b in range(8):
                    bc = slice(b * 64, (b + 1) * 64)
                    P.op("pe", "matmul", out=ps[PT][0:64, bc], lhsT=Nk[nxt][:, bc], rhs=Tk[cur][:, bc], start=True, stop=True)
                P.op("dve", "tensor_tensor", out=Tk[nxt][:], in0=ps[PT][0:64, :], in1=Tk[cur][:], op=ALU.add)
            P.op("pool", "tensor_copy", out=Tbf[:], in_=Tk[1][:])
            TT = Tbf
            for b in range(8):
                bc = slice(b * 64, (b + 1) * 64)
                P.op("pe", "matmul", out=ps[PN][:, bc], lhsT=kbd[:, b, :], rhs=TT[:, bc], start=True, stop=True)
            P.op("act", "activation", out=wTn[:], in_=ps[PN][:], func=AF.Copy, scale=-1.0)
            for hh in range(2):
                h = 2 * gq + hh
                for c in range(NC):
                    b = hh * NC + c
                    bc = slice(b * 64, (b + 1) * 64)
                    v_ = vn[c % 2]
                    P.op("pe", "matmul", out=ps[PT][0:64, 0:128], lhsT=TT[:, bc], rhs=vb[:, b, :], start=True, stop=False)
                    P.op("pe", "matmul", out=ps[PT][0:64, 0:128], lhsT=wTn[:, bc], rhs=Sb[h][:], start=False, stop=True)
                    P.op("act", "activation", out=v_[:], in_=ps[PT][0:64, 0:128], func=AF.Copy)
                    P.op("pe", "matmul", out=ps[PO][:, bc], lhsT=Sb[h][:], rhs=qd[hh][:, c * 64:(c + 1) * 64], start=True, stop=False)
                    P.op("pe", "matmul", out=ps[PO][:, bc], lhsT=v_[:], rhs=qkM[:, bc], start=False, stop=True)
                    P.op("pe", "matmul", out=ps[PM][:, 0:128], lhsT=khat[:, b, :], rhs=v_[:], start=True, stop=True)
                    P.op("dve", "scalar_tensor_tensor", out=S[h][:], in0=S[h][:], scalar=edl4[hh][:, c:c + 1], in1=ps[PM][:, 0:128],
                         op0=ALU.mult, op1=ALU.add)
                    P.op("pool", "tensor_copy", out=Sb[h][:], in_=S[h][:])
            for hh in range(2):
                h = 2 * gq + hh
                cs = slice(hh * GT, (hh + 1) * GT)
                P.op("act", "activation", out=f_a[:], in_=ps[PO][:, cs], func=AF.Square)
                P.op("pe", "matmul", out=ps[PB][:, 0:GT], lhsT=ones_m[:], rhs=f_a[:], start=True, stop=True)
                P.op("act", "activation", out=f_c[:], in_=ps[PB][:, 0:GT], func=AF.Sqrt, bias=epsc[:, 0:1])
                P.op("dve", "reciprocal", out=f_c[:], in_=f_c[:])
                P.op("dve", "scalar_tensor_tensor", out=f_b[:], in0=ps[PO][:, cs], scalar=ong[:, 0:1], in1=f_c[:],
                     op0=ALU.mult, op1=ALU.mult)
                P.op("pool", "tensor_tensor", out=Sub(og[:, h, :], h), in0=f_b[:], in1=zs[hh][:], op=ALU.mult)
        P.mark("epilogue%d" % mt)
        epilogue(g, mt, src, srctag, dst, og, 16, w_out, xb, ps[PA], ps[PB], mtl=GT)


def core_inputs(b, T, inputs, consts):
    f = np.float32
    m = {
        "x": np.ascontiguousarray(inputs["x"][b, :T]).astype(f, copy=False),
        "c8": np.ascontiguousarray(inputs["c"][b].reshape(8, 128)),
        "positions": np.ascontiguousarray(inputs["positions"][b, :T].reshape(1, T)).astype(np.int32, copy=False),
        "hgrn_lb": np.ascontiguousarray(inputs["hgrn_lb"].reshape(32, 128)),
        "ada_w": inputs["ada_w"],
        "ada_b": np.ascontiguousarray(inputs["ada_b"].reshape(4, 24, 128)),
        "ada_b_row": np.ascontiguousarray(inputs["ada_b"].reshape(4, 1, 3 * D)),
        "norm_g": np.ascontiguousarray(inputs["norm_g"].reshape(4, 8, 128)),
        "hg_in_w": inputs["hg_in_w"],
        "hg_out_w": inputs["hg_out_w"],
        "hg_onorm": np.ascontiguousarray(inputs["hg_onorm"].reshape(-1, 1, 128)),
        "sw_in_w": inputs["sw_in_w"],
        "sw_out_w": inputs["sw_out_w"],
        "sw_qnorm": inputs["sw_qnorm"],
        "sw_knorm": inputs["sw_knorm"],
        "sw_sinks": inputs["sw_sinks"],
        "gd_in_w": inputs["gd_in_w"],
        "gd_out_w": inputs["gd_out_w"],
        "gd_conv_w": np.ascontiguousarray(inputs["gd_conv_w"][0]),
        "gd_a_log": np.ascontiguousarray(inputs["gd_a_log"].reshape(16, 1)),
        "gd_dt_bias": np.ascontiguousarray(inputs["gd_dt_bias"].reshape(16, 1)),
        "gd_onorm": np.ascontiguousarray(inputs["gd_onorm"].reshape(1, 128)),
    }
    m.update(consts)
    return m


def run_layers(inputs, layers, T, ncores=2, trace=False):
    inputs = {k: np.asarray(v) for k, v in inputs.items()}
    consts = make_consts()
    nc = build(T, layers)
    in_maps = [core_inputs(b, T, inputs, consts) for b in range(ncores)]
    res = run_bass_kernel_spmd(nc, in_maps, core_ids=list(range(ncores)), trace=trace)
    out = np.stack([np.asarray(r["out"]) for r in res.results], axis=0)
    return out, res


def kernel(**inputs):
    T = inputs["x"].shape[1]
    out, _ = run_layers(inputs, [0, 1, 2, 3], T)
    return out.astype(np.float32, copy=False)
```

```python
import numpy as np
import ml_dtypes
from contextlib import ExitStack

import concourse.bass as bass
import concourse.mybir as mybir
from concourse.bass_utils import run_bass_kernel_spmd

F32 = mybir.dt.float32
BF16 = mybir.dt.bfloat16
I32 = mybir.dt.int32
U32 = mybir.dt.uint32
AF = mybir.ActivationFunctionType
ALU = mybir.AluOpType
AX = mybir.AxisListType

D = 1024
KT = 8
EPS = 1e-6
MT = 512
NDMASEM = 40
SW = 1024


class Sub:
    def __init__(self, ap, tag):
        self.ap = ap
        self.tag = tag


class Prog:
    ENGS = ("pe", "act", "dve", "pool", "sp")
    WRITE_KW = ("out", "accum_out", "ap")

    def __init__(self, nc):
        self.nc = nc
        self.ins = []
        self.writers = {}
        self.readers = {}
        self.dma_sem_of = {}
        self.dma_tot = []
        self.same_engine_sync = True
        self._bar_pending = set()
        self._bar_ids = []
        self.marks = {}
        self.psum_excl = True

    def _key(self, v):
        if isinstance(v, Sub):
            return (v.ap.tensor.name, v.tag), v.ap
        return (v.tensor.name, None), v

    def op(self, eng, meth, **kw):
        reads, writes, real = [], [], {}
        for k, v in kw.items():
            if isinstance(v, (Sub, bass.AP)):
                key, ap = self._key(v)
                real[k] = ap
                (writes if k in self.WRITE_KW else reads).append(key)
            else:
                real[k] = v
        i = len(self.ins)
        deps = set()
        if eng in self._bar_pending:
            deps.update(self._bar_ids)
            self._bar_pending.discard(eng)
        for key in reads:
            deps.update(self.writers.get(key, ()))
            if self.psum_excl and key[0].startswith("ps"):
                deps.update(self.readers.get(key, ()))
        for key in writes:
            deps.update(self.writers.get(key, ()))
            deps.update(self.readers.get(key, ()))
        for key in reads:
            self.readers.setdefault(key, []).append(i)
        for key in writes:
            self.writers[key] = [i]
            self.readers[key] = []
        dma = meth == "dma_start"
        rec = dict(eng=eng, meth=meth, kw=real, deps=deps, dma=dma, inc=False, cnt=0, sem=None, tgt=0)
        if dma:
            sbside = None
            for k in ("out", "in_"):
                ap = real[k]
                if "SB" in type(ap.tensor).__name__:
                    sbside = ap.tensor.name
            assert sbside is not None, "dma needs an SBUF side"
            s = self.dma_sem_of.setdefault(sbside, len(self.dma_sem_of))
            if s >= len(self.dma_tot):
                self.dma_tot.append(0)
            self.dma_tot[s] += 16
            rec["sem"] = s
            rec["tgt"] = self.dma_tot[s]
        self.ins.append(rec)
        return i

    def mark(self, name):
        self.marks[name] = len(self.ins)

    def barrier(self):
        last = {}
        lastdma = {}
        for j, r in enumerate(self.ins):
            if r["dma"]:
                lastdma[r["sem"]] = j
            else:
                last[r["eng"]] = j
        self._bar_ids = list(last.values()) + list(lastdma.values())
        self._bar_pending = set(self.ENGS)
        self.dma_sem_of = {}

    def setup(self, stack):
        nc = self.nc
        self.sems = {e: stack.enter_context(nc.semaphore("s_" + e)) for e in self.ENGS}
        self.dsem = [stack.enter_context(nc.semaphore("d_%d" % k)) for k in range(NDMASEM)]
        self.waited = {e: {p: -1 for p in self.ENGS} for e in self.ENGS}
        self.dma_waited = {e: set() for e in self.ENGS}
        self.cnt = {e: 0 for e in self.ENGS}
        self.done = 0

    def emit(self, final=False):
        nc = self.nc
        import os as _os
        ins = self.ins
        lo, hi = self.done, len(ins)
        stop = _os.environ.get("KSTOP")
        if stop:
            n = int(stop) if stop.isdigit() else self.marks.get(stop, hi)
            hi = max(lo, min(hi, n))
        assert len(self.dma_tot) <= NDMASEM, len(self.dma_tot)
        waited, dma_waited = self.waited, self.dma_waited
        for i in range(lo, hi):
            r = ins[i]
            e = r["eng"]
            need = {}
            dm = []
            for j in r["deps"]:
                p = ins[j]
                if p["dma"]:
                    if j not in dma_waited[e]:
                        dm.append(j)
                else:
                    pe_ = p["eng"]
                    if pe_ == e and not r["dma"]:
                        if not (self.same_engine_sync and e != "pe"):
                            continue
                    if j > waited[e][pe_]:
                        need[pe_] = max(need.get(pe_, -1), j)
            for pe_, j in list(need.items()):
                if j < lo and not ins[j]["inc"]:
                    jj = j
                    while jj < lo and not (ins[jj]["eng"] == pe_ and ins[jj]["inc"] and not ins[jj]["dma"]):
                        jj += 1
                    assert jj < lo, "no increment available for cross-block dependency"
                    need[pe_] = jj
            r["w_cmp"] = need
            r["w_dma"] = dm
            for pe_, j in need.items():
                waited[e][pe_] = max(waited[e][pe_], j)
                ins[j]["inc"] = True
            for j in dm:
                dma_waited[e].add(j)
        lastc = {}
        for i in range(lo, hi):
            if not ins[i]["dma"]:
                lastc[ins[i]["eng"]] = i
        for i in lastc.values():
            ins[i]["inc"] = True
        for i in range(lo, hi):
            r = ins[i]
            if r["inc"] and not r["dma"]:
                self.cnt[r["eng"]] += 1
                r["cnt"] = self.cnt[r["eng"]]
        per = {e: [] for e in self.ENGS}
        for i in range(lo, hi):
            per[ins[i]["eng"]].append(ins[i])
        sems, dsem = self.sems, self.dsem
        dma_tot = [0] * len(self.dma_tot)
        for r in ins[:hi]:
            if r["dma"]:
                dma_tot[r["sem"]] = max(dma_tot[r["sem"]], r["tgt"])
        self.done = len(ins)

        def run(eng_obj, lst, fin=False):
            for r in lst:
                for pe_, j in r["w_cmp"].items():
                    eng_obj.wait_ge(sems[pe_], ins[j]["cnt"])
                for j in r["w_dma"]:
                    eng_obj.wait_ge(dsem[ins[j]["sem"]], ins[j]["tgt"])
                inst = getattr(eng_obj, r["meth"])(**r["kw"])
                if r["dma"]:
                    inst.then_inc(dsem[r["sem"]], 16)
                elif r["inc"]:
                    inst.then_inc(sems[r["eng"]], 1)
            if fin:
                for k in range(len(dma_tot)):
                    if dma_tot[k]:
                        eng_obj.wait_ge(dsem[k], dma_tot[k])

        with nc.Block() as block:
            @block.tensor
            def _(eng):
                run(eng, per["pe"])

            @block.scalar
            def _(eng):
                run(eng, per["act"])

            @block.vector
            def _(eng):
                run(eng, per["dve"])

            @block.gpsimd
            def _(eng):
                run(eng, per["pool"])

            @block.sync
            def _(eng):
                run(eng, per["sp"], fin=final)


def make_consts():
    c = {}
    c["ident_f"] = np.eye(128, dtype=np.float32)
    c["ident_b"] = np.eye(128, dtype=np.float32).astype(ml_dtypes.bfloat16)
    c["ones_f"] = np.ones((128, 128), np.float32)
    for ch in (32, 64):
        s = np.arange(ch)[:, None]
        t = np.arange(MT)[None, :] % ch
        c["hg_maskP%d" % ch] = (s <= t).astype(np.uint32)
        rm = np.ones((128, MT), np.float32)
        rm[:, ::ch] = 0.0
        c["resetmask%d" % ch] = rm
    k = np.arange(128)[:, None]
    q = np.arange(MT)[None, :] % 128
    c["sw_mask_cur"] = (k <= q).astype(np.float32).astype(ml_dtypes.bfloat16)
    c["sw_mask_prev"] = (k > q).astype(np.float32).astype(ml_dtypes.bfloat16)
    rot = np.zeros((128, 128), np.float32)
    for blk in (0, 64):
        for m in range(32):
            rot[blk + m + 32, blk + m] = -1.0
            rot[blk + m, blk + m + 32] = 1.0
    c["sw_rotT"] = rot
    ob = np.zeros((128, 128), np.float32)
    ob[0:64, 0:64] = 1.0 / 64
    ob[64:128, 64:128] = 1.0 / 64
    c["sw_ones_blk"] = ob
    inv = (10000.0 ** (-np.arange(0, 64, 2, dtype=np.float32) / 64)).astype(np.float32)
    c["sw_invfreq"] = np.tile(inv, 4).reshape(128, 1).astype(np.float32)
    jj = np.arange(64)[:, None]
    ii = np.arange(256)[None, :] % 64
    c["gd_maskI"] = (jj <= ii).astype(np.float32)
    rm = np.ones((16, 256), np.float32)
    rm[:, ::64] = 0.0
    c["gd_rm"] = rm
    j2 = np.arange(128)[:, None]
    i2 = np.arange(256)[None, :] % 128
    c["gd_maskSP"] = ((j2 // 64 == i2 // 64) & (j2 % 64 < i2 % 64)).astype(np.float32)
    sel = np.zeros((16, 16, 128), np.float32)
    for h in range(16):
        sel[h, h, :] = 1.0
    c["gd_sel"] = sel.reshape(16, 16 * 128)
    return c


CONST_SPECS = {
    "gd_maskI": ([64, 256], F32),
    "gd_rm": ([16, 256], F32),
    "gd_maskSP": ([128, 256], F32),
    "gd_sel": ([16, 2048], F32),
    "sw_mask_cur": ([128, MT], BF16),
    "sw_mask_prev": ([128, MT], BF16),
    "sw_rotT": ([128, 128], F32),
    "sw_ones_blk": ([128, 128], F32),
    "sw_invfreq": ([128, 1], F32),
    "ident_f": ([128, 128], F32),
    "ident_b": ([128, 128], BF16),
    "ones_f": ([128, 128], F32),
    "hg_maskP32": ([32, MT], U32),
    "hg_maskP64": ([64, MT], U32),
    "resetmask32": ([128, MT], F32),
    "resetmask64": ([128, MT], F32),
}


COMMON_CONSTS = ("ident_f", "ident_b", "ones_f")
HG_CONSTS = ("hg_maskP32", "hg_maskP64", "resetmask32", "resetmask64")
GD_CONSTS = ("gd_maskI", "gd_sel", "gd_rm", "gd_maskSP")
SW_CONSTS = ("sw_mask_cur", "sw_mask_prev", "sw_rotT", "sw_ones_blk", "sw_invfreq")


class Ctx:
    pass


def load_consts(g, names, pfx=""):
    for k in names:
        shape, dt = CONST_SPECS[k]
        g.c[k] = g.sb(pfx + "c_" + k, shape, dt)
        g.P.op("sp", "dma_start", out=g.c[k][:], in_=g.dr[k][:, :])


def build(T, layers, n_hg=2):
    assert T % MT == 0
    nc = bass.Bass("TRN2", target_bir_lowering=False)
    P = Prog(nc)
    g = Ctx()
    g.nc, g.P, g.T = nc, P, T
    dr = {}

    def din(name, shape, dt=F32):
        dr[name] = nc.dram_tensor(name, shape, dt, kind="ExternalInput").ap()
        return dr[name]

    din("x", [T, D])
    din("c8", [8, 128])
    din("positions", [1, T], I32)
    din("hgrn_lb", [32, 128])
    din("ada_w", [4, D, 3 * D])
    din("ada_b", [4, 24, 128])
    din("ada_b_row", [4, 1, 3 * D])
    din("norm_g", [4, 8, 128])
    din("hg_in_w", [n_hg, D, 4 * D])
    din("hg_out_w", [n_hg, D, D])
    din("hg_onorm", [n_hg, 1, 128])
    din("sw_in_w", [1, D, 2560])
    din("sw_out_w", [1, D, D])
    din("sw_qnorm", [1, 64])
    din("sw_knorm", [1, 64])
    din("sw_sinks", [1, 16])
    din("gd_in_w", [1, D, 6176])
    din("gd_out_w", [1, 2 * D, D])
    din("gd_conv_w", [4, 4096])
    din("gd_a_log", [16, 1])
    din("gd_dt_bias", [16, 1])
    din("gd_onorm", [1, 128])
    for k, (shape, dt) in CONST_SPECS.items():
        din(k, shape, dt)
    out = nc.dram_tensor("out", [T, D], F32, kind="ExternalOutput").ap()
    scr = [nc.dram_tensor("xs%d" % i, [T, D], F32, kind="Internal").ap() for i in range(2)]
    g.gd_wscr = nc.dram_tensor("gd_wscr", [32, 128, 1024], BF16, kind="Internal").ap()
    g.dr = dr

    with ExitStack() as st:
        def sbg(name, shape, dt=F32):
            return st.enter_context(nc.sbuf_tensor(name, shape, dt))

        P.setup(st)
        g.sb = sbg
        g.psum = [st.enter_context(nc.psum_tensor("ps%d" % i, [128, 512], F32)) for i in range(8)]
        g.c = {}
        load_consts(g, COMMON_CONSTS)
        g.stage = [sbg("stage%d" % i, [128, SW], F32) for i in range(2)]
        P.mark("prep_common")
        prep_common(g)
        P.mark("after_prep_common")
        src = dr["x"]
        srctag = lambda ap, tag: ap
        for li, layer in enumerate(layers):
            dst = out if li == len(layers) - 1 else scr[li % 2]
            kind = layer % 3
            with ExitStack() as lst:
                g.sb = lambda name, shape, dt=F32: lst.enter_context(nc.sbuf_tensor(name, shape, dt))
                if kind == 0:
                    load_consts(g, HG_CONSTS, "L%d" % layer)
                    hgrn2_layer(g, layer, src, srctag, dst)
                elif kind == 1:
                    load_consts(g, SW_CONSTS, "L%d" % layer)
                    swa_layer(g, layer, src, srctag, dst)
                else:
                    load_consts(g, GD_CONSTS, "L%d" % layer)
                    gdn_layer(g, layer, src, srctag, dst)
                P.barrier()
                P.emit(final=(li == len(layers) - 1))
            src = dst
            srctag = Sub
    return nc


def transpose_small(g, dst, src_rows, nrows, ps):
    P = g.P
    P.op("pe", "transpose", out=ps[:, 0:nrows], in_=src_rows, identity=g.c["ident_f"][0:nrows, 0:nrows])
    P.op("dve", "tensor_copy", out=dst, in_=ps[:, 0:nrows])


def W(ap, k):
    return Sub(ap, ("w", k))


def prep_common(g):
    P, sb, dr = g.P, g.sb, g.dr
    ps = g.psum[0]
    g.rows_a = sb("rows_a", [32, 128], F32)
    g.rows_g = sb("rows_g", [8, 128], F32)
    rows_c = sb("rows_c", [8, 128], F32)
    g.c_col = sb("c_col", [128, 8], F32)
    P.op("sp", "dma_start", out=rows_c[:], in_=dr["c8"][:, :])
    transpose_small(g, g.c_col[:], rows_c[:], 8, ps)
    P.op("sp", "dma_start", out=g.rows_a[:], in_=dr["hgrn_lb"][:, :])
    lbT = sb("lbT", [128, 32], F32)
    transpose_small(g, lbT[:], g.rows_a[:], 32, g.psum[1])
    ex = sb("lb_exp", [128, 32], F32)
    P.op("act", "activation", out=ex[:], in_=lbT[:], func=AF.Exp)
    den = sb("lb_den", [128, 8], F32)
    P.op("dve", "tensor_tensor", out=den[:], in0=ex[:, 0:8], in1=ex[:, 8:16], op=ALU.add)
    P.op("dve", "tensor_tensor", out=den[:], in0=den[:], in1=ex[:, 16:24], op=ALU.add)
    P.op("dve", "tensor_tensor", out=den[:], in0=den[:], in1=ex[:, 24:32], op=ALU.add)
    rden = sb("lb_rden", [128, 8], F32)
    P.op("dve", "reciprocal", out=rden[:], in_=den[:])
    g.lb = {}
    lb0 = sb("lb_l0", [128, 8], F32)
    P.op("dve", "memset", ap=lb0[:], constant=0.0)
    g.lb[0] = lb0
    num = sb("lb_num", [128, 8], F32)
    P.op("dve", "tensor_tensor", out=num[:], in0=ex[:, 8:16], in1=ex[:, 16:24], op=ALU.add)
    P.op("dve", "tensor_tensor", out=num[:], in0=num[:], in1=ex[:, 24:32], op=ALU.add)
    lb3 = sb("lb_l3", [128, 8], F32)
    P.op("dve", "tensor_tensor", out=lb3[:], in0=num[:], in1=rden[:], op=ALU.mult)
    g.lb[3] = lb3
    g.modT = sb("modT", [128, 16], F32)
    g.gsT = sb("gsT", [128, 8], F32)
    g.gate_row = sb("gate_row", [1, D], F32)
    g.gate_b = sb("gate_b", [128, D], F32)
    g.adab_T = sb("adab_T", [128, 24], F32)
    g.ng_T = sb("ng_T", [128, 8], F32)


def prep_layer(g, layer):
    P, dr = g.P, g.dr
    ps = g.psum[0]
    P.op("sp", "dma_start", out=g.rows_a[0:24, :], in_=dr["ada_b"][layer])
    transpose_small(g, g.adab_T[:], g.rows_a[0:24, :], 24, g.psum[1])
    P.op("sp", "dma_start", out=g.rows_g[:], in_=dr["norm_g"][layer])
    transpose_small(g, g.ng_T[:], g.rows_g[:], 8, g.psum[2])
    aw = dr["ada_w"][layer].rearrange("(k p) n -> p k n", p=128)
    for j in range(16):
        t = g.stage[j % 2][:, 0:1024].rearrange("p (k n) -> p k n", k=KT)
        P.op("sp", "dma_start", out=t, in_=aw[:, :, j * 128:(j + 1) * 128])
        for k in range(KT):
            P.op("pe", "matmul", out=ps[:, j:j + 1], lhsT=t[:, k, :], rhs=g.c_col[:, k:k + 1],
                 start=(k == 0), stop=(k == KT - 1))
    P.op("dve", "tensor_tensor", out=g.modT[:], in0=ps[:, 0:16], in1=g.adab_T[:, 0:16], op=ALU.add)
    P.op("dve", "scalar_tensor_tensor", out=g.gsT[:], in0=g.modT[:, 8:16], scalar=1.0, in1=g.ng_T[:],
         op0=ALU.add, op1=ALU.mult)
    P.op("sp", "dma_start", out=g.gate_row[:], in_=dr["ada_b_row"][layer][:, 2 * D:3 * D])
    for cg in range(8):
        psg = g.psum[3 + cg % 2]
        t = g.stage[cg % 2][:, 0:1024].rearrange("p (k n) -> p k n", k=KT)
        P.op("sp", "dma_start", out=t, in_=aw[:, :, 2 * D + cg * 128:2 * D + (cg + 1) * 128])
        for k in range(KT):
            P.op("pe", "matmul", out=psg[0:1, 0:128], lhsT=g.c_col[:, k:k + 1], rhs=t[:, k, :],
                 start=(k == 0), stop=(k == KT - 1))
        P.op("dve", "tensor_tensor", out=g.gate_row[0:1, cg * 128:(cg + 1) * 128], in0=psg[0:1, 0:128],
             in1=g.gate_row[0:1, cg * 128:(cg + 1) * 128], op=ALU.add)
    for cg in range(2):
        psg = g.psum[5 + cg]
        P.op("pe", "matmul", out=psg[:], lhsT=g.c["ones_f"][0:1, :], rhs=g.gate_row[0:1, cg * 512:(cg + 1) * 512],
             start=True, stop=True)
        P.op("dve", "tensor_copy", out=g.gate_b[:, cg * 512:(cg + 1) * 512], in_=psg[:])


def load_weight_bf16(g, dst3, wsrc, ncols, colscale=None, kbase=0):
    P = g.P
    nk = wsrc.shape[0] // 128
    i = 0
    for k in range(nk):
        for c0 in range(0, ncols, SW):
            w = min(SW, ncols - c0)
            stg = g.stage[i % 2]
            P.op("sp", "dma_start", out=stg[:, 0:w], in_=wsrc[k * 128:(k + 1) * 128, c0:c0 + w])
            eng = ("dve", "pool")[i % 2]
            dstv = W(dst3[:, k, c0:c0 + w], kbase + k)
            if colscale is None:
                P.op(eng, "tensor_copy", out=dstv, in_=stg[:, 0:w])
            else:
                P.op(eng, "tensor_tensor", out=dstv, in0=stg[:, 0:w], in1=colscale[:, c0:c0 + w], op=ALU.mult)
            i += 1


def prologue(g, mt, src, srctag, hT, xb, xnb, ss, rstd, mtl=MT):
    P = g.P
    P.op("pool", "memset", ap=ss[:], constant=0.0)
    for blk in range(mtl // 128):
        t0 = mt * mtl + blk * 128
        x = xb[blk % len(xb)]
        xn = xnb[blk % len(xnb)]
        P.op("sp", "dma_start", out=x[:], in_=srctag(src[t0:t0 + 128, :], t0 // 128))
        P.op("act", "activation", out=xn[:], in_=x[:], func=AF.Square, accum_out=ss[:, blk:blk + 1])
        P.op("dve", "tensor_scalar", out=rstd[:, blk:blk + 1], in0=ss[:, blk:blk + 1], scalar1=1.0 / D, scalar2=EPS,
             op0=ALU.mult, op1=ALU.add)
        P.op("act", "activation", out=rstd[:, blk:blk + 1], in_=rstd[:, blk:blk + 1], func=AF.Sqrt)
        P.op("dve", "reciprocal", out=rstd[:, blk:blk + 1], in_=rstd[:, blk:blk + 1])
        P.op("dve", "tensor_scalar", out=xn[:], in0=x[:], scalar1=rstd[:, blk:blk + 1], scalar2=None, op0=ALU.mult)
        for kg in range(2):
            ps = g.psum[kg]
            psb = ps[:].bitcast(BF16)
            for kk in range(4):
                k = kg * 4 + kk
                P.op("pe", "transpose", out=psb[:, kk * 128:(kk + 1) * 128], in_=xn[:, k * 128:(k + 1) * 128],
                     identity=g.c["ident_b"][:])
            for kk in range(4):
                k = kg * 4 + kk
                dstv = Sub(hT[:, k, blk * 128:(blk + 1) * 128], k)
                srcv = psb[:, kk * 128:(kk + 1) * 128]
                if k % 2 == 0:
                    P.op("dve", "tensor_scalar", out=dstv, in0=srcv, scalar1=g.gsT[:, k:k + 1],
                         scalar2=g.modT[:, k:k + 1], op0=ALU.mult, op1=ALU.add)
                else:
                    P.op("act", "activation", out=dstv, in_=srcv, func=AF.Identity, scale=g.gsT[:, k:k + 1],
                         bias=g.modT[:, k:k + 1])


def epilogue(g, mt, src, srctag, dst, og, nk, w_out, xb, psa, psb_, ogtag=True, mtl=MT):
    P = g.P
    for blk in range(mtl // 128):
        t0 = mt * mtl + blk * 128
        x = xb[blk % len(xb)]
        P.op("sp", "dma_start", out=x[:], in_=srctag(src[t0:t0 + 128, :], t0 // 128))
        for cg in range(2):
            pst = (psa, psb_)[cg]
            for k in range(nk):
                ogv = og[:, k, blk * 128:(blk + 1) * 128]
                P.op("pe", "matmul", out=pst[:], lhsT=Sub(ogv, k) if ogtag else ogv,
                     rhs=W(w_out[:, k, cg * 512:(cg + 1) * 512], 100 + k), start=(k == 0), stop=(k == nk - 1))
            P.op("dve", "tensor_tensor", out=x[:, cg * 512:(cg + 1) * 512], in0=pst[:], in1=x[:, cg * 512:(cg + 1) * 512],
                 op=ALU.add)
        P.op("sp", "dma_start", out=Sub(dst[t0:t0 + 128, :], t0 // 128), in_=x[:])


def hgrn2_layer(g, layer, src, srctag, dst):
    P, sb, dr, nc = g.P, g.sb, g.dr, g.nc
    j = layer // 3
    T = g.T
    CH = 32 if layer == 0 else 64
    NCH = MT // CH
    P.mark("prep_layer")
    prep_layer(g, layer)
    P.mark("after_prep_layer")
    L = "L%d_" % layer
    g.warena = sb(L + "warena", [128, KT * 5120], BF16)
    w_in = g.warena[:, 0:KT * 4096].rearrange("p (k n) -> p k n", k=KT)
    w_out = g.warena[:, KT * 4096:KT * 4096 + KT * 1024].rearrange("p (k n) -> p k n", k=KT)
    load_weight_bf16(g, w_in, dr["hg_in_w"][j], 4096)
    load_weight_bf16(g, w_out, dr["hg_out_w"][j], 1024, colscale=g.gate_b, kbase=100)
    lb = g.lb[layer]
    oml = sb(L + "oml", [128, 8])
    noml = sb(L + "noml", [128, 8])
    P.op("dve", "tensor_scalar", out=oml[:], in0=lb[:], scalar1=-1.0, scalar2=1.0, op0=ALU.mult, op1=ALU.add)
    P.op("dve", "tensor_scalar", out=noml[:], in0=oml[:], scalar1=-1.0, scalar2=None, op0=ALU.mult)
    onr = sb(L + "onr", [1, 128])
    ong = sb(L + "ong", [128, 1])
    P.op("sp", "dma_start", out=onr[:], in_=dr["hg_onorm"][j])
    transpose_small(g, ong[:], onr[0:1, :], 1, g.psum[7])
    ones_m = sb(L + "ones_m", [128, 128])
    epsc = sb(L + "epsc", [128, 1])
    P.op("dve", "memset", ap=epsc[:], constant=EPS)
    P.op("dve", "tensor_scalar", out=ones_m[:], in0=g.c["ones_f"][:], scalar1=1.0 / 128, scalar2=None, op0=ALU.mult)

    xb = [sb(L + "xb%d" % i, [128, D]) for i in range(2)]
    xnb = [sb(L + "xn%d" % i, [128, D], BF16) for i in range(2)]
    ss = sb(L + "ss", [128, 4])
    rstd = sb(L + "rstd", [128, 4])
    hT = sb(L + "hT", [128, KT, MT], BF16)
    t_q = sb(L + "t_q", [128, MT])
    t_f = sb(L + "t_f", [128, MT])
    t_k = sb(L + "t_k", [128, MT])
    t_l = sb(L + "t_l", [128, MT])
    t_b = sb(L + "t_b", [128, MT])
    t_bm = sb(L + "t_bm", [128, MT])
    t_e1 = sb(L + "t_e1", [128, MT])
    t_e2 = sb(L + "t_e2", [128, MT])
    qT = [sb(L + "qT%d" % i, [128, MT], BF16) for i in range(2)]
    kT = [sb(L + "kT%d" % i, [128, MT], BF16) for i in range(2)]
    ktok = [sb(L + "ktok%d" % i, [CH, NCH, 128], BF16) for i in range(2)]
    vtok = [sb(L + "vtok%d" % i, [CH, NCH, 128], BF16) for i in range(2)]
    vT = sb(L + "vT", [128, MT], BF16)
    zs = [sb(L + "zs%d" % i, [128, MT], BF16) for i in range(2)]
    er = sb(L + "er", [128, NCH])
    ebl = sb(L + "ebl", [128, NCH])
    eblr = sb(L + "eblr", [128, NCH])
    atm = [sb(L + "atm%d" % i, [CH, MT], BF16) for i in range(2)]
    for i in range(2):
        P.op("pool", "memset", ap=atm[i][:], constant=0.0)
    S = [sb(L + "S%d" % h, [128, 128]) for h in range(8)]
    srb = [sb(L + "srb%d" % i, [128, 128], BF16) for i in range(2)]
    stmp = [sb(L + "stmp%d" % i, [128, 128]) for i in range(2)]
    osq = sb(L + "osq", [128, MT])
    orstd = sb(L + "orstd", [128, MT])
    otmp = sb(L + "otmp", [128, MT])
    og = sb(L + "og", [128, 8, MT], BF16)
    for h in range(8):
        P.op("pool", "memset", ap=S[h][:], constant=0.0)

    ps = g.psum
    PS_Q, PS_F, PS_Z, PS_V, PS_A, PS_O, PS_S, PS_X = range(8)
    PS_O2, PS_S2 = PS_Q, PS_F
    khT = [sb(L + "khT%d" % i, [128, MT], BF16) for i in range(2)]
    er2 = [er, sb(L + "er_b", [128, NCH])]
    ebl2 = [ebl, sb(L + "ebl_b", [128, NCH])]
    srb2 = [srb, [sb(L + "srbB%d" % i, [128, 128], BF16) for i in range(2)]]
    P.mark("after_wload")
    RI = CH // 2 - 1
    for mt in range(T // MT):
        prologue(g, mt, src, srctag, hT, xb, xnb, ss, rstd)
        P.mark("after_prologue%d" % mt)
        for hp in range(4):
            for pp in range(2):
                h = 2 * hp + pp
                P.mark("head%d_%d" % (mt, h))
                for (pst, c0) in ((PS_Q, h * 128), (PS_F, 1024 + h * 128), (PS_V, 2048 + h * 128), (PS_Z, 3072 + h * 128)):
                    for k in range(KT):
                        P.op("pe", "matmul", out=ps[pst][:], lhsT=W(w_in[:, k, c0:c0 + 128], k), rhs=Sub(hT[:, k, :], k),
                             start=(k == 0), stop=(k == KT - 1))
                P.op("act", "activation", out=t_q[:], in_=ps[PS_Q][:], func=AF.Silu)
                P.op("act", "activation", out=t_f[:], in_=ps[PS_F][:], func=AF.Sigmoid)
                P.op("act", "activation", out=zs[pp][:], in_=ps[PS_Z][:], func=AF.Silu)
                P.op("act", "activation", out=vT[:], in_=ps[PS_V][:], func=AF.Copy)
                P.op("dve", "tensor_scalar", out=t_k[:], in0=t_f[:], scalar1=noml[:, h:h + 1], scalar2=oml[:, h:h + 1],
                     op0=ALU.mult, op1=ALU.add)
                P.op("act", "activation", out=t_l[:], in_=t_f[:], func=AF.Ln, scale=oml[:, h:h + 1], bias=lb[:, h:h + 1])
                P.op("dve", "tensor_tensor_scan", out=t_b[:], data0=g.c["resetmask%d" % CH][:], data1=t_l[:], initial=0.0,
                     op0=ALU.mult, op1=ALU.add)
                b3 = t_b[:].rearrange("p (c j) -> p c j", j=CH)
                bm3 = t_bm[:].rearrange("p (c j) -> p c j", j=CH)
                P.op("dve", "tensor_tensor", out=bm3, in0=b3, in1=b3[:, :, RI:RI + 1].to_broadcast([128, NCH, CH]),
                     op=ALU.subtract)
                P.op("act", "activation", out=t_e1[:], in_=t_bm[:], func=AF.Exp)
                P.op("act", "activation", out=t_e2[:], in_=t_bm[:], func=AF.Exp, scale=-1.0)
                P.op("act", "activation", out=er2[pp][:], in_=b3[:, :, RI], func=AF.Exp)
                P.op("act", "activation", out=ebl2[pp][:], in_=b3[:, :, CH - 1], func=AF.Exp)
                P.op("pool", "tensor_tensor", out=qT[pp][:], in0=t_q[:], in1=t_e1[:], op=ALU.mult)
                P.op("dve", "tensor_tensor", out=kT[pp][:], in0=t_k[:], in1=t_e2[:], op=ALU.mult)
                P.op("dve", "tensor_tensor", out=bm3, in0=b3[:, :, CH - 1:CH].to_broadcast([128, NCH, CH]), in1=b3,
                     op=ALU.subtract)
                P.op("act", "activation", out=t_e1[:], in_=t_bm[:], func=AF.Exp)
                P.op("dve", "tensor_tensor", out=khT[pp][:], in0=t_k[:], in1=t_e1[:], op=ALU.mult)
                psb = ps[PS_X][:].bitcast(BF16)
                for (srcT, dstk) in ((khT[pp], ktok[pp]), (vT, vtok[pp])):
                    for c0 in range(0, NCH, 8):
                        for c in range(c0, c0 + 8):
                            P.op("pe", "transpose", out=psb[0:CH, (c - c0) * 128:(c - c0 + 1) * 128],
                                 in_=srcT[:, c * CH:(c + 1) * CH], identity=g.c["ident_b"][:])
                        P.op("act", "activation", out=dstk[:, c0:c0 + 8, :].rearrange("p a b -> p (a b)"), in_=psb[0:CH, :],
                             func=AF.Copy)
                a = atm[pp]
                for c in range(NCH):
                    tcs = slice(c * CH, (c + 1) * CH)
                    P.op("pe", "matmul", out=ps[PS_A][0:CH, tcs], lhsT=kT[pp][:, tcs], rhs=qT[pp][:, tcs], start=True, stop=True)
                P.op("dve", "copy_predicated", out=a[:], mask=g.c["hg_maskP%d" % CH][:], data=ps[PS_A][0:CH, :])
            P.mark("scan%d_%d" % (mt, hp))
            for c in range(NCH):
                tcs = slice(c * CH, (c + 1) * CH)
                for pp in range(2):
                    h = 2 * hp + pp
                    pso = ps[(PS_O, PS_O2)[pp]]
                    pss = ps[(PS_S, PS_S2)[pp]]
                    sr = srb2[pp][c % 2]
                    P.op("act", "activation", out=sr[:], in_=S[h][:], func=AF.Copy, scale=er2[pp][:, c:c + 1])
                    P.op("pe", "matmul", out=pso[:, tcs], lhsT=sr[:], rhs=qT[pp][:, tcs], start=True, stop=False)
                    P.op("pe", "matmul", out=pso[:, tcs], lhsT=vtok[pp][:, c, :], rhs=atm[pp][:, tcs], start=False, stop=True)
                    P.op("pe", "matmul", out=pss[:, 0:128], lhsT=ktok[pp][:, c, :], rhs=vtok[pp][:, c, :],
                         start=True, stop=True)
                    P.op("dve", "scalar_tensor_tensor", out=S[h][:], in0=S[h][:], scalar=ebl2[pp][:, c:c + 1],
                         in1=pss[:, 0:128], op0=ALU.mult, op1=ALU.add)
            for pp in range(2):
                h = 2 * hp + pp
                pso = ps[(PS_O, PS_O2)[pp]]
                P.op("act", "activation", out=osq[:], in_=pso[:], func=AF.Square)
                P.op("pe", "matmul", out=ps[PS_A][:], lhsT=ones_m[:], rhs=osq[:], start=True, stop=True)
                P.op("act", "activation", out=orstd[:], in_=ps[PS_A][:], func=AF.Sqrt, bias=epsc[:, 0:1])
                P.op("dve", "reciprocal", out=orstd[:], in_=orstd[:])
                P.op("dve", "scalar_tensor_tensor", out=otmp[:], in0=pso[:], scalar=ong[:, 0:1], in1=orstd[:],
                     op0=ALU.mult, op1=ALU.mult)
                P.op("pool", "tensor_tensor", out=Sub(og[:, h, :], h), in0=otmp[:], in1=zs[pp][:], op=ALU.mult)
        P.mark("epilogue%d" % mt)
        epilogue(g, mt, src, srctag, dst, og, KT, w_out, xb, ps[PS_V], ps[PS_Z])


def swa_layer(g, layer, src, srctag, dst):
    import math
    P, sb, dr, nc = g.P, g.sb, g.dr, g.nc
    T = g.T
    P.mark("prep_layer")
    prep_layer(g, layer)
    L = "L%d_" % layer
    NB = MT // 128
    g.warena = sb(L + "warena", [128, KT * 4096], BF16)
    w_in = g.warena[:, 0:KT * 3072].rearrange("p (k n) -> p k n", k=KT)
    w_out = g.warena[:, KT * 3072:KT * 3072 + KT * 1024].rearrange("p (k n) -> p k n", k=KT)
    WQ, WK, WV, WZ = 0, 1024, 1536, 2048
    win = dr["sw_in_w"][0]
    i = 0
    for k in range(KT):
        for (c0, w, dup, d0) in ((0, 1024, False, WQ), (1024, 256, True, WK), (1280, 256, True, WV), (1536, 1024, False, WZ)):
            stg = g.stage[i % 2]
            eng = ("dve", "pool")[i % 2]
            i += 1
            P.op("sp", "dma_start", out=stg[:, 0:w], in_=win[k * 128:(k + 1) * 128, c0:c0 + w])
            if not dup:
                P.op(eng, "tensor_copy", out=W(w_in[:, k, d0:d0 + w], k), in_=stg[:, 0:w])
            else:
                dv = w_in[:, k, d0:d0 + 512].rearrange("p (h r d) -> p h r d", h=4, r=2)
                sv = stg[:, 0:256].rearrange("p (h d) -> p h d", h=4)
                for r in range(2):
                    P.op(eng, "tensor_copy", out=W(dv[:, :, r, :], k), in_=sv)
    load_weight_bf16(g, w_out, dr["sw_out_w"][0], 1024, colscale=g.gate_b, kbase=100)

    gq = sb(L + "gq", [128, 1])
    gk = sb(L + "gk", [128, 1])
    for (dst_, nm) in ((gq, "sw_qnorm"), (gk, "sw_knorm")):
        r1 = sb(L + nm + "_r", [1, 128])
        P.op("sp", "dma_start", out=r1[0:1, 0:64], in_=dr[nm][0:1, :])
        P.op("sp", "dma_start", out=r1[0:1, 64:128], in_=dr[nm][0:1, :])
        transpose_small(g, dst_[:], r1[0:1, :], 1, g.psum[7])
    epsc = sb(L + "epsc", [128, 1])
    P.op("dve", "memset", ap=epsc[:], constant=EPS)
    negpi = sb(L + "negpi", [128, 1])
    P.op("dve", "memset", ap=negpi[:], constant=-math.pi)
    cpi = sb(L + "cpi", [128, 3])
    P.op("dve", "memset", ap=cpi[:, 0:1], constant=math.pi)
    P.op("dve", "memset", ap=cpi[:, 1:2], constant=1.5 * math.pi)
    P.op("dve", "memset", ap=cpi[:, 2:3], constant=2 * math.pi)
    sk_r = sb(L + "sk_r", [1, 16])
    P.op("sp", "dma_start", out=sk_r[:], in_=dr["sw_sinks"][0:1, :])
    P.op("act", "activation", out=sk_r[:], in_=sk_r[:], func=AF.Exp)
    P.op("pe", "matmul", out=g.psum[6][:, 0:16], lhsT=g.c["ones_f"][0:1, :], rhs=sk_r[0:1, :], start=True, stop=True)
    esb = sb(L + "esb", [128, 16])
    P.op("dve", "tensor_copy", out=esb[:], in_=g.psum[6][:, 0:16])
    esb4 = esb[:].rearrange("p (h j s) -> p h j s", h=4, j=2)
    esb2 = sb(L + "esb2", [128, 4, 2])
    P.op("dve", "tensor_copy", out=esb2[0:64, :, :], in_=esb4[0:64, :, :, 0])
    P.op("dve", "tensor_copy", out=esb2[64:128, :, :], in_=esb4[64:128, :, :, 1])
    esf = sb(L + "esf", [128, 4, MT])
    for h in range(4):
        for sc in range(2):
            P.op("dve", "tensor_copy", out=esf[:, h, sc * 256:(sc + 1) * 256].rearrange("p (j q) -> p j q", j=2),
                 in_=esb2[:, h, :].unsqueeze(2).to_broadcast([128, 2, 128]))
    ones_b = sb(L + "ones_b", [128, 128], BF16)
    P.op("dve", "tensor_copy", out=ones_b[:], in_=g.c["ones_f"][:])

    xb = [sb(L + "xb%d" % i, [128, D]) for i in range(2)]
    xnb = [sb(L + "xn%d" % i, [128, D], BF16) for i in range(2)]
    ss = sb(L + "ss", [128, 4])
    rstd = sb(L + "rstd", [128, 4])
    hT = sb(L + "hT", [128, KT, MT], BF16)
    cos2 = sb(L + "cos2", [128, MT])
    sin2 = sb(L + "sin2", [128, MT])
    t_sq = sb(L + "t_sq", [128, MT])
    t_rs = sb(L + "t_rs", [128, MT])
    t_qn = sb(L + "t_qn", [128, MT])
    t_a = sb(L + "t_a", [128, MT])
    t_b = sb(L + "t_b", [128, MT])
    posi, ang, ua = t_a[:].bitcast(I32), t_b, t_sq
    qg = sb(L + "qg", [128, 8, MT], BF16)
    kg = sb(L + "kg", [128, 4, (NB + 1) * 128], BF16)
    vd = sb(L + "vd", [128, NB + 1, 512], BF16)
    zs = sb(L + "zs", [128, 8, MT], BF16)
    pT = [sb(L + "pT%d" % i, [128, MT], BF16) for i in range(2)]
    dtmp, otmp = t_rs, t_qn
    og = sb(L + "og", [128, 8, MT], BF16)
    ps = g.psum
    PS_A, PS_B, PS_N, PS_R, PS_S0, PS_S1, PS_O, PS_D = range(8)

    def normrope(pst, gain, out_ap):
        P.op("act", "activation", out=t_sq[:], in_=pst[:], func=AF.Square)
        P.op("pe", "matmul", out=ps[PS_N][:], lhsT=g.c["sw_ones_blk"][:], rhs=t_sq[:], start=True, stop=True)
        P.op("act", "activation", out=t_rs[:], in_=ps[PS_N][:], func=AF.Sqrt, bias=epsc[:, 0:1])
        P.op("dve", "reciprocal", out=t_rs[:], in_=t_rs[:])
        P.op("dve", "scalar_tensor_tensor", out=t_qn[:], in0=pst[:], scalar=gain[:, 0:1], in1=t_rs[:],
             op0=ALU.mult, op1=ALU.mult)
        P.op("pe", "matmul", out=ps[PS_R][:], lhsT=g.c["sw_rotT"][:], rhs=t_qn[:], start=True, stop=True)
        P.op("pool", "tensor_tensor", out=t_a[:], in0=t_qn[:], in1=cos2[:], op=ALU.mult)
        P.op("dve", "tensor_tensor", out=t_b[:], in0=ps[PS_R][:], in1=sin2[:], op=ALU.mult)
        P.op("pool", "tensor_tensor", out=out_ap, in0=t_a[:], in1=t_b[:], op=ALU.add)

    for mt in range(T // MT):
        prologue(g, mt, src, srctag, hT, xb, xnb, ss, rstd)
        P.mark("sw_rope%d" % mt)
        P.op("sp", "dma_start", out=posi, in_=dr["positions"][0:1, mt * MT:(mt + 1) * MT].partition_broadcast(128))
        P.op("dve", "tensor_copy", out=ang[:], in_=posi)
        P.op("dve", "tensor_scalar", out=ang[:], in0=ang[:], scalar1=g.c["sw_invfreq"][:, 0:1], scalar2=None, op0=ALU.mult)
        C1 = 6.28125
        C2 = 2 * math.pi - C1
        P.op("dve", "tensor_scalar", out=ua[:], in0=ang[:], scalar1=1.0 / (2 * math.pi), scalar2=None, op0=ALU.mult)
        P.op("dve", "tensor_copy", out=posi, in_=ua[:])
        P.op("dve", "tensor_copy", out=ua[:], in_=posi)
        P.op("dve", "scalar_tensor_tensor", out=ang[:], in0=ua[:], scalar=-C1, in1=ang[:], op0=ALU.mult, op1=ALU.add)
        P.op("dve", "scalar_tensor_tensor", out=ang[:], in0=ua[:], scalar=-C2, in1=ang[:], op0=ALU.mult, op1=ALU.add)
        P.op("dve", "tensor_scalar", out=ua[:], in0=ang[:], scalar1=math.pi, scalar2=None, op0=ALU.is_gt)
        P.op("dve", "scalar_tensor_tensor", out=ang[:], in0=ua[:], scalar=-2 * math.pi, in1=ang[:], op0=ALU.mult, op1=ALU.add)
        P.op("dve", "tensor_scalar", out=ang[:], in0=ang[:], scalar1=math.pi, scalar2=-math.pi, op0=ALU.min, op1=ALU.max)
        P.op("act", "activation", out=sin2[:], in_=ang[:], func=AF.Sin)
        P.op("act", "activation", out=ua[:], in_=ang[:], func=AF.Sin, scale=0.5)
        P.op("dve", "tensor_tensor", out=ua[:], in0=ua[:], in1=ua[:], op=ALU.mult)
        P.op("dve", "tensor_scalar", out=cos2[:], in0=ua[:], scalar1=-2.0, scalar2=1.0, op0=ALU.mult, op1=ALU.add)
        P.mark("sw_proj%d" % mt)
        for p in range(8):
            pst = ps[PS_A + p % 2]
            for k in range(KT):
                P.op("pe", "matmul", out=pst[:], lhsT=W(w_in[:, k, WQ + p * 128:WQ + (p + 1) * 128], k), rhs=Sub(hT[:, k, :], k),
                     start=(k == 0), stop=(k == KT - 1))
            normrope(pst, gq, Sub(qg[:, p, :], p))
        for h in range(4):
            pst = ps[PS_A + h % 2]
            for k in range(KT):
                P.op("pe", "matmul", out=pst[:], lhsT=W(w_in[:, k, WK + h * 128:WK + (h + 1) * 128], k), rhs=Sub(hT[:, k, :], k),
                     start=(k == 0), stop=(k == KT - 1))
            normrope(pst, gk, Sub(kg[:, h, 128:128 + MT], ("k", h)))
        for blk in range(NB):
            pst = ps[PS_A + blk % 2]
            for k in range(KT):
                P.op("pe", "matmul", out=pst[:], lhsT=Sub(hT[:, k, blk * 128:(blk + 1) * 128], k),
                     rhs=W(w_in[:, k, WV:WV + 512], k), start=(k == 0), stop=(k == KT - 1))
            P.op("act", "activation", out=Sub(vd[:, 1 + blk, :], 1 + blk), in_=pst[:], func=AF.Copy)
        for p in range(8):
            pst = ps[PS_A + p % 2]
            for k in range(KT):
                P.op("pe", "matmul", out=pst[:], lhsT=W(w_in[:, k, WZ + p * 128:WZ + (p + 1) * 128], k), rhs=Sub(hT[:, k, :], k),
                     start=(k == 0), stop=(k == KT - 1))
            P.op("act", "activation", out=Sub(zs[:, p, :], p), in_=pst[:], func=AF.Silu)
        P.mark("sw_attn%d" % mt)
        for qb in range(NB):
            qcs = slice(qb * 128, (qb + 1) * 128)
            first = (mt == 0 and qb == 0)
            kbs = ([] if first else [0]) + [1]
            for h in range(4):
                for kb in kbs:
                    kblk = qb + kb
                    kcs = slice(kblk * 128, (kblk + 1) * 128)
                    pt = pT[kb]
                    for sl in range(2):
                        pss = ps[(PS_S0, PS_N)[sl] + kb]
                        rs = slice(sl * 64, (sl + 1) * 64)
                        P.op("pe", "matmul", out=pss[:, 0:256],
                             lhsT=Sub(kg[rs, h, kcs], ("k", h)) if kblk > 0 else Sub(kg[rs, h, kcs], ("kh", h)),
                             rhs=qg[rs, 2 * h:2 * h + 2, qcs], start=True, stop=True)
                        P.op("act", "activation", out=pt[:, sl * 256:(sl + 1) * 256], in_=pss[:, 0:256], func=AF.Exp, scale=0.125)
                    P.op("pool", "tensor_tensor", out=pt[:], in0=pt[:], in1=g.c["sw_mask_prev" if kb == 0 else "sw_mask_cur"][:],
                         op=ALU.mult)
                for ii, kb in enumerate(kbs):
                    kblk = qb + kb
                    vsub = Sub(vd[:, kblk, h * 128:(h + 1) * 128], kblk)
                    P.op("pe", "matmul", out=ps[PS_O][:], lhsT=vsub, rhs=pT[kb][:], start=(ii == 0), stop=(ii == len(kbs) - 1))
                for ii, kb in enumerate(kbs):
                    P.op("pe", "matmul", out=ps[PS_D][:], lhsT=ones_b[:], rhs=pT[kb][:], start=(ii == 0), stop=(ii == len(kbs) - 1))
                P.op("dve", "tensor_tensor", out=dtmp[:], in0=ps[PS_D][:], in1=esf[:, h, :], op=ALU.add)
                P.op("dve", "reciprocal", out=dtmp[:], in_=dtmp[:])
                P.op("dve", "tensor_tensor", out=otmp[:], in0=ps[PS_O][:], in1=dtmp[:], op=ALU.mult)
                for sl in range(2):
                    rs = slice(sl * 64, (sl + 1) * 64)
                    P.op("pool", "tensor_tensor", out=og[rs, 2 * h:2 * h + 2, qcs],
                         in0=otmp[rs, sl * 256:(sl + 1) * 256].rearrange("p (j q) -> p j q", j=2),
                         in1=zs[rs, 2 * h:2 * h + 2, qcs], op=ALU.mult)
        for h in range(4):
            P.op("pool", "tensor_copy", out=Sub(kg[:, h, 0:128], ("kh", h)), in_=Sub(kg[:, h, NB * 128:(NB + 1) * 128], ("k", h)))
        P.op("pool", "tensor_copy", out=Sub(vd[:, 0, :], 0), in_=Sub(vd[:, NB, :], NB))
        P.mark("epilogue%d" % mt)
        epilogue(g, mt, src, srctag, dst, og, KT, w_out, xb, ps[PS_A], ps[PS_B], ogtag=False)


def gdn_layer(g, layer, src, srctag, dst):
    P, sb, dr, nc = g.P, g.sb, g.dr, g.nc
    T = g.T
    GT = 256
    NC = GT // 64
    P.mark("prep_layer")
    prep_layer(g, layer)
    L = "L%d_" % layer
    NRES = 2080
    g.warena = sb(L + "warena", [128, KT * NRES + 16 * 1024], BF16)
    w_in = g.warena[:, 0:KT * NRES].rearrange("p (k n) -> p k n", k=KT)
    w_out = g.warena[:, KT * NRES:KT * NRES + 16 * 1024].rearrange("p (k n) -> p k n", k=16)
    gin = dr["gd_in_w"][0]
    load_weight_bf16(g, w_in[:, :, 0:2048], gin[:, 0:2048], 2048)
    load_weight_bf16(g, w_in[:, :, 2048:2080], gin[:, 6144:6176], 32)
    load_weight_bf16(g, w_out, dr["gd_out_w"][0], 1024, colscale=g.gate_b, kbase=100)
    wscr = g.gd_wscr
    cbf = [sb(L + "cbf%d" % i, [128, SW], BF16) for i in range(2)]
    i = 0
    for k in range(KT):
        for ch in range(4):
            stg = g.stage[i % 2]
            cb = cbf[i % 2]
            P.op("sp", "dma_start", out=stg[:, 0:1024], in_=gin[k * 128:(k + 1) * 128, 2048 + ch * 1024:2048 + (ch + 1) * 1024])
            P.op(("dve", "pool")[i % 2], "tensor_copy", out=cb[:], in_=stg[:, 0:1024])
            P.op("sp", "dma_start", out=wscr[ch * 8:(ch + 1) * 8, :, k * 128:(k + 1) * 128].rearrange("h p n -> p h n"),
                 in_=cb[:].rearrange("p (h n) -> p h n", h=8))
            i += 1
    P.barrier()
    wbuf = [sb(L + "wbuf%d" % i, [128, 1024], BF16) for i in range(2)]
    g.wcnt = 0

    def stream_w(hd):
        wb = wbuf[g.wcnt % 2]
        g.wcnt += 1
        P.op("sp", "dma_start", out=wb[:], in_=wscr[hd])
        return wb
    ps = g.psum
    PA, PB, PX, PQ, PD, PM, PN, PT = range(8)
    PO = PD
    cwr = sb(L + "cwr", [4, 4096])
    P.op("sp", "dma_start", out=cwr[:], in_=dr["gd_conv_w"][:, :])
    for ct in range(32):
        P.op("pe", "transpose", out=ps[PA][:, ct * 4:(ct + 1) * 4], in_=cwr[0:4, ct * 128:(ct + 1) * 128],
             identity=g.c["ident_f"][0:4, 0:4])
    cw = sb(L + "cw", [128, 128])
    P.op("dve", "tensor_copy", out=cw[:], in_=ps[PA][:, 0:128])
    alog = sb(L + "alog", [16, 1])
    dtb = sb(L + "dtb", [16, 1])
    P.op("sp", "dma_start", out=alog[:], in_=dr["gd_a_log"][:, :])
    P.op("sp", "dma_start", out=dtb[:], in_=dr["gd_dt_bias"][:, :])
    nA = sb(L + "nA", [16, 1])
    P.op("act", "activation", out=nA[:], in_=alog[:], func=AF.Exp)
    P.op("dve", "tensor_scalar", out=nA[:], in0=nA[:], scalar1=-1.0, scalar2=None, op0=ALU.mult)
    one16 = sb(L + "one16", [16, 1])
    P.op("dve", "memset", ap=one16[:], constant=1.0)
    onr = sb(L + "onr", [1, 128])
    ong = sb(L + "ong", [128, 1])
    P.op("sp", "dma_start", out=onr[:], in_=dr["gd_onorm"][:, :])
    transpose_small(g, ong[:], onr[0:1, :], 1, ps[PB])
    ones_m = sb(L + "ones_m", [128, 128])
    P.op("dve", "tensor_scalar", out=ones_m[:], in0=g.c["ones_f"][:], scalar1=1.0 / 128, scalar2=None, op0=ALU.mult)
    epsc = sb(L + "epsc", [128, 1])
    P.op("dve", "memset", ap=epsc[:], constant=EPS)
    identb64 = g.c["ident_b"]

    xb = [sb(L + "xb0", [128, D])]
    xnb = [sb(L + "xn0", [128, D], BF16)]
    ss = sb(L + "ss", [128, 4])
    rstd = sb(L + "rstd", [128, 4])
    hT = sb(L + "hT", [128, KT, GT], BF16)
    halo = sb(L + "halo", [128, 32, 3])
    P.op("pool", "memset", ap=halo[:], constant=0.0)
    xc = sb(L + "xc", [128, 3 + GT])
    acc = sb(L + "acc", [128, GT])
    f_a = sb(L + "f_a", [128, GT])
    f_b = sb(L + "f_b", [128, GT])
    f_c = sb(L + "f_c", [128, GT])
    qT = sb(L + "qT", [128, GT], BF16)
    kT = sb(L + "kT", [128, GT], BF16)
    kbT = sb(L + "kbT", [128, GT], BF16)
    qd = [sb(L + "qd%d" % i, [128, GT], BF16) for i in range(2)]
    vT = sb(L + "vT", [128, GT], BF16)
    zs = [sb(L + "zs%d" % i, [128, GT], BF16) for i in range(2)]
    edl4 = [sb(L + "edl4_%d" % i, [128, NC]) for i in range(2)]
    r_beta = sb(L + "r_beta", [16, GT])
    r_g = sb(L + "r_g", [16, GT])
    r_d = sb(L + "r_d", [16, GT])
    r_be = sb(L + "r_be", [16, GT])
    r_edl = sb(L + "r_edl", [16, GT])
    NP = GT // 128
    tkp_b = sb(L + "tkp_b", [128, 16, NP])
    tkp_be = sb(L + "tkp_be", [128, 16, NP])
    tkp_d = sb(L + "tkp_d", [128, 16, NP])
    tk_edl = sb(L + "tk_edl", [64, 16, NC])
    tk_d = sb(L + "tk_d", [64, 16, NC])
    Et = sb(L + "Et", [64, GT])
    Em = sb(L + "Em", [64, GT])
    Ep = sb(L + "Ep", [128, GT])
    Mk = [sb(L + "Mk%d" % i, [128, 512]) for i in range(2)]
    Nk = [sb(L + "Nk%d" % i, [128, 512]) for i in range(2)]
    Tk = [sb(L + "Tk%d" % i, [128, 512]) for i in range(2)]
    Tbf = sb(L + "Tbf", [128, 512], BF16)
    qkM = sb(L + "qkM", [64, 512], BF16)
    vb = sb(L + "vb", [128, 2 * NP, 128], BF16)
    kbd = sb(L + "kbd", [128, 2 * NP, 128], BF16)
    khat = sb(L + "khat", [64, 8, 128], BF16)
    wTn = sb(L + "wTn", [128, 512], BF16)
    vn2 = [[sb(L + "vn%d_%d" % (hh, i), [64, 128], BF16) for i in range(2)] for hh in range(2)]
    S = [sb(L + "S%d" % h, [128, 128]) for h in range(16)]
    Sb = [sb(L + "Sb%d" % h, [128, 128], BF16) for h in range(16)]
    for h in range(16):
        P.op("pool", "memset", ap=S[h][:], constant=0.0)
        P.op("pool", "memset", ap=Sb[h][:], constant=0.0)
    og = sb(L + "og", [128, 16, GT], BF16)
    identI = sb(L + "identI", [128, 512])
    for b in range(4):
        P.op("dve", "tensor_copy", out=identI[:, b * 128:(b + 1) * 128], in_=g.c["ident_f"][:, :])
    rm16 = g.c["gd_rm"][:, :]
    sel = g.c["gd_sel"]

    def conv_silu(pst, ct, out_ap, func=AF.Silu):
        P.op("pool", "tensor_copy", out=xc[:, 0:3], in_=halo[:, ct, :])
        P.op("act", "activation", out=xc[:, 3:3 + GT], in_=pst[:, 0:GT], func=AF.Copy)
        P.op("pool", "tensor_copy", out=halo[:, ct, :], in_=xc[:, GT:GT + 3])
        P.op("dve", "tensor_scalar", out=acc[:], in0=xc[:, 3:3 + GT], scalar1=cw[:, ct * 4 + 3:ct * 4 + 4], scalar2=None,
             op0=ALU.mult)
        for jt in (2, 1, 0):
            P.op("dve", "scalar_tensor_tensor", out=acc[:], in0=xc[:, jt:jt + GT], scalar=cw[:, ct * 4 + jt:ct * 4 + jt + 1],
                 in1=acc[:], op0=ALU.mult, op1=ALU.add)
        P.op("act", "activation", out=out_ap, in_=acc[:], func=func)

    def inproj(pst, c0, m=128):
        for k in range(KT):
            P.op("pe", "matmul", out=pst, lhsT=W(w_in[:, k, c0:c0 + m], k), rhs=Sub(hT[:, k, :], k),
                 start=(k == 0), stop=(k == KT - 1))

    def inproj_s(pst, hd):
        wb = stream_w(hd)
        for k in range(KT):
            P.op("pe", "matmul", out=pst, lhsT=wb[:, k * 128:(k + 1) * 128], rhs=Sub(hT[:, k, :], k),
                 start=(k == 0), stop=(k == KT - 1))

    def l2norm(src_f, out_bf, scale):
        P.op("pool", "tensor_tensor", out=f_b[:], in0=src_f[:], in1=src_f[:], op=ALU.mult)
        P.op("pe", "matmul", out=ps[PB][:, 0:GT], lhsT=g.c["ones_f"][:], rhs=f_b[:], start=True, stop=True)
        P.op("act", "activation", out=f_c[:], in_=ps[PB][:, 0:GT], func=AF.Sqrt, bias=epsc[:, 0:1])
        P.op("dve", "reciprocal", out=f_c[:], in_=f_c[:])
        P.op("dve", "scalar_tensor_tensor", out=out_bf, in0=src_f[:], scalar=scale, in1=f_c[:], op0=ALU.mult, op1=ALU.mult)

    for mt in range(T // GT):
        prologue(g, mt, src, srctag, hT, xb, xnb, ss, rstd, mtl=GT)
        P.mark("gd_rows%d" % mt)
        inproj(ps[PA][0:16, 0:GT], 2048, 16)
        inproj(ps[PA][0:16, GT:2 * GT], 2064, 16)
        P.op("act", "activation", out=r_beta[:], in_=ps[PA][0:16, GT:2 * GT], func=AF.Sigmoid)
        P.op("act", "activation", out=r_g[:], in_=ps[PA][0:16, 0:GT], func=AF.Exp, bias=dtb[:, 0:1])
        P.op("act", "activation", out=r_g[:], in_=r_g[:], func=AF.Ln, bias=one16[:, 0:1])
        P.op("dve", "tensor_scalar", out=r_g[:], in0=r_g[:], scalar1=nA[:, 0:1], scalar2=None, op0=ALU.mult)
        P.op("dve", "tensor_tensor_scan", out=r_d[:], data0=rm16, data1=r_g[:], initial=0.0, op0=ALU.mult, op1=ALU.add)
        P.op("act", "activation", out=r_be[:], in_=r_d[:], func=AF.Exp)
        P.op("dve", "tensor_tensor", out=r_be[:], in0=r_be[:], in1=r_beta[:], op=ALU.mult)
        d3 = r_d[:].rearrange("p (c j) -> p c j", j=64)
        P.op("dve", "tensor_tensor", out=r_edl[:].rearrange("p (c j) -> p c j", j=64), in0=d3,
             in1=d3[:, :, 63:64].to_broadcast([16, NC, 64]), op=ALU.subtract)
        P.op("act", "activation", out=r_edl[:], in_=r_edl[:], func=AF.Exp, scale=-1.0)
        for (row, tok) in ((r_edl, tk_edl), (r_d, tk_d)):
            for c in range(NC):
                P.op("pe", "transpose", out=ps[PB][0:64, c * 16:(c + 1) * 16], in_=row[0:16, c * 64:(c + 1) * 64],
                     identity=g.c["ident_f"][0:16, 0:16])
            P.op("dve", "tensor_copy", out=tok[:].rearrange("p h c -> p c h"),
                 in_=ps[PB][0:64, 0:NC * 16].rearrange("p (c h) -> p c h", h=16))
        for (row, tok) in ((r_beta, tkp_b), (r_be, tkp_be), (r_d, tkp_d)):
            for pr in range(NP):
                P.op("pe", "transpose", out=ps[PB][:, pr * 16:(pr + 1) * 16], in_=row[0:16, pr * 128:(pr + 1) * 128],
                     identity=g.c["ident_f"][0:16, 0:16])
            P.op("dve", "tensor_copy", out=tok[:].rearrange("p h c -> p c h"),
                 in_=ps[PB][:, 0:NP * 16].rearrange("p (c h) -> p c h", h=16))
        for gq in range(8):
            P.mark("gd_g%d_%d" % (mt, gq))
            inproj(ps[PA][:, 0:GT], gq * 128)
            conv_silu(ps[PA], gq, f_a[:])
            l2norm(f_a, qT[:], 128 ** -0.5)
            inproj(ps[PA][:, 0:GT], 1024 + gq * 128)
            conv_silu(ps[PA], 8 + gq, f_a[:])
            l2norm(f_a, kT[:], 1.0)
            pxb = ps[PX][:].bitcast(BF16)
            for c in range(NC):
                P.op("pe", "transpose", out=pxb[0:64, c * 128:(c + 1) * 128], in_=kT[:, c * 64:(c + 1) * 64],
                     identity=g.c["ident_b"][:])
            for hh in range(2):
                h = 2 * gq + hh
                bs = slice(hh * NC, (hh + 1) * NC)
                P.op("dve", "tensor_tensor", out=khat[:, bs, :],
                     in0=pxb[0:64, 0:NC * 128].rearrange("p (c d) -> p c d", d=128),
                     in1=tk_edl[:, h, :].unsqueeze(2).to_broadcast([64, NC, 128]), op=ALU.mult)
            for pr in range(NP):
                P.op("pe", "transpose", out=pxb[:, pr * 128:(pr + 1) * 128], in_=kT[:, pr * 128:(pr + 1) * 128],
                     identity=g.c["ident_b"][:])
            for hh in range(2):
                h = 2 * gq + hh
                P.op("dve", "tensor_tensor", out=kbd[:, hh * NP:(hh + 1) * NP, :],
                     in0=pxb[:, 0:NP * 128].rearrange("p (c d) -> p c d", d=128),
                     in1=tkp_be[:, h, :].unsqueeze(2).to_broadcast([128, NP, 128]), op=ALU.mult)
            for c in range(NC):
                P.op("pe", "matmul", out=ps[PQ][0:64, c * 64:(c + 1) * 64], lhsT=kT[:, c * 64:(c + 1) * 64],
                     rhs=qT[:, c * 64:(c + 1) * 64], start=True, stop=True)
            for hh in range(2):
                h = 2 * gq + hh
                bs = slice(hh * NC, (hh + 1) * NC)
                cs = slice(hh * GT, (hh + 1) * GT)
                inproj_s(ps[PA][:, 0:GT], h)
                conv_silu(ps[PA], 16 + h, vT[:])
                inproj_s(ps[PB][:, 0:GT], 16 + h)
                P.op("act", "activation", out=zs[hh][:], in_=ps[PB][:, 0:GT], func=AF.Silu)
                for pr in range(NP):
                    P.op("pe", "transpose", out=pxb[:, pr * 128:(pr + 1) * 128], in_=vT[:, pr * 128:(pr + 1) * 128],
                         identity=g.c["ident_b"][:])
                P.op("dve", "tensor_tensor", out=vb[:, hh * NP:(hh + 1) * NP, :],
                     in0=pxb[:, 0:NP * 128].rearrange("p (c d) -> p c d", d=128),
                     in1=tkp_b[:, h, :].unsqueeze(2).to_broadcast([128, NP, 128]), op=ALU.mult)
                P.op("pe", "matmul", out=ps[PD][:, 0:GT], lhsT=sel[0:16, h * 128:(h + 1) * 128], rhs=r_beta[:], start=True, stop=True)
                P.op("pe", "matmul", out=ps[PD][:, GT:2 * GT], lhsT=sel[0:16, h * 128:(h + 1) * 128], rhs=r_d[:], start=True, stop=True)
                P.op("dve", "tensor_tensor", out=kbT[:], in0=kT[:], in1=ps[PD][:, 0:GT], op=ALU.mult)
                P.op("act", "activation", out=f_b[:], in_=ps[PD][:, GT:2 * GT], func=AF.Exp)
                P.op("pool", "tensor_tensor", out=qd[hh][:], in0=qT[:], in1=f_b[:], op=ALU.mult)
                P.op("act", "activation", out=edl4[hh][:], in_=ps[PD][:, GT:2 * GT].rearrange("p (c j) -> p c j", j=64)[:, :, 63],
                     func=AF.Exp)
                P.op("dve", "tensor_tensor", out=Et[:].rearrange("p (c j) -> p c j", j=64),
                     in0=ps[PD][0:64, GT:2 * GT].rearrange("p (c j) -> p c j", j=64),
                     in1=tk_d[:, h, :].unsqueeze(2).to_broadcast([64, NC, 64]), op=ALU.subtract)
                P.op("dve", "tensor_scalar", out=Et[:], in0=Et[:], scalar1=0.0, scalar2=None, op0=ALU.min)
                P.op("act", "activation", out=Et[:], in_=Et[:], func=AF.Exp)
                P.op("dve", "tensor_tensor", out=Ep[:].rearrange("p (c j) -> p c j", j=128),
                     in0=ps[PD][:, GT:2 * GT].rearrange("p (c j) -> p c j", j=128),
                     in1=tkp_d[:, h, :].unsqueeze(2).to_broadcast([128, NP, 128]), op=ALU.subtract)
                P.op("dve", "tensor_scalar", out=Ep[:], in0=Ep[:], scalar1=0.0, scalar2=None, op0=ALU.min)
                P.op("act", "activation", out=Ep[:], in_=Ep[:], func=AF.Exp)
                P.op("pool", "tensor_tensor", out=Ep[:], in0=Ep[:], in1=g.c["gd_maskSP"][:, :], op=ALU.mult)
                for pr in range(NP):
                    P.op("pe", "matmul", out=ps[PM][:, (hh * NP + pr) * 128:(hh * NP + pr + 1) * 128],
                         lhsT=kT[:, pr * 128:(pr + 1) * 128], rhs=kbT[:, pr * 128:(pr + 1) * 128], start=True, stop=True)
                P.op("dve", "scalar_tensor_tensor", out=Mk[0][:, cs], in0=ps[PM][:, cs], scalar=-1.0, in1=Ep[:],
                     op0=ALU.mult, op1=ALU.mult)
                P.op("pool", "tensor_tensor", out=Em[:], in0=Et[:], in1=g.c["gd_maskI"][:, :], op=ALU.mult)
                P.op("dve", "tensor_tensor", out=qkM[:, cs], in0=ps[PQ][0:64, 0:GT], in1=Em[:], op=ALU.mult)
            for b in range(4):
                P.op("pe", "transpose", out=ps[PX][:, b * 128:(b + 1) * 128], in_=Mk[0][:, b * 128:(b + 1) * 128],
                     identity=g.c["ident_f"][:, :])
            P.op("act", "activation", out=Nk[0][:], in_=ps[PX][:, 0:512], func=AF.Copy)
            P.op("pool", "tensor_tensor", out=Tk[0][:], in0=Mk[0][:], in1=identI[:], op=ALU.add)
            for st_ in range(5):
                cur, nxt = st_ % 2, (st_ + 1) % 2
                if st_ < 4:
                    for b in range(4):
                        bc = slice(b * 128, (b + 1) * 128)
                        P.op("pe", "matmul", out=ps[PM][:, bc], lhsT=Nk[cur][:, bc], rhs=Mk[cur][:, bc], start=True, stop=True)
                    P.op("dve", "tensor_copy", out=Mk[nxt][:], in_=ps[PM][:, :])
                for b in range(4):
                    bc = slice(b * 128, (b + 1) * 128)
                    P.op("pe", "matmul", out=ps[PN][:, bc], lhsT=Mk[cur][:, bc], rhs=Nk[cur][:, bc], start=True, stop=True)
                P.op("act", "activation", out=Nk[nxt][:], in_=ps[PN][:, :], func=AF.Copy)
                for b in range(4):
                    bc = slice(b * 128, (b + 1) * 128)
                    P.op("pe", "matmul", out=ps[PT][:, bc], lhsT=Nk[nxt][:, bc], rhs=Tk[cur][:, bc], start=True, stop=True)
                P.op("dve", "tensor_tensor", out=Tk[nxt][:], in0=ps[PT][:, :], in1=Tk[cur][:], op=ALU.add)
            P.op("pool", "tensor_copy", out=Tbf[:], in_=Tk[1][:])
            TT = Tbf
            for b in range(4):
                bc = slice(b * 128, (b + 1) * 128)
                P.op("pe", "matmul", out=ps[PN][:, bc], lhsT=kbd[:, b, :], rhs=TT[:, bc], start=True, stop=True)
            P.op("act", "activation", out=wTn[:], in_=ps[PN][:], func=AF.Copy, scale=-1.0)
            for c in range(NC):
                for hh in range(2):
                    h = 2 * gq + hh
                    b = hh * NC + c
                    bc = slice(b * 64, (b + 1) * 64)
                    v_ = vn2[hh][c % 2]
                    pv = ps[(PT, PX)[hh]]
                    pst_ = ps[(PM, PN)[hh]]
                    P.op("pe", "matmul", out=pv[0:64, 0:128], lhsT=TT[:, bc], rhs=vb[:, hh * NP + c // 2, :], start=True, stop=False)
                    P.op("pe", "matmul", out=pv[0:64, 0:128], lhsT=wTn[:, bc], rhs=Sb[h][:], start=False, stop=True)
                    P.op("act", "activation", out=v_[:], in_=pv[0:64, 0:128], func=AF.Copy)
                    P.op("pe", "matmul", out=ps[PO][:, bc], lhsT=Sb[h][:], rhs=qd[hh][:, c * 64:(c + 1) * 64], start=True, stop=False)
                    P.op("pe", "matmul", out=ps[PO][:, bc], lhsT=v_[:], rhs=qkM[:, bc], start=False, stop=True)
                    P.op("pe", "matmul", out=pst_[:, 0:128], lhsT=khat[:, b, :], rhs=v_[:], start=True, stop=True)
                    P.op("dve", "scalar_tensor_tensor", out=S[h][:], in0=S[h][:], scalar=edl4[hh][:, c:c + 1], in1=pst_[:, 0:128],
                         op0=ALU.mult, op1=ALU.add)
                    P.op("pool", "tensor_copy", out=Sb[h][:], in_=S[h][:])
            for hh in range(2):
                h = 2 * gq + hh
                cs = slice(hh * GT, (hh + 1) * GT)
                P.op("act", "activation", out=f_a[:], in_=ps[PO][:, cs], func=AF.Square)
                P.op("pe", "matmul", out=ps[PB][:, 0:GT], lhsT=ones_m[:], rhs=f_a[:], start=True, stop=True)
                P.op("act", "activation", out=f_c[:], in_=ps[PB][:, 0:GT], func=AF.Sqrt, bias=epsc[:, 0:1])
                P.op("dve", "reciprocal", out=f_c[:], in_=f_c[:])
                P.op("dve", "scalar_tensor_tensor", out=f_b[:], in0=ps[PO][:, cs], scalar=ong[:, 0:1], in1=f_c[:],
                     op0=ALU.mult, op1=ALU.mult)
                P.op("pool", "tensor_tensor", out=Sub(og[:, h, :], h), in0=f_b[:], in1=zs[hh][:], op=ALU.mult)
        P.mark("epilogue%d" % mt)
        epilogue(g, mt, src, srctag, dst, og, 16, w_out, xb, ps[PA], ps[PB], mtl=GT)


def core_inputs(b, T, inputs, consts):
    f = np.float32
    m = {
        "x": np.ascontiguousarray(inputs["x"][b, :T]).astype(f, copy=False),
        "c8": np.ascontiguousarray(inputs["c"][b].reshape(8, 128)),
        "positions": np.ascontiguousarray(inputs["positions"][b, :T].reshape(1, T)).astype(np.int32, copy=False),
        "hgrn_lb": np.ascontiguousarray(inputs["hgrn_lb"].reshape(32, 128)),
        "ada_w": inputs["ada_w"],
        "ada_b": np.ascontiguousarray(inputs["ada_b"].reshape(4, 24, 128)),
        "ada_b_row": np.ascontiguousarray(inputs["ada_b"].reshape(4, 1, 3 * D)),
        "norm_g": np.ascontiguousarray(inputs["norm_g"].reshape(4, 8, 128)),
        "hg_in_w": inputs["hg_in_w"],
        "hg_out_w": inputs["hg_out_w"],
        "hg_onorm": np.ascontiguousarray(inputs["hg_onorm"].reshape(-1, 1, 128)),
        "sw_in_w": inputs["sw_in_w"],
        "sw_out_w": inputs["sw_out_w"],
        "sw_qnorm": inputs["sw_qnorm"],
        "sw_knorm": inputs["sw_knorm"],
        "sw_sinks": inputs["sw_sinks"],
        "gd_in_w": inputs["gd_in_w"],
        "gd_out_w": inputs["gd_out_w"],
        "gd_conv_w": np.ascontiguousarray(inputs["gd_conv_w"][0]),
        "gd_a_log": np.ascontiguousarray(inputs["gd_a_log"].reshape(16, 1)),
        "gd_dt_bias": np.ascontiguousarray(inputs["gd_dt_bias"].reshape(16, 1)),
        "gd_onorm": np.ascontiguousarray(inputs["gd_onorm"].reshape(1, 128)),
    }
    m.update(consts)
    return m


def run_layers(inputs, layers, T, ncores=2, trace=False):
    inputs = {k: np.asarray(v) for k, v in inputs.items()}
    consts = make_consts()
    nc = build(T, layers)
    in_maps = [core_inputs(b, T, inputs, consts) for b in range(ncores)]
    res = run_bass_kernel_spmd(nc, in_maps, core_ids=list(range(ncores)), trace=trace)
    out = np.stack([np.asarray(r["out"]) for r in res.results], axis=0)
    return out, res


def kernel(**inputs):
    T = inputs["x"].shape[1]
    out, _ = run_layers(inputs, [0, 1, 2, 3], T)
    return out.astype(np.float32, copy=False)
```

```python
import numpy as np
import ml_dtypes
from contextlib import ExitStack

import concourse.bass as bass
import concourse.mybir as mybir
from concourse.bass_utils import run_bass_kernel_spmd

F32 = mybir.dt.float32
BF16 = mybir.dt.bfloat16
I32 = mybir.dt.int32
U32 = mybir.dt.uint32
AF = mybir.ActivationFunctionType
ALU = mybir.AluOpType
AX = mybir.AxisListType

D = 1024
KT = 8
EPS = 1e-6
MT = 512
NDMASEM = 40
SW = 1024


class Sub:
    def __init__(self, ap, tag):
        self.ap = ap
        self.tag = tag


class Prog:
    ENGS = ("pe", "act", "dve", "pool", "sp")
    WRITE_KW = ("out", "accum_out", "ap")

    def __init__(self, nc):
        self.nc = nc
        self.ins = []
        self.writers = {}
        self.readers = {}
        self.dma_sem_of = {}
        self.dma_tot = []
        self.same_engine_sync = True
        self._bar_pending = set()
        self._bar_ids = []
        self.marks = {}
        self.psum_excl = True

    def _key(self, v):
        if isinstance(v, Sub):
            return (v.ap.tensor.name, v.tag), v.ap
        return (v.tensor.name, None), v

    def op(self, eng, meth, **kw):
        reads, writes, real = [], [], {}
        for k, v in kw.items():
            if isinstance(v, (Sub, bass.AP)):
                key, ap = self._key(v)
                real[k] = ap
                (writes if k in self.WRITE_KW else reads).append(key)
            else:
                real[k] = v
        i = len(self.ins)
        deps = set()
        if eng in self._bar_pending:
            deps.update(self._bar_ids)
            self._bar_pending.discard(eng)
        for key in reads:
            deps.update(self.writers.get(key, ()))
            if self.psum_excl and key[0].startswith("ps"):
                deps.update(self.readers.get(key, ()))
        for key in writes:
            deps.update(self.writers.get(key, ()))
            deps.update(self.readers.get(key, ()))
        for key in reads:
            self.readers.setdefault(key, []).append(i)
        for key in writes:
            self.writers[key] = [i]
            self.readers[key] = []
        dma = meth == "dma_start"
        rec = dict(eng=eng, meth=meth, kw=real, deps=deps, dma=dma, inc=False, cnt=0, sem=None, tgt=0)
        if dma:
            sbside = None
            for k in ("out", "in_"):
                ap = real[k]
                if "SB" in type(ap.tensor).__name__:
                    sbside = ap.tensor.name
            assert sbside is not None, "dma needs an SBUF side"
            s = self.dma_sem_of.setdefault(sbside, len(self.dma_sem_of))
            if s >= len(self.dma_tot):
                self.dma_tot.append(0)
            self.dma_tot[s] += 16
            rec["sem"] = s
            rec["tgt"] = self.dma_tot[s]
        self.ins.append(rec)
        return i

    def mark(self, name):
        self.marks[name] = len(self.ins)

    def barrier(self):
        last = {}
        lastdma = {}
        for j, r in enumerate(self.ins):
            if r["dma"]:
                lastdma[r["sem"]] = j
            else:
                last[r["eng"]] = j
        self._bar_ids = list(last.values()) + list(lastdma.values())
        self._bar_pending = set(self.ENGS)
        self.dma_sem_of = {}

    def setup(self, stack):
        nc = self.nc
        self.sems = {e: stack.enter_context(nc.semaphore("s_" + e)) for e in self.ENGS}
        self.dsem = [stack.enter_context(nc.semaphore("d_%d" % k)) for k in range(NDMASEM)]
        self.waited = {e: {p: -1 for p in self.ENGS} for e in self.ENGS}
        self.dma_waited = {e: set() for e in self.ENGS}
        self.cnt = {e: 0 for e in self.ENGS}
        self.done = 0

    def emit(self, final=False):
        nc = self.nc
        import os as _os
        ins = self.ins
        lo, hi = self.done, len(ins)
        stop = _os.environ.get("KSTOP")
        if stop:
            n = int(stop) if stop.isdigit() else self.marks.get(stop, hi)
            hi = max(lo, min(hi, n))
        assert len(self.dma_tot) <= NDMASEM, len(self.dma_tot)
        waited, dma_waited = self.waited, self.dma_waited
        for i in range(lo, hi):
            r = ins[i]
            e = r["eng"]
            need = {}
            dm = []
            for j in r["deps"]:
                p = ins[j]
                if p["dma"]:
                    if j not in dma_waited[e]:
                        dm.append(j)
                else:
                    pe_ = p["eng"]
                    if pe_ == e and not r["dma"]:
                        if not (self.same_engine_sync and e != "pe"):
                            continue
                    if j > waited[e][pe_]:
                        need[pe_] = max(need.get(pe_, -1), j)
            for pe_, j in list(need.items()):
                if j < lo and not ins[j]["inc"]:
                    jj = j
                    while jj < lo and not (ins[jj]["eng"] == pe_ and ins[jj]["inc"] and not ins[jj]["dma"]):
                        jj += 1
                    assert jj < lo, "no increment available for cross-block dependency"
                    need[pe_] = jj
            r["w_cmp"] = need
            r["w_dma"] = dm
            for pe_, j in need.items():
                waited[e][pe_] = max(waited[e][pe_], j)
                ins[j]["inc"] = True
            for j in dm:
                dma_waited[e].add(j)
        lastc = {}
        for i in range(lo, hi):
            if not ins[i]["dma"]:
                lastc[ins[i]["eng"]] = i
        for i in lastc.values():
            ins[i]["inc"] = True
        for i in range(lo, hi):
            r = ins[i]
            if r["inc"] and not r["dma"]:
                self.cnt[r["eng"]] += 1
                r["cnt"] = self.cnt[r["eng"]]
        per = {e: [] for e in self.ENGS}
        for i in range(lo, hi):
            per[ins[i]["eng"]].append(ins[i])
        sems, dsem = self.sems, self.dsem
        dma_tot = [0] * len(self.dma_tot)
        for r in ins[:hi]:
            if r["dma"]:
                dma_tot[r["sem"]] = max(dma_tot[r["sem"]], r["tgt"])
        self.done = len(ins)

        def run(eng_obj, lst, fin=False):
            for r in lst:
                for pe_, j in r["w_cmp"].items():
                    eng_obj.wait_ge(sems[pe_], ins[j]["cnt"])
                for j in r["w_dma"]:
                    eng_obj.wait_ge(dsem[ins[j]["sem"]], ins[j]["tgt"])
                inst = getattr(eng_obj, r["meth"])(**r["kw"])
                if r["dma"]:
                    inst.then_inc(dsem[r["sem"]], 16)
                elif r["inc"]:
                    inst.then_inc(sems[r["eng"]], 1)
            if fin:
                for k in range(len(dma_tot)):
                    if dma_tot[k]:
                        eng_obj.wait_ge(dsem[k], dma_tot[k])

        with nc.Block() as block:
            @block.tensor
            def _(eng):
                run(eng, per["pe"])

            @block.scalar
            def _(eng):
                run(eng, per["act"])

            @block.vector
            def _(eng):
                run(eng, per["dve"])

            @block.gpsimd
            def _(eng):
                run(eng, per["pool"])

            @block.sync
            def _(eng):
                run(eng, per["sp"], fin=final)


def make_consts():
    c = {}
    c["ident_f"] = np.eye(128, dtype=np.float32)
    c["ident_b"] = np.eye(128, dtype=np.float32).astype(ml_dtypes.bfloat16)
    c["ones_f"] = np.ones((128, 128), np.float32)
    for ch in (32, 64):
        s = np.arange(ch)[:, None]
        t = np.arange(MT)[None, :] % ch
        c["hg_maskP%d" % ch] = (s <= t).astype(np.uint32)
        rm = np.ones((128, MT), np.float32)
        rm[:, ::ch] = 0.0
        c["resetmask%d" % ch] = rm
    k = np.arange(128)[:, None]
    q = np.arange(MT)[None, :] % 128
    c["sw_mask_cur"] = (k <= q).astype(np.float32).astype(ml_dtypes.bfloat16)
    c["sw_mask_prev"] = (k > q).astype(np.float32).astype(ml_dtypes.bfloat16)
    rot = np.zeros((128, 128), np.float32)
    for blk in (0, 64):
        for m in range(32):
            rot[blk + m + 32, blk + m] = -1.0
            rot[blk + m, blk + m + 32] = 1.0
    c["sw_rotT"] = rot
    ob = np.zeros((128, 128), np.float32)
    ob[0:64, 0:64] = 1.0 / 64
    ob[64:128, 64:128] = 1.0 / 64
    c["sw_ones_blk"] = ob
    inv = (10000.0 ** (-np.arange(0, 64, 2, dtype=np.float32) / 64)).astype(np.float32)
    c["sw_invfreq"] = np.tile(inv, 4).reshape(128, 1).astype(np.float32)
    jj = np.arange(64)[:, None]
    ii = np.arange(256)[None, :] % 64
    c["gd_maskI"] = (jj <= ii).astype(np.float32)
    rm = np.ones((16, 256), np.float32)
    rm[:, ::64] = 0.0
    c["gd_rm"] = rm
    j2 = np.arange(128)[:, None]
    i2 = np.arange(256)[None, :] % 128
    c["gd_maskSP"] = ((j2 // 64 == i2 // 64) & (j2 % 64 < i2 % 64)).astype(np.float32)
    sel = np.zeros((16, 16, 128), np.float32)
    for h in range(16):
        sel[h, h, :] = 1.0
    c["gd_sel"] = sel.reshape(16, 16 * 128)
    return c


CONST_SPECS = {
    "gd_maskI": ([64, 256], F32),
    "gd_rm": ([16, 256], F32),
    "gd_maskSP": ([128, 256], F32),
    "gd_sel": ([16, 2048], F32),
    "sw_mask_cur": ([128, MT], BF16),
    "sw_mask_prev": ([128, MT], BF16),
    "sw_rotT": ([128, 128], F32),
    "sw_ones_blk": ([128, 128], F32),
    "sw_invfreq": ([128, 1], F32),
    "ident_f": ([128, 128], F32),
    "ident_b": ([128, 128], BF16),
    "ones_f": ([128, 128], F32),
    "hg_maskP32": ([32, MT], U32),
    "hg_maskP64": ([64, MT], U32),
    "resetmask32": ([128, MT], F32),
    "resetmask64": ([128, MT], F32),
}


COMMON_CONSTS = ("ident_f", "ident_b", "ones_f")
HG_CONSTS = ("hg_maskP32", "hg_maskP64", "resetmask32", "resetmask64")
GD_CONSTS = ("gd_maskI", "gd_sel", "gd_rm", "gd_maskSP")
SW_CONSTS = ("sw_mask_cur", "sw_mask_prev", "sw_rotT", "sw_ones_blk", "sw_invfreq")


class Ctx:
    pass


def load_consts(g, names, pfx=""):
    for k in names:
        shape, dt = CONST_SPECS[k]
        g.c[k] = g.sb(pfx + "c_" + k, shape, dt)
        g.P.op("sp", "dma_start", out=g.c[k][:], in_=g.dr[k][:, :])


def build(T, layers, n_hg=2):
    assert T % MT == 0
    nc = bass.Bass("TRN2", target_bir_lowering=False)
    P = Prog(nc)
    g = Ctx()
    g.nc, g.P, g.T = nc, P, T
    dr = {}

    def din(name, shape, dt=F32):
        dr[name] = nc.dram_tensor(name, shape, dt, kind="ExternalInput").ap()
        return dr[name]

    din("x", [T, D])
    din("c8", [8, 128])
    din("positions", [1, T], I32)
    din("hgrn_lb", [32, 128])
    din("ada_w", [4, D, 3 * D])
    din("ada_b", [4, 24, 128])
    din("ada_b_row", [4, 1, 3 * D])
    din("norm_g", [4, 8, 128])
    din("hg_in_w", [n_hg, D, 4 * D])
    din("hg_out_w", [n_hg, D, D])
    din("hg_onorm", [n_hg, 1, 128])
    din("sw_in_w", [1, D, 2560])
    din("sw_out_w", [1, D, D])
    din("sw_qnorm", [1, 64])
    din("sw_knorm", [1, 64])
    din("sw_sinks", [1, 16])
    din("gd_in_w", [1, D, 6176])
    din("gd_out_w", [1, 2 * D, D])
    din("gd_conv_w", [4, 4096])
    din("gd_a_log", [16, 1])
    din("gd_dt_bias", [16, 1])
    din("gd_onorm", [1, 128])
    for k, (shape, dt) in CONST_SPECS.items():
        din(k, shape, dt)
    out = nc.dram_tensor("out", [T, D], F32, kind="ExternalOutput").ap()
    scr = [nc.dram_tensor("xs%d" % i, [T, D], F32, kind="Internal").ap() for i in range(2)]
    g.gd_wscr = nc.dram_tensor("gd_wscr", [32, 128, 1024], BF16, kind="Internal").ap()
    g.dr = dr

    with ExitStack() as st:
        def sbg(name, shape, dt=F32):
            return st.enter_context(nc.sbuf_tensor(name, shape, dt))

        P.setup(st)
        g.sb = sbg
        g.psum = [st.enter_context(nc.psum_tensor("ps%d" % i, [128, 512], F32)) for i in range(8)]
        g.c = {}
        load_consts(g, COMMON_CONSTS)
        g.stage = [sbg("stage%d" % i, [128, SW], F32) for i in range(2)]
        P.mark("prep_common")
        prep_common(g)
        P.mark("after_prep_common")
        src = dr["x"]
        srctag = lambda ap, tag: ap
        for li, layer in enumerate(layers):
            dst = out if li == len(layers) - 1 else scr[li % 2]
            kind = layer % 3
            with ExitStack() as lst:
                g.sb = lambda name, shape, dt=F32: lst.enter_context(nc.sbuf_tensor(name, shape, dt))
                if kind == 0:
                    load_consts(g, HG_CONSTS, "L%d" % layer)
                    hgrn2_layer(g, layer, src, srctag, dst)
                elif kind == 1:
                    load_consts(g, SW_CONSTS, "L%d" % layer)
                    swa_layer(g, layer, src, srctag, dst)
                else:
                    load_consts(g, GD_CONSTS, "L%d" % layer)
                    gdn_layer(g, layer, src, srctag, dst)
                P.barrier()
                P.emit(final=(li == len(layers) - 1))
            src = dst
            srctag = Sub
    return nc


def transpose_small(g, dst, src_rows, nrows, ps):
    P = g.P
    P.op("pe", "transpose", out=ps[:, 0:nrows], in_=src_rows, identity=g.c["ident_f"][0:nrows, 0:nrows])
    P.op("dve", "tensor_copy", out=dst, in_=ps[:, 0:nrows])


def W(ap, k):
    return Sub(ap, ("w", k))


def prep_common(g):
    P, sb, dr = g.P, g.sb, g.dr
    ps = g.psum[0]
    g.rows_a = sb("rows_a", [32, 128], F32)
    g.rows_g = sb("rows_g", [8, 128], F32)
    rows_c = sb("rows_c", [8, 128], F32)
    g.c_col = sb("c_col", [128, 8], F32)
    P.op("sp", "dma_start", out=rows_c[:], in_=dr["c8"][:, :])
    transpose_small(g, g.c_col[:], rows_c[:], 8, ps)
    P.op("sp", "dma_start", out=g.rows_a[:], in_=dr["hgrn_lb"][:, :])
    lbT = sb("lbT", [128, 32], F32)
    transpose_small(g, lbT[:], g.rows_a[:], 32, g.psum[1])
    ex = sb("lb_exp", [128, 32], F32)
    P.op("act", "activation", out=ex[:], in_=lbT[:], func=AF.Exp)
    den = sb("lb_den", [128, 8], F32)
    P.op("dve", "tensor_tensor", out=den[:], in0=ex[:, 0:8], in1=ex[:, 8:16], op=ALU.add)
    P.op("dve", "tensor_tensor", out=den[:], in0=den[:], in1=ex[:, 16:24], op=ALU.add)
    P.op("dve", "tensor_tensor", out=den[:], in0=den[:], in1=ex[:, 24:32], op=ALU.add)
    rden = sb("lb_rden", [128, 8], F32)
    P.op("dve", "reciprocal", out=rden[:], in_=den[:])
    g.lb = {}
    lb0 = sb("lb_l0", [128, 8], F32)
    P.op("dve", "memset", ap=lb0[:], constant=0.0)
    g.lb[0] = lb0
    num = sb("lb_num", [128, 8], F32)
    P.op("dve", "tensor_tensor", out=num[:], in0=ex[:, 8:16], in1=ex[:, 16:24], op=ALU.add)
    P.op("dve", "tensor_tensor", out=num[:], in0=num[:], in1=ex[:, 24:32], op=ALU.add)
    lb3 = sb("lb_l3", [128, 8], F32)
    P.op("dve", "tensor_tensor", out=lb3[:], in0=num[:], in1=rden[:], op=ALU.mult)
    g.lb[3] = lb3
    g.modT = sb("modT", [128, 16], F32)
    g.gsT = sb("gsT", [128, 8], F32)
    g.gate_row = sb("gate_row", [1, D], F32)
    g.gate_b = sb("gate_b", [128, D], F32)
    g.adab_T = sb("adab_T", [128, 24], F32)
    g.ng_T = sb("ng_T", [128, 8], F32)


def prep_layer(g, layer):
    P, dr = g.P, g.dr
    ps = g.psum[0]
    P.op("sp", "dma_start", out=g.rows_a[0:24, :], in_=dr["ada_b"][layer])
    transpose_small(g, g.adab_T[:], g.rows_a[0:24, :], 24, g.psum[1])
    P.op("sp", "dma_start", out=g.rows_g[:], in_=dr["norm_g"][layer])
    transpose_small(g, g.ng_T[:], g.rows_g[:], 8, g.psum[2])
    aw = dr["ada_w"][layer].rearrange("(k p) n -> p k n", p=128)
    for j in range(16):
        t = g.stage[j % 2][:, 0:1024].rearrange("p (k n) -> p k n", k=KT)
        P.op("sp", "dma_start", out=t, in_=aw[:, :, j * 128:(j + 1) * 128])
        for k in range(KT):
            P.op("pe", "matmul", out=ps[:, j:j + 1], lhsT=t[:, k, :], rhs=g.c_col[:, k:k + 1],
                 start=(k == 0), stop=(k == KT - 1))
    P.op("dve", "tensor_tensor", out=g.modT[:], in0=ps[:, 0:16], in1=g.adab_T[:, 0:16], op=ALU.add)
    P.op("dve", "scalar_tensor_tensor", out=g.gsT[:], in0=g.modT[:, 8:16], scalar=1.0, in1=g.ng_T[:],
         op0=ALU.add, op1=ALU.mult)
    P.op("sp", "dma_start", out=g.gate_row[:], in_=dr["ada_b_row"][layer][:, 2 * D:3 * D])
    for cg in range(8):
        psg = g.psum[3 + cg % 2]
        t = g.stage[cg % 2][:, 0:1024].rearrange("p (k n) -> p k n", k=KT)
        P.op("sp", "dma_start", out=t, in_=aw[:, :, 2 * D + cg * 128:2 * D + (cg + 1) * 128])
        for k in range(KT):
            P.op("pe", "matmul", out=psg[0:1, 0:128], lhsT=g.c_col[:, k:k + 1], rhs=t[:, k, :],
                 start=(k == 0), stop=(k == KT - 1))
        P.op("dve", "tensor_tensor", out=g.gate_row[0:1, cg * 128:(cg + 1) * 128], in0=psg[0:1, 0:128],
             in1=g.gate_row[0:1, cg * 128:(cg + 1) * 128], op=ALU.add)
    for cg in range(2):
        psg = g.psum[5 + cg]
        P.op("pe", "matmul", out=psg[:], lhsT=g.c["ones_f"][0:1, :], rhs=g.gate_row[0:1, cg * 512:(cg + 1) * 512],
             start=True, stop=True)
        P.op("dve", "tensor_copy", out=g.gate_b[:, cg * 512:(cg + 1) * 512], in_=psg[:])


def load_weight_bf16(g, dst3, wsrc, ncols, colscale=None, kbase=0):
    P = g.P
    nk = wsrc.shape[0] // 128
    i = 0
    for k in range(nk):
        for c0 in range(0, ncols, SW):
            w = min(SW, ncols - c0)
            stg = g.stage[i % 2]
            P.op("sp", "dma_start", out=stg[:, 0:w], in_=wsrc[k * 128:(k + 1) * 128, c0:c0 + w])
            eng = ("dve", "pool")[i % 2]
            dstv = W(dst3[:, k, c0:c0 + w], kbase + k)
            if colscale is None:
                P.op(eng, "tensor_copy", out=dstv, in_=stg[:, 0:w])
            else:
                P.op(eng, "tensor_tensor", out=dstv, in0=stg[:, 0:w], in1=colscale[:, c0:c0 + w], op=ALU.mult)
            i += 1


def prologue(g, mt, src, srctag, hT, xb, xnb, ss, rstd, mtl=MT):
    P = g.P
    P.op("pool", "memset", ap=ss[:], constant=0.0)
    for blk in range(mtl // 128):
        t0 = mt * mtl + blk * 128
        x = xb[blk % len(xb)]
        xn = xnb[blk % len(xnb)]
        P.op("sp", "dma_start", out=x[:], in_=srctag(src[t0:t0 + 128, :], t0 // 128))
        P.op("act", "activation", out=xn[:], in_=x[:], func=AF.Square, accum_out=ss[:, blk:blk + 1])
        P.op("dve", "tensor_scalar", out=rstd[:, blk:blk + 1], in0=ss[:, blk:blk + 1], scalar1=1.0 / D, scalar2=EPS,
             op0=ALU.mult, op1=ALU.add)
        P.op("act", "activation", out=rstd[:, blk:blk + 1], in_=rstd[:, blk:blk + 1], func=AF.Sqrt)
        P.op("dve", "reciprocal", out=rstd[:, blk:blk + 1], in_=rstd[:, blk:blk + 1])
        P.op("dve", "tensor_scalar", out=xn[:], in0=x[:], scalar1=rstd[:, blk:blk + 1], scalar2=None, op0=ALU.mult)
        for kg in range(2):
            ps = g.psum[kg]
            psb = ps[:].bitcast(BF16)
            for kk in range(4):
                k = kg * 4 + kk
                P.op("pe", "transpose", out=psb[:, kk * 128:(kk + 1) * 128], in_=xn[:, k * 128:(k + 1) * 128],
                     identity=g.c["ident_b"][:])
            for kk in range(4):
                k = kg * 4 + kk
                dstv = Sub(hT[:, k, blk * 128:(blk + 1) * 128], k)
                srcv = psb[:, kk * 128:(kk + 1) * 128]
                if k % 2 == 0:
                    P.op("dve", "tensor_scalar", out=dstv, in0=srcv, scalar1=g.gsT[:, k:k + 1],
                         scalar2=g.modT[:, k:k + 1], op0=ALU.mult, op1=ALU.add)
                else:
                    P.op("act", "activation", out=dstv, in_=srcv, func=AF.Identity, scale=g.gsT[:, k:k + 1],
                         bias=g.modT[:, k:k + 1])


def epilogue(g, mt, src, srctag, dst, og, nk, w_out, xb, psa, psb_, ogtag=True, mtl=MT):
    P = g.P
    for blk in range(mtl // 128):
        t0 = mt * mtl + blk * 128
        x = xb[blk % len(xb)]
        P.op("sp", "dma_start", out=x[:], in_=srctag(src[t0:t0 + 128, :], t0 // 128))
        for cg in range(2):
            pst = (psa, psb_)[cg]
            for k in range(nk):
                ogv = og[:, k, blk * 128:(blk + 1) * 128]
                P.op("pe", "matmul", out=pst[:], lhsT=Sub(ogv, k) if ogtag else ogv,
                     rhs=W(w_out[:, k, cg * 512:(cg + 1) * 512], 100 + k), start=(k == 0), stop=(k == nk - 1))
            P.op("dve", "tensor_tensor", out=x[:, cg * 512:(cg + 1) * 512], in0=pst[:], in1=x[:, cg * 512:(cg + 1) * 512],
                 op=ALU.add)
        P.op("sp", "dma_start", out=Sub(dst[t0:t0 + 128, :], t0 // 128), in_=x[:])


def hgrn2_layer(g, layer, src, srctag, dst):
    P, sb, dr, nc = g.P, g.sb, g.dr, g.nc
    j = layer // 3
    T = g.T
    CH = 32 if layer == 0 else 64
    NCH = MT // CH
    P.mark("prep_layer")
    prep_layer(g, layer)
    P.mark("after_prep_layer")
    L = "L%d_" % layer
    g.warena = sb(L + "warena", [128, KT * 5120], BF16)
    w_in = g.warena[:, 0:KT * 4096].rearrange("p (k n) -> p k n", k=KT)
    w_out = g.warena[:, KT * 4096:KT * 4096 + KT * 1024].rearrange("p (k n) -> p k n", k=KT)
    load_weight_bf16(g, w_in, dr["hg_in_w"][j], 4096)
    load_weight_bf16(g, w_out, dr["hg_out_w"][j], 1024, colscale=g.gate_b, kbase=100)
    lb = g.lb[layer]
    oml = sb(L + "oml", [128, 8])
    noml = sb(L + "noml", [128, 8])
    P.op("dve", "tensor_scalar", out=oml[:], in0=lb[:], scalar1=-1.0, scalar2=1.0, op0=ALU.mult, op1=ALU.add)
    P.op("dve", "tensor_scalar", out=noml[:], in0=oml[:], scalar1=-1.0, scalar2=None, op0=ALU.mult)
    onr = sb(L + "onr", [1, 128])
    ong = sb(L + "ong", [128, 1])
    P.op("sp", "dma_start", out=onr[:], in_=dr["hg_onorm"][j])
    transpose_small(g, ong[:], onr[0:1, :], 1, g.psum[7])
    ones_m = sb(L + "ones_m", [128, 128])
    epsc = sb(L + "epsc", [128, 1])
    P.op("dve", "memset", ap=epsc[:], constant=EPS)
    P.op("dve", "tensor_scalar", out=ones_m[:], in0=g.c["ones_f"][:], scalar1=1.0 / 128, scalar2=None, op0=ALU.mult)

    xb = [sb(L + "xb%d" % i, [128, D]) for i in range(2)]
    xnb = [sb(L + "xn%d" % i, [128, D], BF16) for i in range(2)]
    ss = sb(L + "ss", [128, 4])
    rstd = sb(L + "rstd", [128, 4])
    hT = sb(L + "hT", [128, KT, MT], BF16)
    t_q = sb(L + "t_q", [128, MT])
    t_f = sb(L + "t_f", [128, MT])
    t_k = sb(L + "t_k", [128, MT])
    t_l = sb(L + "t_l", [128, MT])
    t_e1 = sb(L + "t_e1", [128, MT])
    t_e2 = sb(L + "t_e2", [128, MT])
    qT = [sb(L + "qT%d" % i, [128, MT], BF16) for i in range(2)]
    kT = [sb(L + "kT%d" % i, [128, MT], BF16) for i in range(2)]
    ktok = [sb(L + "ktok%d" % i, [CH, NCH, 128], BF16) for i in range(2)]
    vtok = [sb(L + "vtok%d" % i, [CH, NCH, 128], BF16) for i in range(2)]
    vT = sb(L + "vT", [128, MT], BF16)
    zs = [sb(L + "zs%d" % i, [128, MT], BF16) for i in range(2)]
    er = sb(L + "er", [128, NCH])
    ebl = sb(L + "ebl", [128, NCH])
    eblr = sb(L + "eblr", [128, NCH])
    atm = [sb(L + "atm%d" % i, [CH, MT], BF16) for i in range(2)]
    for i in range(2):
        P.op("pool", "memset", ap=atm[i][:], constant=0.0)
    S = [sb(L + "S%d" % h, [128, 128]) for h in range(8)]
    srb = [sb(L + "srb%d" % i, [128, 128], BF16) for i in range(2)]
    stmp = [sb(L + "stmp%d" % i, [128, 128]) for i in range(2)]
    osq = sb(L + "osq", [128, MT])
    orstd = sb(L + "orstd", [128, MT])
    otmp = sb(L + "otmp", [128, MT])
    og = sb(L + "og", [128, 8, MT], BF16)
    for h in range(8):
        P.op("pool", "memset", ap=S[h][:], constant=0.0)

    ps = g.psum
    PS_Q, PS_F, PS_Z, PS_V, PS_A, PS_O, PS_S, PS_X = range(8)
    PS_O2, PS_S2 = PS_Q, PS_F
    khT = [sb(L + "khT%d" % i, [128, MT], BF16) for i in range(2)]
    TG = [(t_q, t_f, t_k, t_l, t_e1, t_e2),
          tuple(sb(L + "tg1_%d" % i, [128, MT]) for i in range(6))]
    vT2 = [vT, sb(L + "vT_b", [128, MT], BF16)]
    er2 = [er, sb(L + "er_b", [128, NCH])]
    ebl2 = [ebl, sb(L + "ebl_b", [128, NCH])]
    srb2 = [srb, [sb(L + "srbB%d" % i, [128, 128], BF16) for i in range(2)]]
    P.mark("after_wload")
    RI = CH // 2 - 1
    for mt in range(T // MT):
        prologue(g, mt, src, srctag, hT, xb, xnb, ss, rstd)
        P.mark("after_prologue%d" % mt)
        for hp in range(4):
            def gates_chain(pp, h):
                tq, tB, tk, tD, te1, te2 = TG[pp]
                banks = ((PS_Q, PS_F, PS_V, PS_Z), (PS_A, PS_O, PS_S, PS_X))[pp]
                for (pst, c0) in ((banks[0], h * 128), (banks[1], 1024 + h * 128), (banks[2], 2048 + h * 128),
                                  (banks[3], 3072 + h * 128)):
                    for k in range(KT):
                        P.op("pe", "matmul", out=ps[pst][:], lhsT=W(w_in[:, k, c0:c0 + 128], k), rhs=Sub(hT[:, k, :], k),
                             start=(k == 0), stop=(k == KT - 1))
                yield
                P.op("act", "activation", out=tq[:], in_=ps[banks[0]][:], func=AF.Silu)
                yield
                P.op("act", "activation", out=tB[:], in_=ps[banks[1]][:], func=AF.Sigmoid)
                yield
                P.op("act", "activation", out=zs[pp][:], in_=ps[banks[3]][:], func=AF.Silu)
                yield
                P.op("act", "activation", out=vT2[pp][:], in_=ps[banks[2]][:], func=AF.Copy)
                yield
                P.op("dve", "tensor_scalar", out=tk[:], in0=tB[:], scalar1=noml[:, h:h + 1], scalar2=oml[:, h:h + 1],
                     op0=ALU.mult, op1=ALU.add)
                yield
                P.op("act", "activation", out=tD[:], in_=tB[:], func=AF.Ln, scale=oml[:, h:h + 1], bias=lb[:, h:h + 1])
                yield
                P.op("dve", "tensor_tensor_scan", out=tB[:], data0=g.c["resetmask%d" % CH][:], data1=tD[:], initial=0.0,
                     op0=ALU.mult, op1=ALU.add)
                yield
                b3 = tB[:].rearrange("p (c j) -> p c j", j=CH)
                bm3 = tD[:].rearrange("p (c j) -> p c j", j=CH)
                P.op("dve", "tensor_tensor", out=bm3, in0=b3, in1=b3[:, :, RI:RI + 1].to_broadcast([128, NCH, CH]),
                     op=ALU.subtract)
                yield
                P.op("act", "activation", out=te1[:], in_=tD[:], func=AF.Exp)
                yield
                P.op("act", "activation", out=te2[:], in_=tD[:], func=AF.Exp, scale=-1.0)
                yield
                P.op("act", "activation", out=er2[pp][:], in_=b3[:, :, RI], func=AF.Exp)
                yield
                P.op("act", "activation", out=ebl2[pp][:], in_=b3[:, :, CH - 1], func=AF.Exp)
                yield
                P.op("pool", "tensor_tensor", out=qT[pp][:], in0=tq[:], in1=te1[:], op=ALU.mult)
                yield
                P.op("dve", "tensor_tensor", out=kT[pp][:], in0=tk[:], in1=te2[:], op=ALU.mult)
                yield
                P.op("dve", "tensor_tensor", out=bm3, in0=b3[:, :, CH - 1:CH].to_broadcast([128, NCH, CH]), in1=b3,
                     op=ALU.subtract)
                yield
                P.op("act", "activation", out=te1[:], in_=tD[:], func=AF.Exp)
                yield
                P.op("dve", "tensor_tensor", out=khT[pp][:], in0=tk[:], in1=te1[:], op=ALU.mult)
                yield

            gens = [gates_chain(0, 2 * hp), gates_chain(1, 2 * hp + 1)]
            while gens:
                for g_ in list(gens):
                    try:
                        next(g_)
                    except StopIteration:
                        gens.remove(g_)
            for pp in range(2):
                h = 2 * hp + pp
                vT = vT2[pp]
                P.mark("head%d_%d" % (mt, h))
                psb = ps[PS_X][:].bitcast(BF16)
                for (srcT, dstk) in ((khT[pp], ktok[pp]), (vT, vtok[pp])):
                    for c0 in range(0, NCH, 8):
                        for c in range(c0, c0 + 8):
                            P.op("pe", "transpose", out=psb[0:CH, (c - c0) * 128:(c - c0 + 1) * 128],
                                 in_=srcT[:, c * CH:(c + 1) * CH], identity=g.c["ident_b"][:])
                        P.op("act", "activation", out=dstk[:, c0:c0 + 8, :].rearrange("p a b -> p (a b)"), in_=psb[0:CH, :],
                             func=AF.Copy)
                a = atm[pp]
                for c in range(NCH):
                    tcs = slice(c * CH, (c + 1) * CH)
                    P.op("pe", "matmul", out=ps[PS_A][0:CH, tcs], lhsT=kT[pp][:, tcs], rhs=qT[pp][:, tcs], start=True, stop=True)
                P.op("dve", "copy_predicated", out=a[:], mask=g.c["hg_maskP%d" % CH][:], data=ps[PS_A][0:CH, :])
            P.mark("scan%d_%d" % (mt, hp))
            for c in range(NCH):
                tcs = slice(c * CH, (c + 1) * CH)
                for pp in range(2):
                    h = 2 * hp + pp
                    pso = ps[(PS_O, PS_O2)[pp]]
                    pss = ps[(PS_S, PS_S2)[pp]]
                    sr = srb2[pp][c % 2]
                    P.op("act", "activation", out=sr[:], in_=S[h][:], func=AF.Copy, scale=er2[pp][:, c:c + 1])
                    P.op("pe", "matmul", out=pso[:, tcs], lhsT=sr[:], rhs=qT[pp][:, tcs], start=True, stop=False)
                    P.op("pe", "matmul", out=pso[:, tcs], lhsT=vtok[pp][:, c, :], rhs=atm[pp][:, tcs], start=False, stop=True)
                    P.op("pe", "matmul", out=pss[:, 0:128], lhsT=ktok[pp][:, c, :], rhs=vtok[pp][:, c, :],
                         start=True, stop=True)
                    P.op("dve", "scalar_tensor_tensor", out=S[h][:], in0=S[h][:], scalar=ebl2[pp][:, c:c + 1],
                         in1=pss[:, 0:128], op0=ALU.mult, op1=ALU.add)
            for pp in range(2):
                h = 2 * hp + pp
                pso = ps[(PS_O, PS_O2)[pp]]
                P.op("act", "activation", out=osq[:], in_=pso[:], func=AF.Square)
                P.op("pe", "matmul", out=ps[PS_A][:], lhsT=ones_m[:], rhs=osq[:], start=True, stop=True)
                P.op("act", "activation", out=orstd[:], in_=ps[PS_A][:], func=AF.Sqrt, bias=epsc[:, 0:1])
                P.op("dve", "reciprocal", out=orstd[:], in_=orstd[:])
                P.op("dve", "scalar_tensor_tensor", out=otmp[:], in0=pso[:], scalar=ong[:, 0:1], in1=orstd[:],
                     op0=ALU.mult, op1=ALU.mult)
                P.op("pool", "tensor_tensor", out=Sub(og[:, h, :], h), in0=otmp[:], in1=zs[pp][:], op=ALU.mult)
        P.mark("epilogue%d" % mt)
        epilogue(g, mt, src, srctag, dst, og, KT, w_out, xb, ps[PS_V], ps[PS_Z])


def swa_layer(g, layer, src, srctag, dst):
    import math
    P, sb, dr, nc = g.P, g.sb, g.dr, g.nc
    T = g.T
    P.mark("prep_layer")
    prep_layer(g, layer)
    L = "L%d_" % layer
    NB = MT // 128
    g.warena = sb(L + "warena", [128, KT * 4096], BF16)
    w_in = g.warena[:, 0:KT * 3072].rearrange("p (k n) -> p k n", k=KT)
    w_out = g.warena[:, KT * 3072:KT * 3072 + KT * 1024].rearrange("p (k n) -> p k n", k=KT)
    WQ, WK, WV, WZ = 0, 1024, 1536, 2048
    win = dr["sw_in_w"][0]
    i = 0
    for k in range(KT):
        for (c0, w, dup, d0) in ((0, 1024, False, WQ), (1024, 256, True, WK), (1280, 256, True, WV), (1536, 1024, False, WZ)):
            stg = g.stage[i % 2]
            eng = ("dve", "pool")[i % 2]
            i += 1
            P.op("sp", "dma_start", out=stg[:, 0:w], in_=win[k * 128:(k + 1) * 128, c0:c0 + w])
            if not dup:
                P.op(eng, "tensor_copy", out=W(w_in[:, k, d0:d0 + w], k), in_=stg[:, 0:w])
            else:
                dv = w_in[:, k, d0:d0 + 512].rearrange("p (h r d) -> p h r d", h=4, r=2)
                sv = stg[:, 0:256].rearrange("p (h d) -> p h d", h=4)
                for r in range(2):
                    P.op(eng, "tensor_copy", out=W(dv[:, :, r, :], k), in_=sv)
    load_weight_bf16(g, w_out, dr["sw_out_w"][0], 1024, colscale=g.gate_b, kbase=100)

    gq = sb(L + "gq", [128, 1])
    gk = sb(L + "gk", [128, 1])
    for (dst_, nm) in ((gq, "sw_qnorm"), (gk, "sw_knorm")):
        r1 = sb(L + nm + "_r", [1, 128])
        P.op("sp", "dma_start", out=r1[0:1, 0:64], in_=dr[nm][0:1, :])
        P.op("sp", "dma_start", out=r1[0:1, 64:128], in_=dr[nm][0:1, :])
        transpose_small(g, dst_[:], r1[0:1, :], 1, g.psum[7])
    epsc = sb(L + "epsc", [128, 1])
    P.op("dve", "memset", ap=epsc[:], constant=EPS)
    negpi = sb(L + "negpi", [128, 1])
    P.op("dve", "memset", ap=negpi[:], constant=-math.pi)
    cpi = sb(L + "cpi", [128, 3])
    P.op("dve", "memset", ap=cpi[:, 0:1], constant=math.pi)
    P.op("dve", "memset", ap=cpi[:, 1:2], constant=1.5 * math.pi)
    P.op("dve", "memset", ap=cpi[:, 2:3], constant=2 * math.pi)
    sk_r = sb(L + "sk_r", [1, 16])
    P.op("sp", "dma_start", out=sk_r[:], in_=dr["sw_sinks"][0:1, :])
    P.op("act", "activation", out=sk_r[:], in_=sk_r[:], func=AF.Exp)
    P.op("pe", "matmul", out=g.psum[6][:, 0:16], lhsT=g.c["ones_f"][0:1, :], rhs=sk_r[0:1, :], start=True, stop=True)
    esb = sb(L + "esb", [128, 16])
    P.op("dve", "tensor_copy", out=esb[:], in_=g.psum[6][:, 0:16])
    esb4 = esb[:].rearrange("p (h j s) -> p h j s", h=4, j=2)
    esb2 = sb(L + "esb2", [128, 4, 2])
    P.op("dve", "tensor_copy", out=esb2[0:64, :, :], in_=esb4[0:64, :, :, 0])
    P.op("dve", "tensor_copy", out=esb2[64:128, :, :], in_=esb4[64:128, :, :, 1])
    esf = sb(L + "esf", [128, 4, MT])
    for h in range(4):
        for sc in range(2):
            P.op("dve", "tensor_copy", out=esf[:, h, sc * 256:(sc + 1) * 256].rearrange("p (j q) -> p j q", j=2),
                 in_=esb2[:, h, :].unsqueeze(2).to_broadcast([128, 2, 128]))
    ones_b = sb(L + "ones_b", [128, 128], BF16)
    P.op("dve", "tensor_copy", out=ones_b[:], in_=g.c["ones_f"][:])

    xb = [sb(L + "xb%d" % i, [128, D]) for i in range(2)]
    xnb = [sb(L + "xn%d" % i, [128, D], BF16) for i in range(2)]
    ss = sb(L + "ss", [128, 4])
    rstd = sb(L + "rstd", [128, 4])
    hT = sb(L + "hT", [128, KT, MT], BF16)
    cos2 = sb(L + "cos2", [128, MT])
    sin2 = sb(L + "sin2", [128, MT])
    t_sq = sb(L + "t_sq", [128, MT])
    t_rs = sb(L + "t_rs", [128, MT])
    t_qn = sb(L + "t_qn", [128, MT])
    t_a = sb(L + "t_a", [128, MT])
    t_b = sb(L + "t_b", [128, MT])
    posi, ang, ua = t_a[:].bitcast(I32), t_b, t_sq
    qg = sb(L + "qg", [128, 8, MT], BF16)
    kg = sb(L + "kg", [128, 4, (NB + 1) * 128], BF16)
    vd = sb(L + "vd", [128, NB + 1, 512], BF16)
    zs = sb(L + "zs", [128, 8, MT], BF16)
    pT = [sb(L + "pT%d" % i, [128, MT], BF16) for i in range(2)]
    dtmp, otmp = t_rs, t_qn
    og = sb(L + "og", [128, 8, MT], BF16)
    ps = g.psum
    PS_A, PS_B, PS_N, PS_R, PS_S0, PS_S1, PS_O, PS_D = range(8)

    def normrope(pst, gain, out_ap):
        P.op("act", "activation", out=t_sq[:], in_=pst[:], func=AF.Square)
        P.op("pe", "matmul", out=ps[PS_N][:], lhsT=g.c["sw_ones_blk"][:], rhs=t_sq[:], start=True, stop=True)
        P.op("act", "activation", out=t_rs[:], in_=ps[PS_N][:], func=AF.Sqrt, bias=epsc[:, 0:1])
        P.op("dve", "reciprocal", out=t_rs[:], in_=t_rs[:])
        P.op("dve", "scalar_tensor_tensor", out=t_qn[:], in0=pst[:], scalar=gain[:, 0:1], in1=t_rs[:],
             op0=ALU.mult, op1=ALU.mult)
        P.op("pe", "matmul", out=ps[PS_R][:], lhsT=g.c["sw_rotT"][:], rhs=t_qn[:], start=True, stop=True)
        P.op("pool", "tensor_tensor", out=t_a[:], in0=t_qn[:], in1=cos2[:], op=ALU.mult)
        P.op("dve", "tensor_tensor", out=t_b[:], in0=ps[PS_R][:], in1=sin2[:], op=ALU.mult)
        P.op("pool", "tensor_tensor", out=out_ap, in0=t_a[:], in1=t_b[:], op=ALU.add)

    for mt in range(T // MT):
        prologue(g, mt, src, srctag, hT, xb, xnb, ss, rstd)
        P.mark("sw_rope%d" % mt)
        P.op("sp", "dma_start", out=posi, in_=dr["positions"][0:1, mt * MT:(mt + 1) * MT].partition_broadcast(128))
        P.op("dve", "tensor_copy", out=ang[:], in_=posi)
        P.op("dve", "tensor_scalar", out=ang[:], in0=ang[:], scalar1=g.c["sw_invfreq"][:, 0:1], scalar2=None, op0=ALU.mult)
        C1 = 6.28125
        C2 = 2 * math.pi - C1
        P.op("dve", "tensor_scalar", out=ua[:], in0=ang[:], scalar1=1.0 / (2 * math.pi), scalar2=None, op0=ALU.mult)
        P.op("dve", "tensor_copy", out=posi, in_=ua[:])
        P.op("dve", "tensor_copy", out=ua[:], in_=posi)
        P.op("dve", "scalar_tensor_tensor", out=ang[:], in0=ua[:], scalar=-C1, in1=ang[:], op0=ALU.mult, op1=ALU.add)
        P.op("dve", "scalar_tensor_tensor", out=ang[:], in0=ua[:], scalar=-C2, in1=ang[:], op0=ALU.mult, op1=ALU.add)
        P.op("dve", "tensor_scalar", out=ua[:], in0=ang[:], scalar1=math.pi, scalar2=None, op0=ALU.is_gt)
        P.op("dve", "scalar_tensor_tensor", out=ang[:], in0=ua[:], scalar=-2 * math.pi, in1=ang[:], op0=ALU.mult, op1=ALU.add)
        P.op("dve", "tensor_scalar", out=ang[:], in0=ang[:], scalar1=math.pi, scalar2=-math.pi, op0=ALU.min, op1=ALU.max)
        P.op("act", "activation", out=sin2[:], in_=ang[:], func=AF.Sin)
        P.op("act", "activation", out=ua[:], in_=ang[:], func=AF.Sin, scale=0.5)
        P.op("dve", "tensor_tensor", out=ua[:], in0=ua[:], in1=ua[:], op=ALU.mult)
        P.op("dve", "tensor_scalar", out=cos2[:], in0=ua[:], scalar1=-2.0, scalar2=1.0, op0=ALU.mult, op1=ALU.add)
        P.mark("sw_proj%d" % mt)
        for p in range(8):
            pst = ps[PS_A + p % 2]
            for k in range(KT):
                P.op("pe", "matmul", out=pst[:], lhsT=W(w_in[:, k, WQ + p * 128:WQ + (p + 1) * 128], k), rhs=Sub(hT[:, k, :], k),
                     start=(k == 0), stop=(k == KT - 1))
            normrope(pst, gq, Sub(qg[:, p, :], p))
        for h in range(4):
            pst = ps[PS_A + h % 2]
            for k in range(KT):
                P.op("pe", "matmul", out=pst[:], lhsT=W(w_in[:, k, WK + h * 128:WK + (h + 1) * 128], k), rhs=Sub(hT[:, k, :], k),
                     start=(k == 0), stop=(k == KT - 1))
            normrope(pst, gk, Sub(kg[:, h, 128:128 + MT], ("k", h)))
        for blk in range(NB):
            pst = ps[PS_A + blk % 2]
            for k in range(KT):
                P.op("pe", "matmul", out=pst[:], lhsT=Sub(hT[:, k, blk * 128:(blk + 1) * 128], k),
                     rhs=W(w_in[:, k, WV:WV + 512], k), start=(k == 0), stop=(k == KT - 1))
            P.op("act", "activation", out=Sub(vd[:, 1 + blk, :], 1 + blk), in_=pst[:], func=AF.Copy)
        for p in range(8):
            pst = ps[PS_A + p % 2]
            for k in range(KT):
                P.op("pe", "matmul", out=pst[:], lhsT=W(w_in[:, k, WZ + p * 128:WZ + (p + 1) * 128], k), rhs=Sub(hT[:, k, :], k),
                     start=(k == 0), stop=(k == KT - 1))
            P.op("act", "activation", out=Sub(zs[:, p, :], p), in_=pst[:], func=AF.Silu)
        P.mark("sw_attn%d" % mt)
        for qb in range(NB):
            qcs = slice(qb * 128, (qb + 1) * 128)
            first = (mt == 0 and qb == 0)
            kbs = ([] if first else [0]) + [1]
            for h in range(4):
                for kb in kbs:
                    kblk = qb + kb
                    kcs = slice(kblk * 128, (kblk + 1) * 128)
                    pt = pT[kb]
                    for sl in range(2):
                        pss = ps[(PS_S0, PS_N)[sl] + kb]
                        rs = slice(sl * 64, (sl + 1) * 64)
                        P.op("pe", "matmul", out=pss[:, 0:256],
                             lhsT=Sub(kg[rs, h, kcs], ("k", h)) if kblk > 0 else Sub(kg[rs, h, kcs], ("kh", h)),
                             rhs=qg[rs, 2 * h:2 * h + 2, qcs], start=True, stop=True)
                        P.op("act", "activation", out=pt[:, sl * 256:(sl + 1) * 256], in_=pss[:, 0:256], func=AF.Exp, scale=0.125)
                    P.op("pool", "tensor_tensor", out=pt[:], in0=pt[:], in1=g.c["sw_mask_prev" if kb == 0 else "sw_mask_cur"][:],
                         op=ALU.mult)
                for ii, kb in enumerate(kbs):
                    kblk = qb + kb
                    vsub = Sub(vd[:, kblk, h * 128:(h + 1) * 128], kblk)
                    P.op("pe", "matmul", out=ps[PS_O][:], lhsT=vsub, rhs=pT[kb][:], start=(ii == 0), stop=(ii == len(kbs) - 1))
                for ii, kb in enumerate(kbs):
                    P.op("pe", "matmul", out=ps[PS_D][:], lhsT=ones_b[:], rhs=pT[kb][:], start=(ii == 0), stop=(ii == len(kbs) - 1))
                P.op("dve", "tensor_tensor", out=dtmp[:], in0=ps[PS_D][:], in1=esf[:, h, :], op=ALU.add)
                P.op("dve", "reciprocal", out=dtmp[:], in_=dtmp[:])
                P.op("dve", "tensor_tensor", out=otmp[:], in0=ps[PS_O][:], in1=dtmp[:], op=ALU.mult)
                for sl in range(2):
                    rs = slice(sl * 64, (sl + 1) * 64)
                    P.op("pool", "tensor_tensor", out=og[rs, 2 * h:2 * h + 2, qcs],
                         in0=otmp[rs, sl * 256:(sl + 1) * 256].rearrange("p (j q) -> p j q", j=2),
                         in1=zs[rs, 2 * h:2 * h + 2, qcs], op=ALU.mult)
        for h in range(4):
            P.op("pool", "tensor_copy", out=Sub(kg[:, h, 0:128], ("kh", h)), in_=Sub(kg[:, h, NB * 128:(NB + 1) * 128], ("k", h)))
        P.op("pool", "tensor_copy", out=Sub(vd[:, 0, :], 0), in_=Sub(vd[:, NB, :], NB))
        P.mark("epilogue%d" % mt)
        epilogue(g, mt, src, srctag, dst, og, KT, w_out, xb, ps[PS_A], ps[PS_B], ogtag=False)


def gdn_layer(g, layer, src, srctag, dst):
    P, sb, dr, nc = g.P, g.sb, g.dr, g.nc
    T = g.T
    GT = 256
    NC = GT // 64
    P.mark("prep_layer")
    prep_layer(g, layer)
    L = "L%d_" % layer
    NRES = 2080
    g.warena = sb(L + "warena", [128, KT * NRES + 16 * 1024], BF16)
    w_in = g.warena[:, 0:KT * NRES].rearrange("p (k n) -> p k n", k=KT)
    w_out = g.warena[:, KT * NRES:KT * NRES + 16 * 1024].rearrange("p (k n) -> p k n", k=16)
    gin = dr["gd_in_w"][0]
    load_weight_bf16(g, w_in[:, :, 0:2048], gin[:, 0:2048], 2048)
    load_weight_bf16(g, w_in[:, :, 2048:2080], gin[:, 6144:6176], 32)
    load_weight_bf16(g, w_out, dr["gd_out_w"][0], 1024, colscale=g.gate_b, kbase=100)
    wscr = g.gd_wscr
    cbf = [sb(L + "cbf%d" % i, [128, SW], BF16) for i in range(2)]
    i = 0
    for k in range(KT):
        for ch in range(4):
            stg = g.stage[i % 2]
            cb = cbf[i % 2]
            P.op("sp", "dma_start", out=stg[:, 0:1024], in_=gin[k * 128:(k + 1) * 128, 2048 + ch * 1024:2048 + (ch + 1) * 1024])
            P.op(("dve", "pool")[i % 2], "tensor_copy", out=cb[:], in_=stg[:, 0:1024])
            P.op("sp", "dma_start", out=wscr[ch * 8:(ch + 1) * 8, :, k * 128:(k + 1) * 128].rearrange("h p n -> p h n"),
                 in_=cb[:].rearrange("p (h n) -> p h n", h=8))
            i += 1
    P.barrier()
    wbuf = [sb(L + "wbuf%d" % i, [128, 1024], BF16) for i in range(4)]
    g.wcnt = 0

    def stream_w(hd):
        wb = wbuf[g.wcnt % 4]
        g.wcnt += 1
        P.op("sp", "dma_start", out=wb[:], in_=wscr[hd])
        return wb
    ps = g.psum
    PA, PB, PX, PQ, PD, PM, PN, PT = range(8)
    PO = PD
    cwr = sb(L + "cwr", [4, 4096])
    P.op("sp", "dma_start", out=cwr[:], in_=dr["gd_conv_w"][:, :])
    for ct in range(32):
        P.op("pe", "transpose", out=ps[PA][:, ct * 4:(ct + 1) * 4], in_=cwr[0:4, ct * 128:(ct + 1) * 128],
             identity=g.c["ident_f"][0:4, 0:4])
    cw = sb(L + "cw", [128, 128])
    P.op("dve", "tensor_copy", out=cw[:], in_=ps[PA][:, 0:128])
    alog = sb(L + "alog", [16, 1])
    dtb = sb(L + "dtb", [16, 1])
    P.op("sp", "dma_start", out=alog[:], in_=dr["gd_a_log"][:, :])
    P.op("sp", "dma_start", out=dtb[:], in_=dr["gd_dt_bias"][:, :])
    nA = sb(L + "nA", [16, 1])
    P.op("act", "activation", out=nA[:], in_=alog[:], func=AF.Exp)
    P.op("dve", "tensor_scalar", out=nA[:], in0=nA[:], scalar1=-1.0, scalar2=None, op0=ALU.mult)
    one16 = sb(L + "one16", [16, 1])
    P.op("dve", "memset", ap=one16[:], constant=1.0)
    onr = sb(L + "onr", [1, 128])
    ong = sb(L + "ong", [128, 1])
    P.op("sp", "dma_start", out=onr[:], in_=dr["gd_onorm"][:, :])
    transpose_small(g, ong[:], onr[0:1, :], 1, ps[PB])
    ones_m = sb(L + "ones_m", [128, 128])
    P.op("dve", "tensor_scalar", out=ones_m[:], in0=g.c["ones_f"][:], scalar1=1.0 / 128, scalar2=None, op0=ALU.mult)
    epsc = sb(L + "epsc", [128, 1])
    P.op("dve", "memset", ap=epsc[:], constant=EPS)
    identb64 = g.c["ident_b"]

    xb = [sb(L + "xb0", [128, D])]
    xnb = [sb(L + "xn0", [128, D], BF16)]
    ss = sb(L + "ss", [128, 4])
    rstd = sb(L + "rstd", [128, 4])
    hT = sb(L + "hT", [128, KT, GT], BF16)
    halo = sb(L + "halo", [128, 32, 3])
    for ct in range(32):
        P.op("pool", "memset", ap=Sub(halo[:, ct, :], ct), constant=0.0)
    xc = sb(L + "xc", [128, 3 + GT])
    acc = sb(L + "acc", [128, GT])
    f_a = sb(L + "f_a", [128, GT])
    f_b = sb(L + "f_b", [128, GT])
    f_c = sb(L + "f_c", [128, GT])
    TS = [dict(xc=xc, acc=acc, f_a=f_a, f_b=f_b, f_c=f_c),
          dict(xc=sb(L + "xc1", [128, 3 + GT]), acc=sb(L + "acc1", [128, GT]), f_a=sb(L + "f_a1", [128, GT]),
               f_b=sb(L + "f_b1", [128, GT]), f_c=sb(L + "f_c1", [128, GT]))]
    vT2 = [sb(L + "vT0", [128, GT], BF16), sb(L + "vT1", [128, GT], BF16)]
    qT = sb(L + "qT", [128, GT], BF16)
    kT = sb(L + "kT", [128, GT], BF16)
    kbT = sb(L + "kbT", [128, GT], BF16)
    qd = [sb(L + "qd%d" % i, [128, GT], BF16) for i in range(2)]
    zs = [sb(L + "zs%d" % i, [128, GT], BF16) for i in range(2)]
    edl4 = [sb(L + "edl4_%d" % i, [128, NC]) for i in range(2)]
    r_beta = sb(L + "r_beta", [16, GT])
    r_g = sb(L + "r_g", [16, GT])
    r_d = sb(L + "r_d", [16, GT])
    r_be = sb(L + "r_be", [16, GT])
    r_edl = sb(L + "r_edl", [16, GT])
    NP = GT // 128
    tkp_b = sb(L + "tkp_b", [128, 16, NP])
    tkp_be = sb(L + "tkp_be", [128, 16, NP])
    tkp_d = sb(L + "tkp_d", [128, 16, NP])
    tk_edl = sb(L + "tk_edl", [64, 16, NC])
    tk_d = sb(L + "tk_d", [64, 16, NC])
    Et = sb(L + "Et", [64, GT])
    Em = sb(L + "Em", [64, GT])
    Ep = sb(L + "Ep", [128, GT])
    Mk = [sb(L + "Mk%d" % i, [128, 512]) for i in range(2)]
    Nk = [sb(L + "Nk%d" % i, [128, 512]) for i in range(2)]
    Tk = [sb(L + "Tk%d" % i, [128, 512]) for i in range(2)]
    Tbf = sb(L + "Tbf", [128, 512], BF16)
    qkM = sb(L + "qkM", [64, 512], BF16)
    vb = sb(L + "vb", [128, 2 * NP, 128], BF16)
    kbd = sb(L + "kbd", [128, 2 * NP, 128], BF16)
    khat = sb(L + "khat", [64, 8, 128], BF16)
    wTn = sb(L + "wTn", [128, 512], BF16)
    vn2 = [[sb(L + "vn%d_%d" % (hh, i), [64, 128], BF16) for i in range(2)] for hh in range(2)]
    S = [sb(L + "S%d" % h, [128, 128]) for h in range(16)]
    Sb = [sb(L + "Sb%d" % h, [128, 128], BF16) for h in range(16)]
    for h in range(16):
        P.op("pool", "memset", ap=S[h][:], constant=0.0)
        P.op("pool", "memset", ap=Sb[h][:], constant=0.0)
    og = sb(L + "og", [128, 16, GT], BF16)
    identI = sb(L + "identI", [128, 512])
    for b in range(4):
        P.op("dve", "tensor_copy", out=identI[:, b * 128:(b + 1) * 128], in_=g.c["ident_f"][:, :])
    rm16 = g.c["gd_rm"][:, :]
    sel = g.c["gd_sel"]

    def conv_silu(pst, ct, out_ap, func=AF.Silu):
        P.op("pool", "tensor_copy", out=xc[:, 0:3], in_=halo[:, ct, :])
        P.op("act", "activation", out=xc[:, 3:3 + GT], in_=pst[:, 0:GT], func=AF.Copy)
        P.op("pool", "tensor_copy", out=halo[:, ct, :], in_=xc[:, GT:GT + 3])
        P.op("dve", "tensor_scalar", out=acc[:], in0=xc[:, 3:3 + GT], scalar1=cw[:, ct * 4 + 3:ct * 4 + 4], scalar2=None,
             op0=ALU.mult)
        for jt in (2, 1, 0):
            P.op("dve", "scalar_tensor_tensor", out=acc[:], in0=xc[:, jt:jt + GT], scalar=cw[:, ct * 4 + jt:ct * 4 + jt + 1],
                 in1=acc[:], op0=ALU.mult, op1=ALU.add)
        P.op("act", "activation", out=out_ap, in_=acc[:], func=func)

    def interleave(gens):
        gens = list(gens)
        while gens:
            for g_ in list(gens):
                try:
                    next(g_)
                except StopIteration:
                    gens.remove(g_)

    def conv_silu_g(pst, ct, out_ap, t_):
        xc_, acc_ = t_["xc"], t_["acc"]
        P.op("pool", "tensor_copy", out=xc_[:, 0:3], in_=Sub(halo[:, ct, :], ct))
        yield
        P.op("act", "activation", out=xc_[:, 3:3 + GT], in_=pst[:, 0:GT], func=AF.Copy)
        yield
        P.op("pool", "tensor_copy", out=Sub(halo[:, ct, :], ct), in_=xc_[:, GT:GT + 3])
        yield
        P.op("dve", "tensor_scalar", out=acc_[:], in0=xc_[:, 3:3 + GT], scalar1=cw[:, ct * 4 + 3:ct * 4 + 4], scalar2=None,
             op0=ALU.mult)
        yield
        for jt in (2, 1, 0):
            P.op("dve", "scalar_tensor_tensor", out=acc_[:], in0=xc_[:, jt:jt + GT], scalar=cw[:, ct * 4 + jt:ct * 4 + jt + 1],
                 in1=acc_[:], op0=ALU.mult, op1=ALU.add)
            yield
        P.op("act", "activation", out=out_ap, in_=acc_[:], func=AF.Silu)
        yield

    def l2norm_g(t_, out_bf, scale, psb_):
        src_f, fb_, fc_ = t_["f_a"], t_["f_b"], t_["f_c"]
        P.op("pool", "tensor_tensor", out=fb_[:], in0=src_f[:], in1=src_f[:], op=ALU.mult)
        yield
        P.op("pe", "matmul", out=psb_[:, 0:GT], lhsT=g.c["ones_f"][:], rhs=fb_[:], start=True, stop=True)
        yield
        P.op("act", "activation", out=fc_[:], in_=psb_[:, 0:GT], func=AF.Sqrt, bias=epsc[:, 0:1])
        yield
        P.op("dve", "reciprocal", out=fc_[:], in_=fc_[:])
        yield
        P.op("dve", "scalar_tensor_tensor", out=out_bf, in0=src_f[:], scalar=scale, in1=fc_[:], op0=ALU.mult, op1=ALU.mult)
        yield

    def qk_chain(pst, c0, ct, t_, out_bf, scale, psb_):
        inproj(pst[:, 0:GT], c0)
        yield
        yield from conv_silu_g(pst, ct, t_["f_a"][:], t_)
        yield from l2norm_g(t_, out_bf, scale, psb_)

    def head_front(hh, h):
        pv = ps[(PA, PT)[hh]]
        pz = ps[(PB, PN)[hh]]
        inproj_s(pv[:, 0:GT], h)
        yield
        yield from conv_silu_g(pv, 16 + h, vT2[hh][:], TS[hh])
        inproj_s(pz[:, 0:GT], 16 + h)
        yield
        P.op("act", "activation", out=zs[hh][:], in_=pz[:, 0:GT], func=AF.Silu)
        yield

    def inproj(pst, c0, m=128):
        for k in range(KT):
            P.op("pe", "matmul", out=pst, lhsT=W(w_in[:, k, c0:c0 + m], k), rhs=Sub(hT[:, k, :], k),
                 start=(k == 0), stop=(k == KT - 1))

    def inproj_s(pst, hd):
        wb = stream_w(hd)
        for k in range(KT):
            P.op("pe", "matmul", out=pst, lhsT=wb[:, k * 128:(k + 1) * 128], rhs=Sub(hT[:, k, :], k),
                 start=(k == 0), stop=(k == KT - 1))

    def l2norm(src_f, out_bf, scale):
        P.op("pool", "tensor_tensor", out=f_b[:], in0=src_f[:], in1=src_f[:], op=ALU.mult)
        P.op("pe", "matmul", out=ps[PB][:, 0:GT], lhsT=g.c["ones_f"][:], rhs=f_b[:], start=True, stop=True)
        P.op("act", "activation", out=f_c[:], in_=ps[PB][:, 0:GT], func=AF.Sqrt, bias=epsc[:, 0:1])
        P.op("dve", "reciprocal", out=f_c[:], in_=f_c[:])
        P.op("dve", "scalar_tensor_tensor", out=out_bf, in0=src_f[:], scalar=scale, in1=f_c[:], op0=ALU.mult, op1=ALU.mult)

    for mt in range(T // GT):
        prologue(g, mt, src, srctag, hT, xb, xnb, ss, rstd, mtl=GT)
        P.mark("gd_rows%d" % mt)
        inproj(ps[PA][0:16, 0:GT], 2048, 16)
        inproj(ps[PA][0:16, GT:2 * GT], 2064, 16)
        P.op("act", "activation", out=r_beta[:], in_=ps[PA][0:16, GT:2 * GT], func=AF.Sigmoid)
        P.op("act", "activation", out=r_g[:], in_=ps[PA][0:16, 0:GT], func=AF.Exp, bias=dtb[:, 0:1])
        P.op("act", "activation", out=r_g[:], in_=r_g[:], func=AF.Ln, bias=one16[:, 0:1])
        P.op("dve", "tensor_scalar", out=r_g[:], in0=r_g[:], scalar1=nA[:, 0:1], scalar2=None, op0=ALU.mult)
        P.op("dve", "tensor_tensor_scan", out=r_d[:], data0=rm16, data1=r_g[:], initial=0.0, op0=ALU.mult, op1=ALU.add)
        P.op("act", "activation", out=r_be[:], in_=r_d[:], func=AF.Exp)
        P.op("dve", "tensor_tensor", out=r_be[:], in0=r_be[:], in1=r_beta[:], op=ALU.mult)
        d3 = r_d[:].rearrange("p (c j) -> p c j", j=64)
        P.op("dve", "tensor_tensor", out=r_edl[:].rearrange("p (c j) -> p c j", j=64), in0=d3,
             in1=d3[:, :, 63:64].to_broadcast([16, NC, 64]), op=ALU.subtract)
        P.op("act", "activation", out=r_edl[:], in_=r_edl[:], func=AF.Exp, scale=-1.0)
        for (row, tok) in ((r_edl, tk_edl), (r_d, tk_d)):
            for c in range(NC):
                P.op("pe", "transpose", out=ps[PB][0:64, c * 16:(c + 1) * 16], in_=row[0:16, c * 64:(c + 1) * 64],
                     identity=g.c["ident_f"][0:16, 0:16])
            P.op("dve", "tensor_copy", out=tok[:].rearrange("p h c -> p c h"),
                 in_=ps[PB][0:64, 0:NC * 16].rearrange("p (c h) -> p c h", h=16))
        for (row, tok) in ((r_beta, tkp_b), (r_be, tkp_be), (r_d, tkp_d)):
            for pr in range(NP):
                P.op("pe", "transpose", out=ps[PB][:, pr * 16:(pr + 1) * 16], in_=row[0:16, pr * 128:(pr + 1) * 128],
                     identity=g.c["ident_f"][0:16, 0:16])
            P.op("dve", "tensor_copy", out=tok[:].rearrange("p h c -> p c h"),
                 in_=ps[PB][:, 0:NP * 16].rearrange("p (c h) -> p c h", h=16))
        for gq in range(8):
            P.mark("gd_g%d_%d" % (mt, gq))
            interleave([qk_chain(ps[PA], gq * 128, gq, TS[0], qT[:], 128 ** -0.5, ps[PN]),
                        qk_chain(ps[PT], 1024 + gq * 128, 8 + gq, TS[1], kT[:], 1.0, ps[PB])])
            pxb = ps[PX][:].bitcast(BF16)
            for c in range(NC):
                P.op("pe", "transpose", out=pxb[0:64, c * 128:(c + 1) * 128], in_=kT[:, c * 64:(c + 1) * 64],
                     identity=g.c["ident_b"][:])
            for hh in range(2):
                h = 2 * gq + hh
                bs = slice(hh * NC, (hh + 1) * NC)
                P.op("dve", "tensor_tensor", out=khat[:, bs, :],
                     in0=pxb[0:64, 0:NC * 128].rearrange("p (c d) -> p c d", d=128),
                     in1=tk_edl[:, h, :].unsqueeze(2).to_broadcast([64, NC, 128]), op=ALU.mult)
            for pr in range(NP):
                P.op("pe", "transpose", out=pxb[:, pr * 128:(pr + 1) * 128], in_=kT[:, pr * 128:(pr + 1) * 128],
                     identity=g.c["ident_b"][:])
            for hh in range(2):
                h = 2 * gq + hh
                P.op("dve", "tensor_tensor", out=kbd[:, hh * NP:(hh + 1) * NP, :],
                     in0=pxb[:, 0:NP * 128].rearrange("p (c d) -> p c d", d=128),
                     in1=tkp_be[:, h, :].unsqueeze(2).to_broadcast([128, NP, 128]), op=ALU.mult)
            for c in range(NC):
                P.op("pe", "matmul", out=ps[PQ][0:64, c * 64:(c + 1) * 64], lhsT=kT[:, c * 64:(c + 1) * 64],
                     rhs=qT[:, c * 64:(c + 1) * 64], start=True, stop=True)
            interleave([head_front(0, 2 * gq), head_front(1, 2 * gq + 1)])
            for hh in range(2):
                h = 2 * gq + hh
                bs = slice(hh * NC, (hh + 1) * NC)
                cs = slice(hh * GT, (hh + 1) * GT)
                vT = vT2[hh]
                for pr in range(NP):
                    P.op("pe", "transpose", out=pxb[:, pr * 128:(pr + 1) * 128], in_=vT[:, pr * 128:(pr + 1) * 128],
                         identity=g.c["ident_b"][:])
                P.op("dve", "tensor_tensor", out=vb[:, hh * NP:(hh + 1) * NP, :],
                     in0=pxb[:, 0:NP * 128].rearrange("p (c d) -> p c d", d=128),
                     in1=tkp_b[:, h, :].unsqueeze(2).to_broadcast([128, NP, 128]), op=ALU.mult)
                P.op("pe", "matmul", out=ps[PD][:, 0:GT], lhsT=sel[0:16, h * 128:(h + 1) * 128], rhs=r_beta[:], start=True, stop=True)
                P.op("pe", "matmul", out=ps[PD][:, GT:2 * GT], lhsT=sel[0:16, h * 128:(h + 1) * 128], rhs=r_d[:], start=True, stop=True)
                P.op("dve", "tensor_tensor", out=kbT[:], in0=kT[:], in1=ps[PD][:, 0:GT], op=ALU.mult)
                P.op("act", "activation", out=f_b[:], in_=ps[PD][:, GT:2 * GT], func=AF.Exp)
                P.op("pool", "tensor_tensor", out=qd[hh][:], in0=qT[:], in1=f_b[:], op=ALU.mult)
                P.op("act", "activation", out=edl4[hh][:], in_=ps[PD][:, GT:2 * GT].rearrange("p (c j) -> p c j", j=64)[:, :, 63],
                     func=AF.Exp)
                P.op("dve", "tensor_tensor", out=Et[:].rearrange("p (c j) -> p c j", j=64),
                     in0=ps[PD][0:64, GT:2 * GT].rearrange("p (c j) -> p c j", j=64),
                     in1=tk_d[:, h, :].unsqueeze(2).to_broadcast([64, NC, 64]), op=ALU.subtract)
                P.op("dve", "tensor_scalar", out=Et[:], in0=Et[:], scalar1=0.0, scalar2=None, op0=ALU.min)
                P.op("act", "activation", out=Et[:], in_=Et[:], func=AF.Exp)
                P.op("dve", "tensor_tensor", out=Ep[:].rearrange("p (c j) -> p c j", j=128),
                     in0=ps[PD][:, GT:2 * GT].rearrange("p (c j) -> p c j", j=128),
                     in1=tkp_d[:, h, :].unsqueeze(2).to_broadcast([128, NP, 128]), op=ALU.subtract)
                P.op("dve", "tensor_scalar", out=Ep[:], in0=Ep[:], scalar1=0.0, scalar2=None, op0=ALU.min)
                P.op("act", "activation", out=Ep[:], in_=Ep[:], func=AF.Exp)
                P.op("pool", "tensor_tensor", out=Ep[:], in0=Ep[:], in1=g.c["gd_maskSP"][:, :], op=ALU.mult)
                for pr in range(NP):
                    P.op("pe", "matmul", out=ps[PM][:, (hh * NP + pr) * 128:(hh * NP + pr + 1) * 128],
                         lhsT=kT[:, pr * 128:(pr + 1) * 128], rhs=kbT[:, pr * 128:(pr + 1) * 128], start=True, stop=True)
                P.op("dve", "scalar_tensor_tensor", out=Mk[0][:, cs], in0=ps[PM][:, cs], scalar=-1.0, in1=Ep[:],
                     op0=ALU.mult, op1=ALU.mult)
                P.op("pool", "tensor_tensor", out=Em[:], in0=Et[:], in1=g.c["gd_maskI"][:, :], op=ALU.mult)
                P.op("dve", "tensor_tensor", out=qkM[:, cs], in0=ps[PQ][0:64, 0:GT], in1=Em[:], op=ALU.mult)
            for b in range(4):
                P.op("pe", "transpose", out=ps[PX][:, b * 128:(b + 1) * 128], in_=Mk[0][:, b * 128:(b + 1) * 128],
                     identity=g.c["ident_f"][:, :])
            P.op("act", "activation", out=Nk[0][:], in_=ps[PX][:, 0:512], func=AF.Copy)
            P.op("pool", "tensor_tensor", out=Tk[0][:], in0=Mk[0][:], in1=identI[:], op=ALU.add)
            for st_ in range(5):
                cur, nxt = st_ % 2, (st_ + 1) % 2
                if st_ < 4:
                    for b in range(4):
                        bc = slice(b * 128, (b + 1) * 128)
                        P.op("pe", "matmul", out=ps[PM][:, bc], lhsT=Nk[cur][:, bc], rhs=Mk[cur][:, bc], start=True, stop=True)
                    P.op("dve", "tensor_copy", out=Mk[nxt][:], in_=ps[PM][:, :])
                for b in range(4):
                    bc = slice(b * 128, (b + 1) * 128)
                    P.op("pe", "matmul", out=ps[PN][:, bc], lhsT=Mk[cur][:, bc], rhs=Nk[cur][:, bc], start=True, stop=True)
                P.op("act", "activation", out=Nk[nxt][:], in_=ps[PN][:, :], func=AF.Copy)
                for b in range(4):
                    bc = slice(b * 128, (b + 1) * 128)
                    P.op("pe", "matmul", out=ps[PT][:, bc], lhsT=Nk[nxt][:, bc], rhs=Tk[cur][:, bc], start=True, stop=True)
                P.op("dve", "tensor_tensor", out=Tk[nxt][:], in0=ps[PT][:, :], in1=Tk[cur][:], op=ALU.add)
            P.op("pool", "tensor_copy", out=Tbf[:], in_=Tk[1][:])
            TT = Tbf
            for b in range(4):
                bc = slice(b * 128, (b + 1) * 128)
                P.op("pe", "matmul", out=ps[PN][:, bc], lhsT=kbd[:, b, :], rhs=TT[:, bc], start=True, stop=True)
            P.op("act", "activation", out=wTn[:], in_=ps[PN][:], func=AF.Copy, scale=-1.0)
            for c in range(NC):
                for hh in range(2):
                    h = 2 * gq + hh
                    b = hh * NC + c
                    bc = slice(b * 64, (b + 1) * 64)
                    v_ = vn2[hh][c % 2]
                    pv = ps[(PT, PX)[hh]]
                    pst_ = ps[(PM, PN)[hh]]
                    P.op("pe", "matmul", out=pv[0:64, 0:128], lhsT=TT[:, bc], rhs=vb[:, hh * NP + c // 2, :], start=True, stop=False)
                    P.op("pe", "matmul", out=pv[0:64, 0:128], lhsT=wTn[:, bc], rhs=Sb[h][:], start=False, stop=True)
                    P.op("act", "activation", out=v_[:], in_=pv[0:64, 0:128], func=AF.Copy)
                    P.op("pe", "matmul", out=ps[PO][:, bc], lhsT=Sb[h][:], rhs=qd[hh][:, c * 64:(c + 1) * 64], start=True, stop=False)
                    P.op("pe", "matmul", out=ps[PO][:, bc], lhsT=v_[:], rhs=qkM[:, bc], start=False, stop=True)
                    P.op("pe", "matmul", out=pst_[:, 0:128], lhsT=khat[:, b, :], rhs=v_[:], start=True, stop=True)
                    P.op("dve", "scalar_tensor_tensor", out=S[h][:], in0=S[h][:], scalar=edl4[hh][:, c:c + 1], in1=pst_[:, 0:128],
                         op0=ALU.mult, op1=ALU.add)
                    P.op("pool", "tensor_copy", out=Sb[h][:], in_=S[h][:])
            for hh in range(2):
                h = 2 * gq + hh
                cs = slice(hh * GT, (hh + 1) * GT)
                P.op("act", "activation", out=f_a[:], in_=ps[PO][:, cs], func=AF.Square)
                P.op("pe", "matmul", out=ps[PB][:, 0:GT], lhsT=ones_m[:], rhs=f_a[:], start=True, stop=True)
                P.op("act", "activation", out=f_c[:], in_=ps[PB][:, 0:GT], func=AF.Sqrt, bias=epsc[:, 0:1])
                P.op("dve", "reciprocal", out=f_c[:], in_=f_c[:])
                P.op("dve", "scalar_tensor_tensor", out=f_b[:], in0=ps[PO][:, cs], scalar=ong[:, 0:1], in1=f_c[:],
                     op0=ALU.mult, op1=ALU.mult)
                P.op("pool", "tensor_tensor", out=Sub(og[:, h, :], h), in0=f_b[:], in1=zs[hh][:], op=ALU.mult)
        P.mark("epilogue%d" % mt)
        epilogue(g, mt, src, srctag, dst, og, 16, w_out, xb, ps[PA], ps[PB], mtl=GT)


def core_inputs(b, T, inputs, consts):
    f = np.float32
    m = {
        "x": np.ascontiguousarray(inputs["x"][b, :T]).astype(f, copy=False),
        "c8": np.ascontiguousarray(inputs["c"][b].reshape(8, 128)),
        "positions": np.ascontiguousarray(inputs["positions"][b, :T].reshape(1, T)).astype(np.int32, copy=False),
        "hgrn_lb": np.ascontiguousarray(inputs["hgrn_lb"].reshape(32, 128)),
        "ada_w": inputs["ada_w"],
        "ada_b": np.ascontiguousarray(inputs["ada_b"].reshape(4, 24, 128)),
        "ada_b_row": np.ascontiguousarray(inputs["ada_b"].reshape(4, 1, 3 * D)),
        "norm_g": np.ascontiguousarray(inputs["norm_g"].reshape(4, 8, 128)),
        "hg_in_w": inputs["hg_in_w"],
        "hg_out_w": inputs["hg_out_w"],
        "hg_onorm": np.ascontiguousarray(inputs["hg_onorm"].reshape(-1, 1, 128)),
        "sw_in_w": inputs["sw_in_w"],
        "sw_out_w": inputs["sw_out_w"],
        "sw_qnorm": inputs["sw_qnorm"],
        "sw_knorm": inputs["sw_knorm"],
        "sw_sinks": inputs["sw_sinks"],
        "gd_in_w": inputs["gd_in_w"],
        "gd_out_w": inputs["gd_out_w"],
        "gd_conv_w": np.ascontiguousarray(inputs["gd_conv_w"][0]),
        "gd_a_log": np.ascontiguousarray(inputs["gd_a_log"].reshape(16, 1)),
        "gd_dt_bias": np.ascontiguousarray(inputs["gd_dt_bias"].reshape(16, 1)),
        "gd_onorm": np.ascontiguousarray(inputs["gd_onorm"].reshape(1, 128)),
    }
    m.update(consts)
    return m


def run_layers(inputs, layers, T, ncores=2, trace=False):
    inputs = {k: np.asarray(v) for k, v in inputs.items()}
    consts = make_consts()
    nc = build(T, layers)
    in_maps = [core_inputs(b, T, inputs, consts) for b in range(ncores)]
    res = run_bass_kernel_spmd(nc, in_maps, core_ids=list(range(ncores)), trace=trace)
    out = np.stack([np.asarray(r["out"]) for r in res.results], axis=0)
    return out, res


def kernel(**inputs):
    T = inputs["x"].shape[1]
    out, _ = run_layers(inputs, [0, 1, 2, 3], T)
    return out.astype(np.float32, copy=False)
```

```python
import numpy as np
import ml_dtypes
from contextlib import ExitStack

import concourse.bass as bass
import concourse.mybir as mybir
from concourse.bass_utils import run_bass_kernel_spmd

F32 = mybir.dt.float32
BF16 = mybir.dt.bfloat16
I32 = mybir.dt.int32
U32 = mybir.dt.uint32
AF = mybir.ActivationFunctionType
ALU = mybir.AluOpType
AX = mybir.AxisListType

D = 1024
KT = 8
EPS = 1e-6
MT = 512
NDMASEM = 40
SW = 1024


class Sub:
    def __init__(self, ap, tag):
        self.ap = ap
        self.tag = tag


class Prog:
    ENGS = ("pe", "act", "dve", "pool", "sp")
    WRITE_KW = ("out", "accum_out", "ap")

    def __init__(self, nc):
        self.nc = nc
        self.ins = []
        self.writers = {}
        self.readers = {}
        self.dma_sem_of = {}
        self.dma_tot = []
        self.same_engine_sync = True
        self._bar_pending = set()
        self._bar_ids = []
        self.marks = {}
        self.psum_excl = True

    def _key(self, v):
        if isinstance(v, Sub):
            return (v.ap.tensor.name, v.tag), v.ap
        return (v.tensor.name, None), v

    def op(self, eng, meth, **kw):
        reads, writes, real = [], [], {}
        for k, v in kw.items():
            if isinstance(v, (Sub, bass.AP)):
                key, ap = self._key(v)
                real[k] = ap
                (writes if k in self.WRITE_KW else reads).append(key)
            else:
                real[k] = v
        i = len(self.ins)
        deps = set()
        if eng in self._bar_pending:
            deps.update(self._bar_ids)
            self._bar_pending.discard(eng)
        for key in reads:
            deps.update(self.writers.get(key, ()))
            if self.psum_excl and key[0].startswith("ps"):
                deps.update(self.readers.get(key, ()))
        for key in writes:
            deps.update(self.writers.get(key, ()))
            deps.update(self.readers.get(key, ()))
        for key in reads:
            self.readers.setdefault(key, []).append(i)
        for key in writes:
            self.writers[key] = [i]
            self.readers[key] = []
        dma = meth == "dma_start"
        rec = dict(eng=eng, meth=meth, kw=real, deps=deps, dma=dma, inc=False, cnt=0, sem=None, tgt=0)
        if dma:
            sbside = None
            for k in ("out", "in_"):
                ap = real[k]
                if "SB" in type(ap.tensor).__name__:
                    sbside = ap.tensor.name
            assert sbside is not None, "dma needs an SBUF side"
            s = self.dma_sem_of.setdefault(sbside, len(self.dma_sem_of))
            if s >= len(self.dma_tot):
                self.dma_tot.append(0)
            self.dma_tot[s] += 16
            rec["sem"] = s
            rec["tgt"] = self.dma_tot[s]
        self.ins.append(rec)
        return i

    def mark(self, name):
        self.marks[name] = len(self.ins)

    def barrier(self):
        last = {}
        lastdma = {}
        for j, r in enumerate(self.ins):
            if r["dma"]:
                lastdma[r["sem"]] = j
            else:
                last[r["eng"]] = j
        self._bar_ids = list(last.values()) + list(lastdma.values())
        self._bar_pending = set(self.ENGS)
        self.dma_sem_of = {}

    def setup(self, stack):
        nc = self.nc
        self.sems = {e: stack.enter_context(nc.semaphore("s_" + e)) for e in self.ENGS}
        self.dsem = [stack.enter_context(nc.semaphore("d_%d" % k)) for k in range(NDMASEM)]
        self.waited = {e: {p: -1 for p in self.ENGS} for e in self.ENGS}
        self.dma_waited = {e: set() for e in self.ENGS}
        self.cnt = {e: 0 for e in self.ENGS}
        self.done = 0

    def emit(self, final=False):
        nc = self.nc
        import os as _os
        ins = self.ins
        lo, hi = self.done, len(ins)
        stop = _os.environ.get("KSTOP")
        if stop:
            n = int(stop) if stop.isdigit() else self.marks.get(stop, hi)
            hi = max(lo, min(hi, n))
        assert len(self.dma_tot) <= NDMASEM, len(self.dma_tot)
        waited, dma_waited = self.waited, self.dma_waited
        for i in range(lo, hi):
            r = ins[i]
            e = r["eng"]
            need = {}
            dm = []
            for j in r["deps"]:
                p = ins[j]
                if p["dma"]:
                    if j not in dma_waited[e]:
                        dm.append(j)
                else:
                    pe_ = p["eng"]
                    if pe_ == e and not r["dma"]:
                        if not (self.same_engine_sync and e != "pe"):
                            continue
                    if j > waited[e][pe_]:
                        need[pe_] = max(need.get(pe_, -1), j)
            for pe_, j in list(need.items()):
                if j < lo and not ins[j]["inc"]:
                    jj = j
                    while jj < lo and not (ins[jj]["eng"] == pe_ and ins[jj]["inc"] and not ins[jj]["dma"]):
                        jj += 1
                    assert jj < lo, "no increment available for cross-block dependency"
                    need[pe_] = jj
            r["w_cmp"] = need
            r["w_dma"] = dm
            for pe_, j in need.items():
                waited[e][pe_] = max(waited[e][pe_], j)
                ins[j]["inc"] = True
            for j in dm:
                dma_waited[e].add(j)
        lastc = {}
        for i in range(lo, hi):
            if not ins[i]["dma"]:
                lastc[ins[i]["eng"]] = i
        for i in lastc.values():
            ins[i]["inc"] = True
        for i in range(lo, hi):
            r = ins[i]
            if r["inc"] and not r["dma"]:
                self.cnt[r["eng"]] += 1
                r["cnt"] = self.cnt[r["eng"]]
        per = {e: [] for e in self.ENGS}
        for i in range(lo, hi):
            per[ins[i]["eng"]].append(ins[i])
        sems, dsem = self.sems, self.dsem
        dma_tot = [0] * len(self.dma_tot)
        for r in ins[:hi]:
            if r["dma"]:
                dma_tot[r["sem"]] = max(dma_tot[r["sem"]], r["tgt"])
        self.done = len(ins)

        def run(eng_obj, lst, fin=False):
            for r in lst:
                for pe_, j in r["w_cmp"].items():
                    eng_obj.wait_ge(sems[pe_], ins[j]["cnt"])
                for j in r["w_dma"]:
                    eng_obj.wait_ge(dsem[ins[j]["sem"]], ins[j]["tgt"])
                inst = getattr(eng_obj, r["meth"])(**r["kw"])
                if r["dma"]:
                    inst.then_inc(dsem[r["sem"]], 16)
                elif r["inc"]:
                    inst.then_inc(sems[r["eng"]], 1)
            if fin:
                for k in range(len(dma_tot)):
                    if dma_tot[k]:
                        eng_obj.wait_ge(dsem[k], dma_tot[k])

        with nc.Block() as block:
            @block.tensor
            def _(eng):
                run(eng, per["pe"])

            @block.scalar
            def _(eng):
                run(eng, per["act"])

            @block.vector
            def _(eng):
                run(eng, per["dve"])

            @block.gpsimd
            def _(eng):
                run(eng, per["pool"])

            @block.sync
            def _(eng):
                run(eng, per["sp"], fin=final)


def make_consts():
    c = {}
    c["ident_f"] = np.eye(128, dtype=np.float32)
    c["ident_b"] = np.eye(128, dtype=np.float32).astype(ml_dtypes.bfloat16)
    c["ones_f"] = np.ones((128, 128), np.float32)
    for ch in (32, 64):
        s = np.arange(ch)[:, None]
        t = np.arange(MT)[None, :] % ch
        c["hg_maskP%d" % ch] = (s <= t).astype(np.uint32)
        rm = np.ones((128, MT), np.float32)
        rm[:, ::ch] = 0.0
        c["resetmask%d" % ch] = rm
    k = np.arange(128)[:, None]
    q = np.arange(MT)[None, :] % 128
    c["sw_mask_cur"] = (k <= q).astype(np.float32).astype(ml_dtypes.bfloat16)
    c["sw_mask_prev"] = (k > q).astype(np.float32).astype(ml_dtypes.bfloat16)
    rot = np.zeros((128, 128), np.float32)
    for blk in (0, 64):
        for m in range(32):
            rot[blk + m + 32, blk + m] = -1.0
            rot[blk + m, blk + m + 32] = 1.0
    c["sw_rotT"] = rot
    ob = np.zeros((128, 128), np.float32)
    ob[0:64, 0:64] = 1.0 / 64
    ob[64:128, 64:128] = 1.0 / 64
    c["sw_ones_blk"] = ob
    inv = (10000.0 ** (-np.arange(0, 64, 2, dtype=np.float32) / 64)).astype(np.float32)
    c["sw_invfreq"] = np.tile(inv, 4).reshape(128, 1).astype(np.float32)
    jj = np.arange(64)[:, None]
    ii = np.arange(256)[None, :] % 64
    c["gd_maskI"] = (jj <= ii).astype(np.float32)
    rm = np.ones((16, 256), np.float32)
    rm[:, ::64] = 0.0
    c["gd_rm"] = rm
    j2 = np.arange(128)[:, None]
    i2 = np.arange(256)[None, :] % 128
    c["gd_maskSP"] = ((j2 // 64 == i2 // 64) & (j2 % 64 < i2 % 64)).astype(np.float32)
    sel = np.zeros((16, 16, 128), np.float32)
    for h in range(16):
        sel[h, h, :] = 1.0
    c["gd_sel"] = sel.reshape(16, 16 * 128)
    return c


CONST_SPECS = {
    "gd_maskI": ([64, 256], F32),
    "gd_rm": ([16, 256], F32),
    "gd_maskSP": ([128, 256], F32),
    "gd_sel": ([16, 2048], F32),
    "sw_mask_cur": ([128, MT], BF16),
    "sw_mask_prev": ([128, MT], BF16),
    "sw_rotT": ([128, 128], F32),
    "sw_ones_blk": ([128, 128], F32),
    "sw_invfreq": ([128, 1], F32),
    "ident_f": ([128, 128], F32),
    "ident_b": ([128, 128], BF16),
    "ones_f": ([128, 128], F32),
    "hg_maskP32": ([32, MT], U32),
    "hg_maskP64": ([64, MT], U32),
    "resetmask32": ([128, MT], F32),
    "resetmask64": ([128, MT], F32),
}


COMMON_CONSTS = ("ident_f", "ident_b", "ones_f")
HG_CONSTS = ("hg_maskP32", "hg_maskP64", "resetmask32", "resetmask64")
GD_CONSTS = ("gd_maskI", "gd_sel", "gd_rm", "gd_maskSP")
SW_CONSTS = ("sw_mask_cur", "sw_mask_prev", "sw_rotT", "sw_ones_blk", "sw_invfreq")


class Ctx:
    pass


def load_consts(g, names, pfx=""):
    for k in names:
        shape, dt = CONST_SPECS[k]
        g.c[k] = g.sb(pfx + "c_" + k, shape, dt)
        g.P.op("sp", "dma_start", out=g.c[k][:], in_=g.dr[k][:, :])


def build(T, layers, n_hg=2):
    assert T % MT == 0
    nc = bass.Bass("TRN2", target_bir_lowering=False)
    P = Prog(nc)
    g = Ctx()
    g.nc, g.P, g.T = nc, P, T
    dr = {}

    def din(name, shape, dt=F32):
        dr[name] = nc.dram_tensor(name, shape, dt, kind="ExternalInput").ap()
        return dr[name]

    din("x", [T, D])
    din("c8", [8, 128])
    din("positions", [1, T], I32)
    din("hgrn_lb", [32, 128])
    din("ada_w", [4, D, 3 * D])
    din("ada_b", [4, 24, 128])
    din("ada_b_row", [4, 1, 3 * D])
    din("norm_g", [4, 8, 128])
    din("hg_in_w", [n_hg, D, 4 * D])
    din("hg_out_w", [n_hg, D, D])
    din("hg_onorm", [n_hg, 1, 128])
    din("sw_in_w", [1, D, 2560])
    din("sw_out_w", [1, D, D])
    din("sw_qnorm", [1, 64])
    din("sw_knorm", [1, 64])
    din("sw_sinks", [1, 16])
    din("gd_in_w", [1, D, 6176])
    din("gd_out_w", [1, 2 * D, D])
    din("gd_conv_w", [4, 4096])
    din("gd_a_log", [16, 1])
    din("gd_dt_bias", [16, 1])
    din("gd_onorm", [1, 128])
    for k, (shape, dt) in CONST_SPECS.items():
        din(k, shape, dt)
    out = nc.dram_tensor("out", [T, D], F32, kind="ExternalOutput").ap()
    scr = [nc.dram_tensor("xs%d" % i, [T, D], F32, kind="Internal").ap() for i in range(2)]
    g.gd_wscr = nc.dram_tensor("gd_wscr", [32, 128, 1024], BF16, kind="Internal").ap()
    g.dr = dr

    with ExitStack() as st:
        def sbg(name, shape, dt=F32):
            return st.enter_context(nc.sbuf_tensor(name, shape, dt))

        P.setup(st)
        g.sb = sbg
        g.psum = [st.enter_context(nc.psum_tensor("ps%d" % i, [128, 512], F32)) for i in range(8)]
        g.c = {}
        load_consts(g, COMMON_CONSTS)
        g.stage = [sbg("stage%d" % i, [128, SW], F32) for i in range(2)]
        P.mark("prep_common")
        prep_common(g)
        P.mark("after_prep_common")
        src = dr["x"]
        srctag = lambda ap, tag: ap
        for li, layer in enumerate(layers):
            dst = out if li == len(layers) - 1 else scr[li % 2]
            kind = layer % 3
            with ExitStack() as lst:
                g.sb = lambda name, shape, dt=F32: lst.enter_context(nc.sbuf_tensor(name, shape, dt))
                if kind == 0:
                    load_consts(g, HG_CONSTS, "L%d" % layer)
                    hgrn2_layer(g, layer, src, srctag, dst)
                elif kind == 1:
                    load_consts(g, SW_CONSTS, "L%d" % layer)
                    swa_layer(g, layer, src, srctag, dst)
                else:
                    load_consts(g, GD_CONSTS, "L%d" % layer)
                    gdn_layer(g, layer, src, srctag, dst)
                P.barrier()
                P.emit(final=(li == len(layers) - 1))
            src = dst
            srctag = Sub
    return nc


def transpose_small(g, dst, src_rows, nrows, ps):
    P = g.P
    P.op("pe", "transpose", out=ps[:, 0:nrows], in_=src_rows, identity=g.c["ident_f"][0:nrows, 0:nrows])
    P.op("dve", "tensor_copy", out=dst, in_=ps[:, 0:nrows])


def W(ap, k):
    return Sub(ap, ("w", k))


def prep_common(g):
    P, sb, dr = g.P, g.sb, g.dr
    ps = g.psum[0]
    g.rows_a = sb("rows_a", [32, 128], F32)
    g.rows_g = sb("rows_g", [8, 128], F32)
    rows_c = sb("rows_c", [8, 128], F32)
    g.c_col = sb("c_col", [128, 8], F32)
    P.op("sp", "dma_start", out=rows_c[:], in_=dr["c8"][:, :])
    transpose_small(g, g.c_col[:], rows_c[:], 8, ps)
    P.op("sp", "dma_start", out=g.rows_a[:], in_=dr["hgrn_lb"][:, :])
    lbT = sb("lbT", [128, 32], F32)
    transpose_small(g, lbT[:], g.rows_a[:], 32, g.psum[1])
    ex = sb("lb_exp", [128, 32], F32)
    P.op("act", "activation", out=ex[:], in_=lbT[:], func=AF.Exp)
    den = sb("lb_den", [128, 8], F32)
    P.op("dve", "tensor_tensor", out=den[:], in0=ex[:, 0:8], in1=ex[:, 8:16], op=ALU.add)
    P.op("dve", "tensor_tensor", out=den[:], in0=den[:], in1=ex[:, 16:24], op=ALU.add)
    P.op("dve", "tensor_tensor", out=den[:], in0=den[:], in1=ex[:, 24:32], op=ALU.add)
    rden = sb("lb_rden", [128, 8], F32)
    P.op("dve", "reciprocal", out=rden[:], in_=den[:])
    g.lb = {}
    lb0 = sb("lb_l0", [128, 8], F32)
    P.op("dve", "memset", ap=lb0[:], constant=0.0)
    g.lb[0] = lb0
    num = sb("lb_num", [128, 8], F32)
    P.op("dve", "tensor_tensor", out=num[:], in0=ex[:, 8:16], in1=ex[:, 16:24], op=ALU.add)
    P.op("dve", "tensor_tensor", out=num[:], in0=num[:], in1=ex[:, 24:32], op=ALU.add)
    lb3 = sb("lb_l3", [128, 8], F32)
    P.op("dve", "tensor_tensor", out=lb3[:], in0=num[:], in1=rden[:], op=ALU.mult)
    g.lb[3] = lb3
    g.modT = sb("modT", [128, 16], F32)
    g.gsT = sb("gsT", [128, 8], F32)
    g.gate_row = sb("gate_row", [1, D], F32)
    g.gate_b = sb("gate_b", [128, D], F32)
    g.adab_T = sb("adab_T", [128, 24], F32)
    g.ng_T = sb("ng_T", [128, 8], F32)


def prep_layer(g, layer):
    P, dr = g.P, g.dr
    ps = g.psum[0]
    P.op("sp", "dma_start", out=g.rows_a[0:24, :], in_=dr["ada_b"][layer])
    transpose_small(g, g.adab_T[:], g.rows_a[0:24, :], 24, g.psum[1])
    P.op("sp", "dma_start", out=g.rows_g[:], in_=dr["norm_g"][layer])
    transpose_small(g, g.ng_T[:], g.rows_g[:], 8, g.psum[2])
    aw = dr["ada_w"][layer].rearrange("(k p) n -> p k n", p=128)
    for j in range(16):
        t = g.stage[j % 2][:, 0:1024].rearrange("p (k n) -> p k n", k=KT)
        P.op("sp", "dma_start", out=t, in_=aw[:, :, j * 128:(j + 1) * 128])
        for k in range(KT):
            P.op("pe", "matmul", out=ps[:, j:j + 1], lhsT=t[:, k, :], rhs=g.c_col[:, k:k + 1],
                 start=(k == 0), stop=(k == KT - 1))
    P.op("dve", "tensor_tensor", out=g.modT[:], in0=ps[:, 0:16], in1=g.adab_T[:, 0:16], op=ALU.add)
    P.op("dve", "scalar_tensor_tensor", out=g.gsT[:], in0=g.modT[:, 8:16], scalar=1.0, in1=g.ng_T[:],
         op0=ALU.add, op1=ALU.mult)
    P.op("sp", "dma_start", out=g.gate_row[:], in_=dr["ada_b_row"][layer][:, 2 * D:3 * D])
    for cg in range(8):
        psg = g.psum[3 + cg % 2]
        t = g.stage[cg % 2][:, 0:1024].rearrange("p (k n) -> p k n", k=KT)
        P.op("sp", "dma_start", out=t, in_=aw[:, :, 2 * D + cg * 128:2 * D + (cg + 1) * 128])
        for k in range(KT):
            P.op("pe", "matmul", out=psg[0:1, 0:128], lhsT=g.c_col[:, k:k + 1], rhs=t[:, k, :],
                 start=(k == 0), stop=(k == KT - 1))
        P.op("dve", "tensor_tensor", out=g.gate_row[0:1, cg * 128:(cg + 1) * 128], in0=psg[0:1, 0:128],
             in1=g.gate_row[0:1, cg * 128:(cg + 1) * 128], op=ALU.add)
    for cg in range(2):
        psg = g.psum[5 + cg]
        P.op("pe", "matmul", out=psg[:], lhsT=g.c["ones_f"][0:1, :], rhs=g.gate_row[0:1, cg * 512:(cg + 1) * 512],
             start=True, stop=True)
        P.op("dve", "tensor_copy", out=g.gate_b[:, cg * 512:(cg + 1) * 512], in_=psg[:])


def load_weight_bf16(g, dst3, wsrc, ncols, colscale=None, kbase=0):
    P = g.P
    nk = wsrc.shape[0] // 128
    i = 0
    for k in range(nk):
        for c0 in range(0, ncols, SW):
            w = min(SW, ncols - c0)
            stg = g.stage[i % 2]
            P.op("sp", "dma_start", out=stg[:, 0:w], in_=wsrc[k * 128:(k + 1) * 128, c0:c0 + w])
            eng = ("dve", "pool")[i % 2]
            dstv = W(dst3[:, k, c0:c0 + w], kbase + k)
            if colscale is None:
                P.op(eng, "tensor_copy", out=dstv, in_=stg[:, 0:w])
            else:
                P.op(eng, "tensor_tensor", out=dstv, in0=stg[:, 0:w], in1=colscale[:, c0:c0 + w], op=ALU.mult)
            i += 1


def prologue(g, mt, src, srctag, hT, xb, xnb, ss, rstd, mtl=MT):
    P = g.P
    P.op("pool", "memset", ap=ss[:], constant=0.0)
    for blk in range(mtl // 128):
        t0 = mt * mtl + blk * 128
        x = xb[blk % len(xb)]
        xn = xnb[blk % len(xnb)]
        P.op("sp", "dma_start", out=x[:], in_=srctag(src[t0:t0 + 128, :], t0 // 128))
        P.op("act", "activation", out=xn[:], in_=x[:], func=AF.Square, accum_out=ss[:, blk:blk + 1])
        P.op("dve", "tensor_scalar", out=rstd[:, blk:blk + 1], in0=ss[:, blk:blk + 1], scalar1=1.0 / D, scalar2=EPS,
             op0=ALU.mult, op1=ALU.add)
        P.op("act", "activation", out=rstd[:, blk:blk + 1], in_=rstd[:, blk:blk + 1], func=AF.Sqrt)
        P.op("dve", "reciprocal", out=rstd[:, blk:blk + 1], in_=rstd[:, blk:blk + 1])
        P.op("dve", "tensor_scalar", out=xn[:], in0=x[:], scalar1=rstd[:, blk:blk + 1], scalar2=None, op0=ALU.mult)
        for kg in range(2):
            ps = g.psum[kg]
            psb = ps[:].bitcast(BF16)
            for kk in range(4):
                k = kg * 4 + kk
                P.op("pe", "transpose", out=psb[:, kk * 128:(kk + 1) * 128], in_=xn[:, k * 128:(k + 1) * 128],
                     identity=g.c["ident_b"][:])
            for kk in range(4):
                k = kg * 4 + kk
                dstv = Sub(hT[:, k, blk * 128:(blk + 1) * 128], k)
                srcv = psb[:, kk * 128:(kk + 1) * 128]
                if k % 2 == 0:
                    P.op("dve", "tensor_scalar", out=dstv, in0=srcv, scalar1=g.gsT[:, k:k + 1],
                         scalar2=g.modT[:, k:k + 1], op0=ALU.mult, op1=ALU.add)
                else:
                    P.op("act", "activation", out=dstv, in_=srcv, func=AF.Identity, scale=g.gsT[:, k:k + 1],
                         bias=g.modT[:, k:k + 1])


def epilogue(g, mt, src, srctag, dst, og, nk, w_out, xb, psa, psb_, ogtag=True, mtl=MT):
    P = g.P
    for blk in range(mtl // 128):
        t0 = mt * mtl + blk * 128
        x = xb[blk % len(xb)]
        P.op("sp", "dma_start", out=x[:], in_=srctag(src[t0:t0 + 128, :], t0 // 128))
        for cg in range(2):
            pst = (psa, psb_)[cg]
            for k in range(nk):
                ogv = og[:, k, blk * 128:(blk + 1) * 128]
                P.op("pe", "matmul", out=pst[:], lhsT=Sub(ogv, k) if ogtag else ogv,
                     rhs=W(w_out[:, k, cg * 512:(cg + 1) * 512], 100 + k), start=(k == 0), stop=(k == nk - 1))
            P.op("dve", "tensor_tensor", out=x[:, cg * 512:(cg + 1) * 512], in0=pst[:], in1=x[:, cg * 512:(cg + 1) * 512],
                 op=ALU.add)
        P.op("sp", "dma_start", out=Sub(dst[t0:t0 + 128, :], t0 // 128), in_=x[:])


def hgrn2_layer(g, layer, src, srctag, dst):
    P, sb, dr, nc = g.P, g.sb, g.dr, g.nc
    j = layer // 3
    T = g.T
    CH = 32 if layer == 0 else 64
    NCH = MT // CH
    P.mark("prep_layer")
    prep_layer(g, layer)
    P.mark("after_prep_layer")
    L = "L%d_" % layer
    g.warena = sb(L + "warena", [128, KT * 5120], BF16)
    w_in = g.warena[:, 0:KT * 4096].rearrange("p (k n) -> p k n", k=KT)
    w_out = g.warena[:, KT * 4096:KT * 4096 + KT * 1024].rearrange("p (k n) -> p k n", k=KT)
    load_weight_bf16(g, w_in, dr["hg_in_w"][j], 4096)
    load_weight_bf16(g, w_out, dr["hg_out_w"][j], 1024, colscale=g.gate_b, kbase=100)
    lb = g.lb[layer]
    oml = sb(L + "oml", [128, 8])
    noml = sb(L + "noml", [128, 8])
    P.op("dve", "tensor_scalar", out=oml[:], in0=lb[:], scalar1=-1.0, scalar2=1.0, op0=ALU.mult, op1=ALU.add)
    P.op("dve", "tensor_scalar", out=noml[:], in0=oml[:], scalar1=-1.0, scalar2=None, op0=ALU.mult)
    onr = sb(L + "onr", [1, 128])
    ong = sb(L + "ong", [128, 1])
    P.op("sp", "dma_start", out=onr[:], in_=dr["hg_onorm"][j])
    transpose_small(g, ong[:], onr[0:1, :], 1, g.psum[7])
    ones_m = sb(L + "ones_m", [128, 128])
    epsc = sb(L + "epsc", [128, 1])
    P.op("dve", "memset", ap=epsc[:], constant=EPS)
    P.op("dve", "tensor_scalar", out=ones_m[:], in0=g.c["ones_f"][:], scalar1=1.0 / 128, scalar2=None, op0=ALU.mult)

    xb = [sb(L + "xb%d" % i, [128, D]) for i in range(2)]
    xnb = [sb(L + "xn%d" % i, [128, D], BF16) for i in range(2)]
    ss = sb(L + "ss", [128, 4])
    rstd = sb(L + "rstd", [128, 4])
    hT = sb(L + "hT", [128, KT, MT], BF16)
    t_q = sb(L + "t_q", [128, MT])
    t_f = sb(L + "t_f", [128, MT])
    t_k = sb(L + "t_k", [128, MT])
    t_l = sb(L + "t_l", [128, MT])
    t_e1 = sb(L + "t_e1", [128, MT])
    t_e2 = sb(L + "t_e2", [128, MT])
    qT = [sb(L + "qT%d" % i, [128, MT], BF16) for i in range(2)]
    kT = [sb(L + "kT%d" % i, [128, MT], BF16) for i in range(2)]
    ktok = [sb(L + "ktok%d" % i, [CH, NCH, 128], BF16) for i in range(2)]
    vtok = [sb(L + "vtok%d" % i, [CH, NCH, 128], BF16) for i in range(2)]
    vT = sb(L + "vT", [128, MT], BF16)
    zs = [sb(L + "zs%d" % i, [128, MT], BF16) for i in range(2)]
    er = sb(L + "er", [128, NCH])
    ebl = sb(L + "ebl", [128, NCH])
    eblr = sb(L + "eblr", [128, NCH])
    atm = [sb(L + "atm%d" % i, [CH, MT], BF16) for i in range(2)]
    for i in range(2):
        P.op("pool", "memset", ap=atm[i][:], constant=0.0)
    S = [sb(L + "S%d" % h, [128, 128]) for h in range(8)]
    srb = [sb(L + "srb%d" % i, [128, 128], BF16) for i in range(2)]
    stmp = [sb(L + "stmp%d" % i, [128, 128]) for i in range(2)]
    osq = sb(L + "osq", [128, MT])
    orstd = sb(L + "orstd", [128, MT])
    otmp = sb(L + "otmp", [128, MT])
    og = sb(L + "og", [128, 8, MT], BF16)
    for h in range(8):
        P.op("pool", "memset", ap=S[h][:], constant=0.0)

    ps = g.psum
    PS_Q, PS_F, PS_Z, PS_V, PS_A, PS_O, PS_S, PS_X = range(8)
    PS_O2, PS_S2 = PS_Q, PS_F
    khT = [sb(L + "khT%d" % i, [128, MT], BF16) for i in range(2)]
    TG = [(t_q, t_f, t_k, t_l, t_e1, t_e2),
          tuple(sb(L + "tg1_%d" % i, [128, MT]) for i in range(6))]
    vT2 = [vT, sb(L + "vT_b", [128, MT], BF16)]
    er2 = [er, sb(L + "er_b", [128, NCH])]
    ebl2 = [ebl, sb(L + "ebl_b", [128, NCH])]
    srb2 = [srb, [sb(L + "srbB%d" % i, [128, 128], BF16) for i in range(2)]]
    P.mark("after_wload")
    RI = CH // 2 - 1
    for mt in range(T // MT):
        prologue(g, mt, src, srctag, hT, xb, xnb, ss, rstd)
        P.mark("after_prologue%d" % mt)
        for hp in range(4):
            def gates_chain(pp, h):
                tq, tB, tk, tD, te1, te2 = TG[pp]
                banks = ((PS_Q, PS_F, PS_V, PS_Z), (PS_A, PS_O, PS_S, PS_X))[pp]
                for (pst, c0) in ((banks[0], h * 128), (banks[1], 1024 + h * 128), (banks[2], 2048 + h * 128),
                                  (banks[3], 3072 + h * 128)):
                    for k in range(KT):
                        P.op("pe", "matmul", out=ps[pst][:], lhsT=W(w_in[:, k, c0:c0 + 128], k), rhs=Sub(hT[:, k, :], k),
                             start=(k == 0), stop=(k == KT - 1))
                yield
                P.op("act", "activation", out=tq[:], in_=ps[banks[0]][:], func=AF.Silu)
                yield
                P.op("act", "activation", out=tB[:], in_=ps[banks[1]][:], func=AF.Sigmoid)
                yield
                P.op("act", "activation", out=zs[pp][:], in_=ps[banks[3]][:], func=AF.Silu)
                yield
                P.op("act", "activation", out=vT2[pp][:], in_=ps[banks[2]][:], func=AF.Copy)
                yield
                P.op("dve", "tensor_scalar", out=tk[:], in0=tB[:], scalar1=noml[:, h:h + 1], scalar2=oml[:, h:h + 1],
                     op0=ALU.mult, op1=ALU.add)
                yield
                P.op("act", "activation", out=tD[:], in_=tB[:], func=AF.Ln, scale=oml[:, h:h + 1], bias=lb[:, h:h + 1])
                yield
                P.op("dve", "tensor_tensor_scan", out=tB[:], data0=g.c["resetmask%d" % CH][:], data1=tD[:], initial=0.0,
                     op0=ALU.mult, op1=ALU.add)
                yield
                b3 = tB[:].rearrange("p (c j) -> p c j", j=CH)
                bm3 = tD[:].rearrange("p (c j) -> p c j", j=CH)
                P.op("dve", "tensor_tensor", out=bm3, in0=b3, in1=b3[:, :, RI:RI + 1].to_broadcast([128, NCH, CH]),
                     op=ALU.subtract)
                yield
                P.op("act", "activation", out=te1[:], in_=tD[:], func=AF.Exp)
                yield
                P.op("act", "activation", out=te2[:], in_=tD[:], func=AF.Exp, scale=-1.0)
                yield
                P.op("act", "activation", out=er2[pp][:], in_=b3[:, :, RI], func=AF.Exp)
                yield
                P.op("act", "activation", out=ebl2[pp][:], in_=b3[:, :, CH - 1], func=AF.Exp)
                yield
                P.op("pool", "tensor_tensor", out=qT[pp][:], in0=tq[:], in1=te1[:], op=ALU.mult)
                yield
                P.op("dve", "tensor_tensor", out=kT[pp][:], in0=tk[:], in1=te2[:], op=ALU.mult)
                yield
                P.op("dve", "tensor_tensor", out=bm3, in0=b3[:, :, CH - 1:CH].to_broadcast([128, NCH, CH]), in1=b3,
                     op=ALU.subtract)
                yield
                P.op("act", "activation", out=te1[:], in_=tD[:], func=AF.Exp)
                yield
                P.op("dve", "tensor_tensor", out=khT[pp][:], in0=tk[:], in1=te1[:], op=ALU.mult)
                yield

            gens = [gates_chain(0, 2 * hp), gates_chain(1, 2 * hp + 1)]
            while gens:
                for g_ in list(gens):
                    try:
                        next(g_)
                    except StopIteration:
                        gens.remove(g_)
            for pp in range(2):
                h = 2 * hp + pp
                vT = vT2[pp]
                P.mark("head%d_%d" % (mt, h))
                psb = ps[PS_X][:].bitcast(BF16)
                for (srcT, dstk) in ((khT[pp], ktok[pp]), (vT, vtok[pp])):
                    for c0 in range(0, NCH, 8):
                        for c in range(c0, c0 + 8):
                            P.op("pe", "transpose", out=psb[0:CH, (c - c0) * 128:(c - c0 + 1) * 128],
                                 in_=srcT[:, c * CH:(c + 1) * CH], identity=g.c["ident_b"][:])
                        P.op("act", "activation", out=dstk[:, c0:c0 + 8, :].rearrange("p a b -> p (a b)"), in_=psb[0:CH, :],
                             func=AF.Copy)
                a = atm[pp]
                for c in range(NCH):
                    tcs = slice(c * CH, (c + 1) * CH)
                    P.op("pe", "matmul", out=ps[PS_A][0:CH, tcs], lhsT=kT[pp][:, tcs], rhs=qT[pp][:, tcs], start=True, stop=True)
                P.op("dve", "copy_predicated", out=a[:], mask=g.c["hg_maskP%d" % CH][:], data=ps[PS_A][0:CH, :])
            P.mark("scan%d_%d" % (mt, hp))
            for c in range(NCH):
                tcs = slice(c * CH, (c + 1) * CH)
                for pp in range(2):
                    h = 2 * hp + pp
                    pso = ps[(PS_O, PS_O2)[pp]]
                    pss = ps[(PS_S, PS_S2)[pp]]
                    sr = srb2[pp][c % 2]
                    P.op("act", "activation", out=sr[:], in_=S[h][:], func=AF.Copy, scale=er2[pp][:, c:c + 1])
                    P.op("pe", "matmul", out=pso[:, tcs], lhsT=sr[:], rhs=qT[pp][:, tcs], start=True, stop=False)
                    P.op("pe", "matmul", out=pso[:, tcs], lhsT=vtok[pp][:, c, :], rhs=atm[pp][:, tcs], start=False, stop=True)
                    P.op("pe", "matmul", out=pss[:, 0:128], lhsT=ktok[pp][:, c, :], rhs=vtok[pp][:, c, :],
                         start=True, stop=True)
                    P.op("dve", "scalar_tensor_tensor", out=S[h][:], in0=S[h][:], scalar=ebl2[pp][:, c:c + 1],
                         in1=pss[:, 0:128], op0=ALU.mult, op1=ALU.add)
            for pp in range(2):
                h = 2 * hp + pp
                pso = ps[(PS_O, PS_O2)[pp]]
                P.op("act", "activation", out=osq[:], in_=pso[:], func=AF.Square)
                P.op("pe", "matmul", out=ps[PS_A][:], lhsT=ones_m[:], rhs=osq[:], start=True, stop=True)
                P.op("act", "activation", out=orstd[:], in_=ps[PS_A][:], func=AF.Sqrt, bias=epsc[:, 0:1])
                P.op("dve", "reciprocal", out=orstd[:], in_=orstd[:])
                P.op("dve", "scalar_tensor_tensor", out=otmp[:], in0=pso[:], scalar=ong[:, 0:1], in1=orstd[:],
                     op0=ALU.mult, op1=ALU.mult)
                P.op("pool", "tensor_tensor", out=Sub(og[:, h, :], h), in0=otmp[:], in1=zs[pp][:], op=ALU.mult)
        P.mark("epilogue%d" % mt)
        epilogue(g, mt, src, srctag, dst, og, KT, w_out, xb, ps[PS_V], ps[PS_Z])


def swa_layer(g, layer, src, srctag, dst):
    import math
    P, sb, dr, nc = g.P, g.sb, g.dr, g.nc
    T = g.T
    P.mark("prep_layer")
    prep_layer(g, layer)
    L = "L%d_" % layer
    NB = MT // 128
    g.warena = sb(L + "warena", [128, KT * 4096], BF16)
    w_in = g.warena[:, 0:KT * 3072].rearrange("p (k n) -> p k n", k=KT)
    w_out = g.warena[:, KT * 3072:KT * 3072 + KT * 1024].rearrange("p (k n) -> p k n", k=KT)
    WQ, WK, WV, WZ = 0, 1024, 1536, 2048
    win = dr["sw_in_w"][0]
    i = 0
    for k in range(KT):
        for (c0, w, dup, d0) in ((0, 1024, False, WQ), (1024, 256, True, WK), (1280, 256, True, WV), (1536, 1024, False, WZ)):
            stg = g.stage[i % 2]
            eng = ("dve", "pool")[i % 2]
            i += 1
            P.op("sp", "dma_start", out=stg[:, 0:w], in_=win[k * 128:(k + 1) * 128, c0:c0 + w])
            if not dup:
                P.op(eng, "tensor_copy", out=W(w_in[:, k, d0:d0 + w], k), in_=stg[:, 0:w])
            else:
                dv = w_in[:, k, d0:d0 + 512].rearrange("p (h r d) -> p h r d", h=4, r=2)
                sv = stg[:, 0:256].rearrange("p (h d) -> p h d", h=4)
                for r in range(2):
                    P.op(eng, "tensor_copy", out=W(dv[:, :, r, :], k), in_=sv)
    load_weight_bf16(g, w_out, dr["sw_out_w"][0], 1024, colscale=g.gate_b, kbase=100)

    gq = sb(L + "gq", [128, 1])
    gk = sb(L + "gk", [128, 1])
    for (dst_, nm) in ((gq, "sw_qnorm"), (gk, "sw_knorm")):
        r1 = sb(L + nm + "_r", [1, 128])
        P.op("sp", "dma_start", out=r1[0:1, 0:64], in_=dr[nm][0:1, :])
        P.op("sp", "dma_start", out=r1[0:1, 64:128], in_=dr[nm][0:1, :])
        transpose_small(g, dst_[:], r1[0:1, :], 1, g.psum[7])
    epsc = sb(L + "epsc", [128, 1])
    P.op("dve", "memset", ap=epsc[:], constant=EPS)
    negpi = sb(L + "negpi", [128, 1])
    P.op("dve", "memset", ap=negpi[:], constant=-math.pi)
    cpi = sb(L + "cpi", [128, 3])
    P.op("dve", "memset", ap=cpi[:, 0:1], constant=math.pi)
    P.op("dve", "memset", ap=cpi[:, 1:2], constant=1.5 * math.pi)
    P.op("dve", "memset", ap=cpi[:, 2:3], constant=2 * math.pi)
    sk_r = sb(L + "sk_r", [1, 16])
    P.op("sp", "dma_start", out=sk_r[:], in_=dr["sw_sinks"][0:1, :])
    P.op("act", "activation", out=sk_r[:], in_=sk_r[:], func=AF.Exp)
    P.op("pe", "matmul", out=g.psum[6][:, 0:16], lhsT=g.c["ones_f"][0:1, :], rhs=sk_r[0:1, :], start=True, stop=True)
    esb = sb(L + "esb", [128, 16])
    P.op("dve", "tensor_copy", out=esb[:], in_=g.psum[6][:, 0:16])
    esb4 = esb[:].rearrange("p (h j s) -> p h j s", h=4, j=2)
    esb2 = sb(L + "esb2", [128, 4, 2])
    P.op("dve", "tensor_copy", out=esb2[0:64, :, :], in_=esb4[0:64, :, :, 0])
    P.op("dve", "tensor_copy", out=esb2[64:128, :, :], in_=esb4[64:128, :, :, 1])
    esf = sb(L + "esf", [128, 4, MT])
    for h in range(4):
        for sc in range(2):
            P.op("dve", "tensor_copy", out=esf[:, h, sc * 256:(sc + 1) * 256].rearrange("p (j q) -> p j q", j=2),
                 in_=esb2[:, h, :].unsqueeze(2).to_broadcast([128, 2, 128]))
    ones_b = sb(L + "ones_b", [128, 128], BF16)
    P.op("dve", "tensor_copy", out=ones_b[:], in_=g.c["ones_f"][:])

    xb = [sb(L + "xb%d" % i, [128, D]) for i in range(2)]
    xnb = [sb(L + "xn%d" % i, [128, D], BF16) for i in range(2)]
    ss = sb(L + "ss", [128, 4])
    rstd = sb(L + "rstd", [128, 4])
    hT = sb(L + "hT", [128, KT, MT], BF16)
    cos2 = sb(L + "cos2", [128, MT])
    sin2 = sb(L + "sin2", [128, MT])
    t_sq = sb(L + "t_sq", [128, MT])
    t_rs = sb(L + "t_rs", [128, MT])
    t_qn = sb(L + "t_qn", [128, MT])
    t_a = sb(L + "t_a", [128, MT])
    t_b = sb(L + "t_b", [128, MT])
    posi, ang, ua = t_a[:].bitcast(I32), t_b, t_sq
    qg = sb(L + "qg", [128, 8, MT], BF16)
    kg = sb(L + "kg", [128, 4, (NB + 1) * 128], BF16)
    vd = sb(L + "vd", [128, NB + 1, 512], BF16)
    zs = sb(L + "zs", [128, 8, MT], BF16)
    pT = [sb(L + "pT%d" % i, [128, MT], BF16) for i in range(2)]
    dtmp, otmp = t_rs, t_qn
    og = sb(L + "og", [128, 8, MT], BF16)
    ps = g.psum
    PS_A, PS_B, PS_N, PS_R, PS_S0, PS_S1, PS_O, PS_D = range(8)

    def normrope(pst, gain, out_ap):
        P.op("act", "activation", out=t_sq[:], in_=pst[:], func=AF.Square)
        P.op("pe", "matmul", out=ps[PS_N][:], lhsT=g.c["sw_ones_blk"][:], rhs=t_sq[:], start=True, stop=True)
        P.op("act", "activation", out=t_rs[:], in_=ps[PS_N][:], func=AF.Sqrt, bias=epsc[:, 0:1])
        P.op("dve", "reciprocal", out=t_rs[:], in_=t_rs[:])
        P.op("dve", "scalar_tensor_tensor", out=t_qn[:], in0=pst[:], scalar=gain[:, 0:1], in1=t_rs[:],
             op0=ALU.mult, op1=ALU.mult)
        P.op("pe", "matmul", out=ps[PS_R][:], lhsT=g.c["sw_rotT"][:], rhs=t_qn[:], start=True, stop=True)
        P.op("pool", "tensor_tensor", out=t_a[:], in0=t_qn[:], in1=cos2[:], op=ALU.mult)
        P.op("dve", "tensor_tensor", out=t_b[:], in0=ps[PS_R][:], in1=sin2[:], op=ALU.mult)
        P.op("pool", "tensor_tensor", out=out_ap, in0=t_a[:], in1=t_b[:], op=ALU.add)

    for mt in range(T // MT):
        prologue(g, mt, src, srctag, hT, xb, xnb, ss, rstd)
        P.mark("sw_rope%d" % mt)
        P.op("sp", "dma_start", out=posi, in_=dr["positions"][0:1, mt * MT:(mt + 1) * MT].partition_broadcast(128))
        P.op("dve", "tensor_copy", out=ang[:], in_=posi)
        P.op("dve", "tensor_scalar", out=ang[:], in0=ang[:], scalar1=g.c["sw_invfreq"][:, 0:1], scalar2=None, op0=ALU.mult)
        C1 = 6.28125
        C2 = 2 * math.pi - C1
        P.op("dve", "tensor_scalar", out=ua[:], in0=ang[:], scalar1=1.0 / (2 * math.pi), scalar2=None, op0=ALU.mult)
        P.op("dve", "tensor_copy", out=posi, in_=ua[:])
        P.op("dve", "tensor_copy", out=ua[:], in_=posi)
        P.op("dve", "scalar_tensor_tensor", out=ang[:], in0=ua[:], scalar=-C1, in1=ang[:], op0=ALU.mult, op1=ALU.add)
        P.op("dve", "scalar_tensor_tensor", out=ang[:], in0=ua[:], scalar=-C2, in1=ang[:], op0=ALU.mult, op1=ALU.add)
        P.op("dve", "tensor_scalar", out=ua[:], in0=ang[:], scalar1=math.pi, scalar2=None, op0=ALU.is_gt)
        P.op("dve", "scalar_tensor_tensor", out=ang[:], in0=ua[:], scalar=-2 * math.pi, in1=ang[:], op0=ALU.mult, op1=ALU.add)
        P.op("dve", "tensor_scalar", out=ang[:], in0=ang[:], scalar1=math.pi, scalar2=-math.pi, op0=ALU.min, op1=ALU.max)
        P.op("act", "activation", out=sin2[:], in_=ang[:], func=AF.Sin)
        P.op("act", "activation", out=ua[:], in_=ang[:], func=AF.Sin, scale=0.5)
        P.op("dve", "tensor_tensor", out=ua[:], in0=ua[:], in1=ua[:], op=ALU.mult)
        P.op("dve", "tensor_scalar", out=cos2[:], in0=ua[:], scalar1=-2.0, scalar2=1.0, op0=ALU.mult, op1=ALU.add)
        P.mark("sw_proj%d" % mt)
        for p in range(8):
            pst = ps[PS_A + p % 2]
            for k in range(KT):
                P.op("pe", "matmul", out=pst[:], lhsT=W(w_in[:, k, WQ + p * 128:WQ + (p + 1) * 128], k), rhs=Sub(hT[:, k, :], k),
                     start=(k == 0), stop=(k == KT - 1))
            normrope(pst, gq, Sub(qg[:, p, :], p))
        for h in range(4):
            pst = ps[PS_A + h % 2]
            for k in range(KT):
                P.op("pe", "matmul", out=pst[:], lhsT=W(w_in[:, k, WK + h * 128:WK + (h + 1) * 128], k), rhs=Sub(hT[:, k, :], k),
                     start=(k == 0), stop=(k == KT - 1))
            normrope(pst, gk, Sub(kg[:, h, 128:128 + MT], ("k", h)))
        for blk in range(NB):
            pst = ps[PS_A + blk % 2]
            for k in range(KT):
                P.op("pe", "matmul", out=pst[:], lhsT=Sub(hT[:, k, blk * 128:(blk + 1) * 128], k),
                     rhs=W(w_in[:, k, WV:WV + 512], k), start=(k == 0), stop=(k == KT - 1))
            P.op("act", "activation", out=Sub(vd[:, 1 + blk, :], 1 + blk), in_=pst[:], func=AF.Copy)
        for p in range(8):
            pst = ps[PS_A + p % 2]
            for k in range(KT):
                P.op("pe", "matmul", out=pst[:], lhsT=W(w_in[:, k, WZ + p * 128:WZ + (p + 1) * 128], k), rhs=Sub(hT[:, k, :], k),
                     start=(k == 0), stop=(k == KT - 1))
            P.op("act", "activation", out=Sub(zs[:, p, :], p), in_=pst[:], func=AF.Silu)
        P.mark("sw_attn%d" % mt)
        for qb in range(NB):
            qcs = slice(qb * 128, (qb + 1) * 128)
            first = (mt == 0 and qb == 0)
            kbs = ([] if first else [0]) + [1]
            for h in range(4):
                for kb in kbs:
                    kblk = qb + kb
                    kcs = slice(kblk * 128, (kblk + 1) * 128)
                    pt = pT[kb]
                    for sl in range(2):
                        pss = ps[(PS_S0, PS_N)[sl] + kb]
                        rs = slice(sl * 64, (sl + 1) * 64)
                        P.op("pe", "matmul", out=pss[:, 0:256],
                             lhsT=Sub(kg[rs, h, kcs], ("k", h)) if kblk > 0 else Sub(kg[rs, h, kcs], ("kh", h)),
                             rhs=qg[rs, 2 * h:2 * h + 2, qcs], start=True, stop=True)
                        P.op("act", "activation", out=pt[:, sl * 256:(sl + 1) * 256], in_=pss[:, 0:256], func=AF.Exp, scale=0.125)
                    P.op("pool", "tensor_tensor", out=pt[:], in0=pt[:], in1=g.c["sw_mask_prev" if kb == 0 else "sw_mask_cur"][:],
                         op=ALU.mult)
                for ii, kb in enumerate(kbs):
                    kblk = qb + kb
                    vsub = Sub(vd[:, kblk, h * 128:(h + 1) * 128], kblk)
                    P.op("pe", "matmul", out=ps[PS_O][:], lhsT=vsub, rhs=pT[kb][:], start=(ii == 0), stop=(ii == len(kbs) - 1))
                for ii, kb in enumerate(kbs):
                    P.op("pe", "matmul", out=ps[PS_D][:], lhsT=ones_b[:], rhs=pT[kb][:], start=(ii == 0), stop=(ii == len(kbs) - 1))
                P.op("dve", "tensor_tensor", out=dtmp[:], in0=ps[PS_D][:], in1=esf[:, h, :], op=ALU.add)
                P.op("dve", "reciprocal", out=dtmp[:], in_=dtmp[:])
                P.op("dve", "tensor_tensor", out=otmp[:], in0=ps[PS_O][:], in1=dtmp[:], op=ALU.mult)
                for sl in range(2):
                    rs = slice(sl * 64, (sl + 1) * 64)
                    P.op("pool", "tensor_tensor", out=og[rs, 2 * h:2 * h + 2, qcs],
                         in0=otmp[rs, sl * 256:(sl + 1) * 256].rearrange("p (j q) -> p j q", j=2),
                         in1=zs[rs, 2 * h:2 * h + 2, qcs], op=ALU.mult)
        for h in range(4):
            P.op("pool", "tensor_copy", out=Sub(kg[:, h, 0:128], ("kh", h)), in_=Sub(kg[:, h, NB * 128:(NB + 1) * 128], ("k", h)))
        P.op("pool", "tensor_copy", out=Sub(vd[:, 0, :], 0), in_=Sub(vd[:, NB, :], NB))
        P.mark("epilogue%d" % mt)
        epilogue(g, mt, src, srctag, dst, og, KT, w_out, xb, ps[PS_A], ps[PS_B], ogtag=False)


def gdn_layer(g, layer, src, srctag, dst):
    P, sb, dr, nc = g.P, g.sb, g.dr, g.nc
    T = g.T
    GT = 256
    NC = GT // 64
    P.mark("prep_layer")
    prep_layer(g, layer)
    L = "L%d_" % layer
    NRES = 2080
    g.warena = sb(L + "warena", [128, KT * NRES + 16 * 1024], BF16)
    w_in = g.warena[:, 0:KT * NRES].rearrange("p (k n) -> p k n", k=KT)
    w_out = g.warena[:, KT * NRES:KT * NRES + 16 * 1024].rearrange("p (k n) -> p k n", k=16)
    gin = dr["gd_in_w"][0]
    load_weight_bf16(g, w_in[:, :, 0:2048], gin[:, 0:2048], 2048)
    load_weight_bf16(g, w_in[:, :, 2048:2080], gin[:, 6144:6176], 32)
    load_weight_bf16(g, w_out, dr["gd_out_w"][0], 1024, colscale=g.gate_b, kbase=100)
    wscr = g.gd_wscr
    cbf = [sb(L + "cbf%d" % i, [128, SW], BF16) for i in range(2)]
    i = 0
    for k in range(KT):
        for ch in range(4):
            stg = g.stage[i % 2]
            cb = cbf[i % 2]
            P.op("sp", "dma_start", out=stg[:, 0:1024], in_=gin[k * 128:(k + 1) * 128, 2048 + ch * 1024:2048 + (ch + 1) * 1024])
            P.op(("dve", "pool")[i % 2], "tensor_copy", out=cb[:], in_=stg[:, 0:1024])
            P.op("sp", "dma_start", out=wscr[ch * 8:(ch + 1) * 8, :, k * 128:(k + 1) * 128].rearrange("h p n -> p h n"),
                 in_=cb[:].rearrange("p (h n) -> p h n", h=8))
            i += 1
    P.barrier()
    wbuf = [sb(L + "wbuf%d" % i, [128, 1024], BF16) for i in range(4)]
    g.wcnt = 0

    def stream_w(hd):
        wb = wbuf[g.wcnt % 4]
        g.wcnt += 1
        P.op("sp", "dma_start", out=wb[:], in_=wscr[hd])
        return wb
    ps = g.psum
    PA, PB, PX, PQ, PD, PM, PN, PT = range(8)
    PO = PD
    cwr = sb(L + "cwr", [4, 4096])
    P.op("sp", "dma_start", out=cwr[:], in_=dr["gd_conv_w"][:, :])
    for ct in range(32):
        P.op("pe", "transpose", out=ps[PA][:, ct * 4:(ct + 1) * 4], in_=cwr[0:4, ct * 128:(ct + 1) * 128],
             identity=g.c["ident_f"][0:4, 0:4])
    cw = sb(L + "cw", [128, 128])
    P.op("dve", "tensor_copy", out=cw[:], in_=ps[PA][:, 0:128])
    alog = sb(L + "alog", [16, 1])
    dtb = sb(L + "dtb", [16, 1])
    P.op("sp", "dma_start", out=alog[:], in_=dr["gd_a_log"][:, :])
    P.op("sp", "dma_start", out=dtb[:], in_=dr["gd_dt_bias"][:, :])
    nA = sb(L + "nA", [16, 1])
    P.op("act", "activation", out=nA[:], in_=alog[:], func=AF.Exp)
    P.op("dve", "tensor_scalar", out=nA[:], in0=nA[:], scalar1=-1.0, scalar2=None, op0=ALU.mult)
    one16 = sb(L + "one16", [16, 1])
    P.op("dve", "memset", ap=one16[:], constant=1.0)
    onr = sb(L + "onr", [1, 128])
    ong = sb(L + "ong", [128, 1])
    P.op("sp", "dma_start", out=onr[:], in_=dr["gd_onorm"][:, :])
    transpose_small(g, ong[:], onr[0:1, :], 1, ps[PB])
    ones_m = sb(L + "ones_m", [128, 128])
    P.op("dve", "tensor_scalar", out=ones_m[:], in0=g.c["ones_f"][:], scalar1=1.0 / 128, scalar2=None, op0=ALU.mult)
    epsc = sb(L + "epsc", [128, 1])
    P.op("dve", "memset", ap=epsc[:], constant=EPS)
    identb64 = g.c["ident_b"]

    xb = [sb(L + "xb0", [128, D])]
    xnb = [sb(L + "xn0", [128, D], BF16)]
    ss = sb(L + "ss", [128, 4])
    rstd = sb(L + "rstd", [128, 4])
    hT = sb(L + "hT", [128, KT, GT], BF16)
    halo = sb(L + "halo", [128, 32, 3])
    for ct in range(32):
        P.op("pool", "memset", ap=Sub(halo[:, ct, :], ct), constant=0.0)
    xc = sb(L + "xc", [128, 3 + GT])
    acc = sb(L + "acc", [128, GT])
    f_a = sb(L + "f_a", [128, GT])
    f_b = sb(L + "f_b", [128, GT])
    f_c = sb(L + "f_c", [128, GT])
    TS = [dict(xc=xc, acc=acc, f_a=f_a, f_b=f_b, f_c=f_c),
          dict(xc=sb(L + "xc1", [128, 3 + GT]), acc=sb(L + "acc1", [128, GT]), f_a=sb(L + "f_a1", [128, GT]),
               f_b=sb(L + "f_b1", [128, GT]), f_c=sb(L + "f_c1", [128, GT]))]
    vT2 = [sb(L + "vT0", [128, GT], BF16), sb(L + "vT1", [128, GT], BF16)]
    qT = sb(L + "qT", [128, GT], BF16)
    kT = sb(L + "kT", [128, GT], BF16)
    kbT = sb(L + "kbT", [128, GT], BF16)
    kbT2 = [kbT, sb(L + "kbT_b", [128, GT], BF16)]
    qd = [sb(L + "qd%d" % i, [128, GT], BF16) for i in range(2)]
    zs = [sb(L + "zs%d" % i, [128, GT], BF16) for i in range(2)]
    edl4 = [sb(L + "edl4_%d" % i, [128, NC]) for i in range(2)]
    r_beta = sb(L + "r_beta", [16, GT])
    r_g = sb(L + "r_g", [16, GT])
    r_d = sb(L + "r_d", [16, GT])
    r_be = sb(L + "r_be", [16, GT])
    r_edl = sb(L + "r_edl", [16, GT])
    NP = GT // 128
    tkp_b = sb(L + "tkp_b", [128, 16, NP])
    tkp_be = sb(L + "tkp_be", [128, 16, NP])
    tkp_d = sb(L + "tkp_d", [128, 16, NP])
    tk_edl = sb(L + "tk_edl", [64, 16, NC])
    tk_d = sb(L + "tk_d", [64, 16, NC])
    Et = sb(L + "Et", [64, GT])
    Em = sb(L + "Em", [64, GT])
    Ep = sb(L + "Ep", [128, GT])
    Et2 = [Et, sb(L + "Et_b", [64, GT])]
    Em2 = [Em, sb(L + "Em_b", [64, GT])]
    Ep2 = [Ep, sb(L + "Ep_b", [128, GT])]
    Mk = [sb(L + "Mk%d" % i, [128, 512]) for i in range(2)]
    Nk = [sb(L + "Nk%d" % i, [128, 512]) for i in range(2)]
    Tk = [sb(L + "Tk%d" % i, [128, 512]) for i in range(2)]
    Tbf = sb(L + "Tbf", [128, 512], BF16)
    qkM = sb(L + "qkM", [64, 512], BF16)
    vb = sb(L + "vb", [128, 2 * NP, 128], BF16)
    kbd = sb(L + "kbd", [128, 2 * NP, 128], BF16)
    khat = sb(L + "khat", [64, 8, 128], BF16)
    wTn = sb(L + "wTn", [128, 512], BF16)
    vn2 = [[sb(L + "vn%d_%d" % (hh, i), [64, 128], BF16) for i in range(2)] for hh in range(2)]
    S = [sb(L + "S%d" % h, [128, 128]) for h in range(16)]
    Sb = [sb(L + "Sb%d" % h, [128, 128], BF16) for h in range(16)]
    for h in range(16):
        P.op("pool", "memset", ap=S[h][:], constant=0.0)
        P.op("pool", "memset", ap=Sb[h][:], constant=0.0)
    og = sb(L + "og", [128, 16, GT], BF16)
    identI = sb(L + "identI", [128, 512])
    for b in range(4):
        P.op("dve", "tensor_copy", out=identI[:, b * 128:(b + 1) * 128], in_=g.c["ident_f"][:, :])
    rm16 = g.c["gd_rm"][:, :]
    sel = g.c["gd_sel"]

    def conv_silu(pst, ct, out_ap, func=AF.Silu):
        P.op("pool", "tensor_copy", out=xc[:, 0:3], in_=halo[:, ct, :])
        P.op("act", "activation", out=xc[:, 3:3 + GT], in_=pst[:, 0:GT], func=AF.Copy)
        P.op("pool", "tensor_copy", out=halo[:, ct, :], in_=xc[:, GT:GT + 3])
        P.op("dve", "tensor_scalar", out=acc[:], in0=xc[:, 3:3 + GT], scalar1=cw[:, ct * 4 + 3:ct * 4 + 4], scalar2=None,
             op0=ALU.mult)
        for jt in (2, 1, 0):
            P.op("dve", "scalar_tensor_tensor", out=acc[:], in0=xc[:, jt:jt + GT], scalar=cw[:, ct * 4 + jt:ct * 4 + jt + 1],
                 in1=acc[:], op0=ALU.mult, op1=ALU.add)
        P.op("act", "activation", out=out_ap, in_=acc[:], func=func)

    def interleave(gens):
        gens = list(gens)
        while gens:
            for g_ in list(gens):
                try:
                    next(g_)
                except StopIteration:
                    gens.remove(g_)

    def conv_silu_g(pst, ct, out_ap, t_):
        xc_, acc_ = t_["xc"], t_["acc"]
        P.op("pool", "tensor_copy", out=xc_[:, 0:3], in_=Sub(halo[:, ct, :], ct))
        yield
        P.op("act", "activation", out=xc_[:, 3:3 + GT], in_=pst[:, 0:GT], func=AF.Copy)
        yield
        P.op("pool", "tensor_copy", out=Sub(halo[:, ct, :], ct), in_=xc_[:, GT:GT + 3])
        yield
        P.op("dve", "tensor_scalar", out=acc_[:], in0=xc_[:, 3:3 + GT], scalar1=cw[:, ct * 4 + 3:ct * 4 + 4], scalar2=None,
             op0=ALU.mult)
        yield
        for jt in (2, 1, 0):
            P.op("dve", "scalar_tensor_tensor", out=acc_[:], in0=xc_[:, jt:jt + GT], scalar=cw[:, ct * 4 + jt:ct * 4 + jt + 1],
                 in1=acc_[:], op0=ALU.mult, op1=ALU.add)
            yield
        P.op("act", "activation", out=out_ap, in_=acc_[:], func=AF.Silu)
        yield

    def l2norm_g(t_, out_bf, scale, psb_):
        src_f, fb_, fc_ = t_["f_a"], t_["f_b"], t_["f_c"]
        P.op("pool", "tensor_tensor", out=fb_[:], in0=src_f[:], in1=src_f[:], op=ALU.mult)
        yield
        P.op("pe", "matmul", out=psb_[:, 0:GT], lhsT=g.c["ones_f"][:], rhs=fb_[:], start=True, stop=True)
        yield
        P.op("act", "activation", out=fc_[:], in_=psb_[:, 0:GT], func=AF.Sqrt, bias=epsc[:, 0:1])
        yield
        P.op("dve", "reciprocal", out=fc_[:], in_=fc_[:])
        yield
        P.op("dve", "scalar_tensor_tensor", out=out_bf, in0=src_f[:], scalar=scale, in1=fc_[:], op0=ALU.mult, op1=ALU.mult)
        yield

    def qk_chain(pst, c0, ct, t_, out_bf, scale, psb_):
        inproj(pst[:, 0:GT], c0)
        yield
        yield from conv_silu_g(pst, ct, t_["f_a"][:], t_)
        yield from l2norm_g(t_, out_bf, scale, psb_)

    def head_front(hh, h):
        pv = ps[(PA, PT)[hh]]
        pz = ps[(PB, PN)[hh]]
        inproj_s(pv[:, 0:GT], h)
        yield
        yield from conv_silu_g(pv, 16 + h, vT2[hh][:], TS[hh])
        inproj_s(pz[:, 0:GT], 16 + h)
        yield
        P.op("act", "activation", out=zs[hh][:], in_=pz[:, 0:GT], func=AF.Silu)
        yield

    def inproj(pst, c0, m=128):
        for k in range(KT):
            P.op("pe", "matmul", out=pst, lhsT=W(w_in[:, k, c0:c0 + m], k), rhs=Sub(hT[:, k, :], k),
                 start=(k == 0), stop=(k == KT - 1))

    def inproj_s(pst, hd):
        wb = stream_w(hd)
        for k in range(KT):
            P.op("pe", "matmul", out=pst, lhsT=wb[:, k * 128:(k + 1) * 128], rhs=Sub(hT[:, k, :], k),
                 start=(k == 0), stop=(k == KT - 1))

    def l2norm(src_f, out_bf, scale):
        P.op("pool", "tensor_tensor", out=f_b[:], in0=src_f[:], in1=src_f[:], op=ALU.mult)
        P.op("pe", "matmul", out=ps[PB][:, 0:GT], lhsT=g.c["ones_f"][:], rhs=f_b[:], start=True, stop=True)
        P.op("act", "activation", out=f_c[:], in_=ps[PB][:, 0:GT], func=AF.Sqrt, bias=epsc[:, 0:1])
        P.op("dve", "reciprocal", out=f_c[:], in_=f_c[:])
        P.op("dve", "scalar_tensor_tensor", out=out_bf, in0=src_f[:], scalar=scale, in1=f_c[:], op0=ALU.mult, op1=ALU.mult)

    for mt in range(T // GT):
        prologue(g, mt, src, srctag, hT, xb, xnb, ss, rstd, mtl=GT)
        P.mark("gd_rows%d" % mt)
        inproj(ps[PA][0:16, 0:GT], 2048, 16)
        inproj(ps[PA][0:16, GT:2 * GT], 2064, 16)
        P.op("act", "activation", out=r_beta[:], in_=ps[PA][0:16, GT:2 * GT], func=AF.Sigmoid)
        P.op("act", "activation", out=r_g[:], in_=ps[PA][0:16, 0:GT], func=AF.Exp, bias=dtb[:, 0:1])
        P.op("act", "activation", out=r_g[:], in_=r_g[:], func=AF.Ln, bias=one16[:, 0:1])
        P.op("dve", "tensor_scalar", out=r_g[:], in0=r_g[:], scalar1=nA[:, 0:1], scalar2=None, op0=ALU.mult)
        P.op("dve", "tensor_tensor_scan", out=r_d[:], data0=rm16, data1=r_g[:], initial=0.0, op0=ALU.mult, op1=ALU.add)
        P.op("act", "activation", out=r_be[:], in_=r_d[:], func=AF.Exp)
        P.op("dve", "tensor_tensor", out=r_be[:], in0=r_be[:], in1=r_beta[:], op=ALU.mult)
        d3 = r_d[:].rearrange("p (c j) -> p c j", j=64)
        P.op("dve", "tensor_tensor", out=r_edl[:].rearrange("p (c j) -> p c j", j=64), in0=d3,
             in1=d3[:, :, 63:64].to_broadcast([16, NC, 64]), op=ALU.subtract)
        P.op("act", "activation", out=r_edl[:], in_=r_edl[:], func=AF.Exp, scale=-1.0)
        for (row, tok) in ((r_edl, tk_edl), (r_d, tk_d)):
            for c in range(NC):
                P.op("pe", "transpose", out=ps[PB][0:64, c * 16:(c + 1) * 16], in_=row[0:16, c * 64:(c + 1) * 64],
                     identity=g.c["ident_f"][0:16, 0:16])
            P.op("dve", "tensor_copy", out=tok[:].rearrange("p h c -> p c h"),
                 in_=ps[PB][0:64, 0:NC * 16].rearrange("p (c h) -> p c h", h=16))
        for (row, tok) in ((r_beta, tkp_b), (r_be, tkp_be), (r_d, tkp_d)):
            for pr in range(NP):
                P.op("pe", "transpose", out=ps[PB][:, pr * 16:(pr + 1) * 16], in_=row[0:16, pr * 128:(pr + 1) * 128],
                     identity=g.c["ident_f"][0:16, 0:16])
            P.op("dve", "tensor_copy", out=tok[:].rearrange("p h c -> p c h"),
                 in_=ps[PB][:, 0:NP * 16].rearrange("p (c h) -> p c h", h=16))
        for gq in range(8):
            P.mark("gd_g%d_%d" % (mt, gq))
            interleave([qk_chain(ps[PA], gq * 128, gq, TS[0], qT[:], 128 ** -0.5, ps[PN]),
                        qk_chain(ps[PT], 1024 + gq * 128, 8 + gq, TS[1], kT[:], 1.0, ps[PB])])
            pxb = ps[PX][:].bitcast(BF16)
            for c in range(NC):
                P.op("pe", "transpose", out=pxb[0:64, c * 128:(c + 1) * 128], in_=kT[:, c * 64:(c + 1) * 64],
                     identity=g.c["ident_b"][:])
            for hh in range(2):
                h = 2 * gq + hh
                bs = slice(hh * NC, (hh + 1) * NC)
                P.op("dve", "tensor_tensor", out=khat[:, bs, :],
                     in0=pxb[0:64, 0:NC * 128].rearrange("p (c d) -> p c d", d=128),
                     in1=tk_edl[:, h, :].unsqueeze(2).to_broadcast([64, NC, 128]), op=ALU.mult)
            for pr in range(NP):
                P.op("pe", "transpose", out=pxb[:, pr * 128:(pr + 1) * 128], in_=kT[:, pr * 128:(pr + 1) * 128],
                     identity=g.c["ident_b"][:])
            for hh in range(2):
                h = 2 * gq + hh
                P.op("dve", "tensor_tensor", out=kbd[:, hh * NP:(hh + 1) * NP, :],
                     in0=pxb[:, 0:NP * 128].rearrange("p (c d) -> p c d", d=128),
                     in1=tkp_be[:, h, :].unsqueeze(2).to_broadcast([128, NP, 128]), op=ALU.mult)
            for c in range(NC):
                P.op("pe", "matmul", out=ps[PQ][0:64, c * 64:(c + 1) * 64], lhsT=kT[:, c * 64:(c + 1) * 64],
                     rhs=qT[:, c * 64:(c + 1) * 64], start=True, stop=True)
            interleave([head_front(0, 2 * gq), head_front(1, 2 * gq + 1)])
            def head_mid(hh, h):
                cs = slice(hh * GT, (hh + 1) * GT)
                vT = vT2[hh]
                psd = ps[(PD, PA)[hh]]
                pxh = ps[(PX, PB)[hh]][:].bitcast(BF16)
                kbT_, Et_, Em_, Ep_, fb_ = kbT2[hh], Et2[hh], Em2[hh], Ep2[hh], TS[hh]["f_b"]
                for pr in range(NP):
                    P.op("pe", "transpose", out=pxh[:, pr * 128:(pr + 1) * 128], in_=vT[:, pr * 128:(pr + 1) * 128],
                         identity=g.c["ident_b"][:])
                yield
                P.op("dve", "tensor_tensor", out=vb[:, hh * NP:(hh + 1) * NP, :],
                     in0=pxh[:, 0:NP * 128].rearrange("p (c d) -> p c d", d=128),
                     in1=tkp_b[:, h, :].unsqueeze(2).to_broadcast([128, NP, 128]), op=ALU.mult)
                yield
                P.op("pe", "matmul", out=psd[:, 0:GT], lhsT=sel[0:16, h * 128:(h + 1) * 128], rhs=r_beta[:], start=True, stop=True)
                P.op("pe", "matmul", out=psd[:, GT:2 * GT], lhsT=sel[0:16, h * 128:(h + 1) * 128], rhs=r_d[:], start=True, stop=True)
                yield
                P.op("dve", "tensor_tensor", out=kbT_[:], in0=kT[:], in1=psd[:, 0:GT], op=ALU.mult)
                yield
                P.op("act", "activation", out=fb_[:], in_=psd[:, GT:2 * GT], func=AF.Exp)
                yield
                P.op("pool", "tensor_tensor", out=qd[hh][:], in0=qT[:], in1=fb_[:], op=ALU.mult)
                yield
                P.op("act", "activation", out=edl4[hh][:], in_=psd[:, GT:2 * GT].rearrange("p (c j) -> p c j", j=64)[:, :, 63],
                     func=AF.Exp)
                yield
                P.op("dve", "tensor_tensor", out=Et_[:].rearrange("p (c j) -> p c j", j=64),
                     in0=psd[0:64, GT:2 * GT].rearrange("p (c j) -> p c j", j=64),
                     in1=tk_d[:, h, :].unsqueeze(2).to_broadcast([64, NC, 64]), op=ALU.subtract)
                yield
                P.op("dve", "tensor_scalar", out=Et_[:], in0=Et_[:], scalar1=0.0, scalar2=None, op0=ALU.min)
                yield
                P.op("act", "activation", out=Et_[:], in_=Et_[:], func=AF.Exp)
                yield
                P.op("dve", "tensor_tensor", out=Ep_[:].rearrange("p (c j) -> p c j", j=128),
                     in0=psd[:, GT:2 * GT].rearrange("p (c j) -> p c j", j=128),
                     in1=tkp_d[:, h, :].unsqueeze(2).to_broadcast([128, NP, 128]), op=ALU.subtract)
                yield
                P.op("dve", "tensor_scalar", out=Ep_[:], in0=Ep_[:], scalar1=0.0, scalar2=None, op0=ALU.min)
                yield
                P.op("act", "activation", out=Ep_[:], in_=Ep_[:], func=AF.Exp)
                yield
                P.op("pool", "tensor_tensor", out=Ep_[:], in0=Ep_[:], in1=g.c["gd_maskSP"][:, :], op=ALU.mult)
                yield
                for pr in range(NP):
                    P.op("pe", "matmul", out=ps[(PM, PT)[hh]][:, pr * 128:(pr + 1) * 128],
                         lhsT=kT[:, pr * 128:(pr + 1) * 128], rhs=kbT_[:, pr * 128:(pr + 1) * 128], start=True, stop=True)
                yield
                P.op("dve", "scalar_tensor_tensor", out=Mk[0][:, cs], in0=ps[(PM, PT)[hh]][:, 0:GT], scalar=-1.0, in1=Ep_[:],
                     op0=ALU.mult, op1=ALU.mult)
                yield
                P.op("pool", "tensor_tensor", out=Em_[:], in0=Et_[:], in1=g.c["gd_maskI"][:, :], op=ALU.mult)
                yield
                P.op("dve", "tensor_tensor", out=qkM[:, cs], in0=ps[PQ][0:64, 0:GT], in1=Em_[:], op=ALU.mult)
                yield

            interleave([head_mid(0, 2 * gq), head_mid(1, 2 * gq + 1)])
            for b in range(4):
                P.op("pe", "transpose", out=ps[PX][:, b * 128:(b + 1) * 128], in_=Mk[0][:, b * 128:(b + 1) * 128],
                     identity=g.c["ident_f"][:, :])
            P.op("act", "activation", out=Nk[0][:], in_=ps[PX][:, 0:512], func=AF.Copy)
            P.op("pool", "tensor_tensor", out=Tk[0][:], in0=Mk[0][:], in1=identI[:], op=ALU.add)
            for st_ in range(5):
                cur, nxt = st_ % 2, (st_ + 1) % 2
                if st_ < 4:
                    for b in range(4):
                        bc = slice(b * 128, (b + 1) * 128)
                        P.op("pe", "matmul", out=ps[PM][:, bc], lhsT=Nk[cur][:, bc], rhs=Mk[cur][:, bc], start=True, stop=True)
                    P.op("dve", "tensor_copy", out=Mk[nxt][:], in_=ps[PM][:, :])
                for b in range(4):
                    bc = slice(b * 128, (b + 1) * 128)
                    P.op("pe", "matmul", out=ps[PN][:, bc], lhsT=Mk[cur][:, bc], rhs=Nk[cur][:, bc], start=True, stop=True)
                P.op("act", "activation", out=Nk[nxt][:], in_=ps[PN][:, :], func=AF.Copy)
                for b in range(4):
                    bc = slice(b * 128, (b + 1) * 128)
                    P.op("pe", "matmul", out=ps[PT][:, bc], lhsT=Nk[nxt][:, bc], rhs=Tk[cur][:, bc], start=True, stop=True)
                P.op("dve", "tensor_tensor", out=Tk[nxt][:], in0=ps[PT][:, :], in1=Tk[cur][:], op=ALU.add)
            P.op("pool", "tensor_copy", out=Tbf[:], in_=Tk[1][:])
            TT = Tbf
            for b in range(4):
                bc = slice(b * 128, (b + 1) * 128)
                P.op("pe", "matmul", out=ps[PN][:, bc], lhsT=kbd[:, b, :], rhs=TT[:, bc], start=True, stop=True)
            P.op("act", "activation", out=wTn[:], in_=ps[PN][:], func=AF.Copy, scale=-1.0)
            for c in range(NC):
                for hh in range(2):
                    h = 2 * gq + hh
                    b = hh * NC + c
                    bc = slice(b * 64, (b + 1) * 64)
                    v_ = vn2[hh][c % 2]
                    pv = ps[(PT, PX)[hh]]
                    pst_ = ps[(PM, PN)[hh]]
                    P.op("pe", "matmul", out=pv[0:64, 0:128], lhsT=TT[:, bc], rhs=vb[:, hh * NP + c // 2, :], start=True, stop=False)
                    P.op("pe", "matmul", out=pv[0:64, 0:128], lhsT=wTn[:, bc], rhs=Sb[h][:], start=False, stop=True)
                    P.op("act", "activation", out=v_[:], in_=pv[0:64, 0:128], func=AF.Copy)
                    P.op("pe", "matmul", out=ps[PO][:, bc], lhsT=Sb[h][:], rhs=qd[hh][:, c * 64:(c + 1) * 64], start=True, stop=False)
                    P.op("pe", "matmul", out=ps[PO][:, bc], lhsT=v_[:], rhs=qkM[:, bc], start=False, stop=True)
                    P.op("pe", "matmul", out=pst_[:, 0:128], lhsT=khat[:, b, :], rhs=v_[:], start=True, stop=True)
                    P.op("dve", "scalar_tensor_tensor", out=S[h][:], in0=S[h][:], scalar=edl4[hh][:, c:c + 1], in1=pst_[:, 0:128],
                         op0=ALU.mult, op1=ALU.add)
                    P.op("pool", "tensor_copy", out=Sb[h][:], in_=S[h][:])
            for hh in range(2):
                h = 2 * gq + hh
                cs = slice(hh * GT, (hh + 1) * GT)
                P.op("act", "activation", out=f_a[:], in_=ps[PO][:, cs], func=AF.Square)
                P.op("pe", "matmul", out=ps[PB][:, 0:GT], lhsT=ones_m[:], rhs=f_a[:], start=True, stop=True)
                P.op("act", "activation", out=f_c[:], in_=ps[PB][:, 0:GT], func=AF.Sqrt, bias=epsc[:, 0:1])
                P.op("dve", "reciprocal", out=f_c[:], in_=f_c[:])
                P.op("dve", "scalar_tensor_tensor", out=f_b[:], in0=ps[PO][:, cs], scalar=ong[:, 0:1], in1=f_c[:],
                     op0=ALU.mult, op1=ALU.mult)
                P.op("pool", "tensor_tensor", out=Sub(og[:, h, :], h), in0=f_b[:], in1=zs[hh][:], op=ALU.mult)
        P.mark("epilogue%d" % mt)
        epilogue(g, mt, src, srctag, dst, og, 16, w_out, xb, ps[PA], ps[PB], mtl=GT)


def core_inputs(b, T, inputs, consts):
    f = np.float32
    m = {
        "x": np.ascontiguousarray(inputs["x"][b, :T]).astype(f, copy=False),
        "c8": np.ascontiguousarray(inputs["c"][b].reshape(8, 128)),
        "positions": np.ascontiguousarray(inputs["positions"][b, :T].reshape(1, T)).astype(np.int32, copy=False),
        "hgrn_lb": np.ascontiguousarray(inputs["hgrn_lb"].reshape(32, 128)),
        "ada_w": inputs["ada_w"],
        "ada_b": np.ascontiguousarray(inputs["ada_b"].reshape(4, 24, 128)),
        "ada_b_row": np.ascontiguousarray(inputs["ada_b"].reshape(4, 1, 3 * D)),
        "norm_g": np.ascontiguousarray(inputs["norm_g"].reshape(4, 8, 128)),
        "hg_in_w": inputs["hg_in_w"],
        "hg_out_w": inputs["hg_out_w"],
        "hg_onorm": np.ascontiguousarray(inputs["hg_onorm"].reshape(-1, 1, 128)),
        "sw_in_w": inputs["sw_in_w"],
        "sw_out_w": inputs["sw_out_w"],
        "sw_qnorm": inputs["sw_qnorm"],
        "sw_knorm": inputs["sw_knorm"],
        "sw_sinks": inputs["sw_sinks"],
        "gd_in_w": inputs["gd_in_w"],
        "gd_out_w": inputs["gd_out_w"],
        "gd_conv_w": np.ascontiguousarray(inputs["gd_conv_w"][0]),
        "gd_a_log": np.ascontiguousarray(inputs["gd_a_log"].reshape(16, 1)),
        "gd_dt_bias": np.ascontiguousarray(inputs["gd_dt_bias"].reshape(16, 1)),
        "gd_onorm": np.ascontiguousarray(inputs["gd_onorm"].reshape(1, 128)),
    }
    m.update(consts)
    return m


def run_layers(inputs, layers, T, ncores=2, trace=False):
    inputs = {k: np.asarray(v) for k, v in inputs.items()}
    consts = make_consts()
    nc = build(T, layers)
    in_maps = [core_inputs(b, T, inputs, consts) for b in range(ncores)]
    res = run_bass_kernel_spmd(nc, in_maps, core_ids=list(range(ncores)), trace=trace)
    out = np.stack([np.asarray(r["out"]) for r in res.results], axis=0)
    return out, res


def kernel(**inputs):
    T = inputs["x"].shape[1]
    out, _ = run_layers(inputs, [0, 1, 2, 3], T)
    return out.astype(np.float32, copy=False)
```
